# Optimizing a Trainium2 kernel written in Bass

```python
import numpy as np
import jax
import jax.numpy as jnp
from jax import lax

D_MODEL = 1024
BATCH = 4
SEQ = 8192
DEPTH = 4

CHUNK = 64
QBLK = 128
N_GROUPS = 4
GROUP_WIDTH = D_MODEL // N_GROUPS
HEAD_DIM = 64
N_HEADS = GROUP_WIDTH // HEAD_DIM
MIX_WIDTH = N_GROUPS * GROUP_WIDTH
SHORT_CONV = 4
FFN_CONV = 3
D_FF = ((8 * D_MODEL // 3 + 127) // 128) * 128
ROPE_BASE = 10000.0
RET_DECAY_EXP = 5.0
EPS = 1e-6
NEG_BIG = -1e30
PROJ_SIZES = (GROUP_WIDTH,) * 4 + (N_HEADS, N_HEADS) + (GROUP_WIDTH,) * 4 + (GROUP_WIDTH,) * 4 + (N_HEADS,) + (GROUP_WIDTH,) * 4
PROJ_WIDTH = sum(PROJ_SIZES)
F32 = jnp.float32

kernel_name = 'hybrid_parallel_group_streaming_encoder'


def _rmsnorm(x, g):
    xf = x.astype(F32)
    y = xf * lax.rsqrt(jnp.mean(xf * xf, axis=-1, keepdims=True) + EPS)
    return (y * g.astype(F32)).astype(x.dtype)


def _head_rms(t, g):
    t = t.astype(F32)
    return t * lax.rsqrt(jnp.mean(t * t, axis=-1, keepdims=True) + EPS) * g.astype(F32)


def _head_ln(t, g):
    t = t.astype(F32)
    tc = t - jnp.mean(t, axis=-1, keepdims=True)
    return tc * lax.rsqrt(jnp.mean(tc * tc, axis=-1, keepdims=True) + EPS) * g.astype(F32)


def _l2norm(t):
    return t * lax.rsqrt(jnp.sum(t * t, axis=-1, keepdims=True) + EPS)


def _masked_exp(mask, d):
    return jnp.where(mask, jnp.exp(jnp.where(mask, d, 0.0)), 0.0)


def _heads(t):
    b, s, _ = t.shape
    return t.reshape(b, s, -1, HEAD_DIM).transpose(0, 2, 1, 3)


def _merge(t):
    b, h, s, d = t.shape
    return t.transpose(0, 2, 1, 3).reshape(b, s, h * d)


def _chunks(t):
    return t.reshape(t.shape[:2] + (t.shape[2] // CHUNK, CHUNK) + t.shape[3:])


def _unchunk(t):
    return t.reshape(t.shape[:2] + (t.shape[2] * t.shape[3],) + t.shape[4:])


def _causal_dwconv(x, w):
    width, ch = w.shape
    return lax.conv_general_dilated(x, w.astype(x.dtype).reshape(width, 1, ch), window_strides=(1,), padding=((width - 1, 0),), dimension_numbers=('NWC', 'WIO', 'NWC'), feature_group_count=ch)


def _rope(t, cos, sin):
    t1, t2 = jnp.split(t, 2, axis=-1)
    return jnp.concatenate([t1 * cos - t2 * sin, t1 * sin + t2 * cos], axis=-1)


def _scan_states(decay, upd):
    def step(state, inp):
        a, u = inp
        return a * state + u, state
    _, prev = lax.scan(step, jnp.zeros_like(upd[:, :, 0]), (jnp.moveaxis(decay, 2, 0), jnp.moveaxis(upd, 2, 0)))
    return jnp.moveaxis(prev, 0, 2)


def _masks():
    ones = jnp.ones((CHUNK, CHUNK), dtype=bool)
    return jnp.tril(ones), jnp.tril(ones, -1)


def _gated_deltanet(q, k, v, gate, b, a, conv_w, a_log, dt_bias, onorm):
    tril, strict = _masks()
    qkv = jax.nn.silu(_causal_dwconv(jnp.concatenate([q, k, v], axis=-1), conv_w).astype(F32))
    q, k, v = (_heads(t) for t in jnp.split(qkv, 3, axis=-1))
    q = _l2norm(q) * HEAD_DIM ** -0.5
    k = _l2norm(k)
    beta = _chunks(jax.nn.sigmoid(b.astype(F32)).transpose(0, 2, 1))[..., None]
    log_alpha = -jnp.exp(a_log.astype(F32)) * jax.nn.softplus(a.astype(F32) + dt_bias.astype(F32))
    G = jnp.cumsum(_chunks(log_alpha.transpose(0, 2, 1)), axis=-1)
    qc, kc, vc = _chunks(q), _chunks(k), _chunks(v)
    gam = _masked_exp(tril, G[..., :, None] - G[..., None, :])
    kb = kc * beta
    a_mat = jnp.where(strict, jnp.einsum('bhntd,bhnsd->bhnts', kb, kc) * gam, 0.0)
    rhs = jnp.concatenate([vc * beta, kb * jnp.exp(G)[..., None]], axis=-1)
    sol = lax.linalg.triangular_solve(a_mat + jnp.eye(CHUNK, dtype=F32), rhs, left_side=True, lower=True, unit_diagonal=True)
    u, w = jnp.split(sol, 2, axis=-1)
    qk = jnp.einsum('bhntd,bhnsd->bhnts', qc, kc) * gam
    qg = qc * jnp.exp(G)[..., None]
    kd = kc * jnp.exp(G[..., -1:] - G)[..., None]
    gend = jnp.exp(G[..., -1])[..., None, None]

    def step(state, inp):
        u_i, w_i, qk_i, qg_i, kd_i, ge_i = inp
        v_new = u_i - jnp.einsum('bhtk,bhkv->bhtv', w_i, state)
        o_i = jnp.einsum('bhtk,bhkv->bhtv', qg_i, state) + jnp.einsum('bhts,bhsv->bhtv', qk_i, v_new)
        state = ge_i * state + jnp.einsum('bhsk,bhsv->bhkv', kd_i, v_new)
        return state, o_i

    xs = tuple(jnp.moveaxis(t, 2, 0) for t in (u, w, qk, qg, kd, gend))
    state0 = jnp.zeros(qc.shape[:2] + (HEAD_DIM, HEAD_DIM), F32)
    _, o = lax.scan(step, state0, xs)
    o = _unchunk(jnp.moveaxis(o, 0, 2))
    return _merge(_head_rms(o, onorm)) * jax.nn.silu(gate.astype(F32))


def _retention(q, k, v, gate, onorm, cos, sin):
    tril, _ = _masks()
    q = _rope(_heads(q).astype(F32), cos, sin)
    k = _rope(_heads(k).astype(F32), cos, sin) * HEAD_DIM ** -0.5
    v = _heads(v).astype(F32)
    lgh = jnp.log1p(-jnp.exp2(-RET_DECAY_EXP - jnp.arange(N_HEADS, dtype=F32)))
    n = jnp.arange(CHUNK, dtype=F32)
    dmat = _masked_exp(tril[None], (n[:, None] - n[None, :])[None] * lgh[:, None, None])
    w_end = jnp.exp((CHUNK - 1 - n)[None, :] * lgh[:, None])[None, :, None, :, None]
    w_start = jnp.exp((n + 1)[None, :] * lgh[:, None])[None, :, None, :, None]
    qc, kc, vc = _chunks(q), _chunks(k), _chunks(v)
    scores = jnp.einsum('bhntd,bhnsd->bhnts', qc, kc) * dmat[None, :, None]
    inner = jnp.einsum('bhnts,bhnsd->bhntd', scores, vc)
    upd = jnp.einsum('bhnsk,bhnsv->bhnkv', kc * w_end, vc)
    decay = jnp.broadcast_to(jnp.exp(CHUNK * lgh)[None, :, None, None, None], (1, N_HEADS, qc.shape[2], 1, 1))
    prev = _scan_states(decay, upd)
    cross = jnp.einsum('bhntk,bhnkv->bhntv', qc * w_start, prev)
    o = _unchunk(inner + cross)
    return _merge(_head_ln(o, onorm)) * jax.nn.silu(gate.astype(F32))


def _forgetting_attention(q, k, v, gate, f, qnorm, knorm, fbias):
    q = _head_rms(_heads(q), qnorm) * HEAD_DIM ** -0.5
    k = _head_rms(_heads(k), knorm)
    v = _heads(v).astype(F32)
    logf = jax.nn.log_sigmoid(f.astype(F32) + fbias.astype(F32)).transpose(0, 2, 1)
    c = jnp.cumsum(logf, axis=-1)
    bsz, nh, seq, dh = q.shape
    nb = seq // QBLK
    qb = jnp.moveaxis(q.reshape(bsz, nh, nb, QBLK, dh), 2, 0)
    cb = jnp.moveaxis(c.reshape(bsz, nh, nb, QBLK), 2, 0)
    kpos = jnp.arange(seq)

    def block(inp):
        i, q_i, c_i = inp
        qpos = i * QBLK + jnp.arange(QBLK)
        s = jnp.einsum('bhqd,bhkd->bhqk', q_i, k) + c_i[..., None] - c[:, :, None, :]
        s = jnp.where(kpos[None, :] <= qpos[:, None], s, NEG_BIG)
        return jnp.einsum('bhqk,bhkd->bhqd', jax.nn.softmax(s, axis=-1), v)

    o = lax.map(block, (jnp.arange(nb), qb, cb))
    o = jnp.moveaxis(o, 0, 2).reshape(bsz, nh, seq, dh)
    return _merge(o) * jax.nn.sigmoid(gate.astype(F32))


def _hgrn2(q, f, i, gate, lb, onorm):
    tril, _ = _masks()
    q = jax.nn.silu(_heads(q).astype(F32))
    f = _heads(f).astype(F32)
    v = _heads(i).astype(F32)
    lb = lb.astype(F32).reshape(1, N_HEADS, 1, HEAD_DIM)
    log_fg = jnp.log(lb + (1.0 - lb) * jax.nn.sigmoid(f))
    k = (1.0 - lb) * jax.nn.sigmoid(-f)
    qc, kc, vc = _chunks(q), _chunks(k), _chunks(v)
    bcum = jnp.cumsum(_chunks(log_fg), axis=3)
    mask5 = tril[:, :, None]

    def inner(inp):
        q_i, k_i, v_i, b_i = inp
        dec = _masked_exp(mask5, b_i[:, :, :, None, :] - b_i[:, :, None, :, :])
        scores = jnp.einsum('bhtk,bhtsk,bhsk->bhts', q_i, dec, k_i)
        return jnp.einsum('bhts,bhsv->bhtv', scores, v_i)

    o_in = jnp.moveaxis(lax.map(inner, tuple(jnp.moveaxis(t, 2, 0) for t in (qc, kc, vc, bcum))), 0, 2)
    bend = bcum[..., -1:, :]
    upd = jnp.einsum('bhnsk,bhnsv->bhnkv', kc * jnp.exp(bend - bcum), vc)
    prev = _scan_states(jnp.swapaxes(jnp.exp(bend), -1, -2), upd)
    cross = jnp.einsum('bhntk,bhnkv->bhntv', qc * jnp.exp(bcum), prev)
    o = _unchunk(o_in + cross)
    return _merge(_head_rms(o, onorm)) * jax.nn.silu(gate.astype(F32))


def setup_inputs(seed: int = 0) -> dict:
    key = jax.random.key(seed)
    ks = jax.random.split(key, 20)
    L = DEPTH
    nrm = jax.random.normal
    x = nrm(ks[0], (BATCH, SEQ, D_MODEL), F32)
    norm_mix = 1.0 + 0.02 * nrm(ks[1], (L, D_MODEL), F32)
    norm_ffn = 1.0 + 0.02 * nrm(ks[2], (L, D_MODEL), F32)
    w_in = nrm(ks[3], (L, D_MODEL, PROJ_WIDTH), F32) * D_MODEL ** -0.5
    conv_qkv_a = nrm(ks[4], (L, SHORT_CONV, 3 * GROUP_WIDTH), F32) * SHORT_CONV ** -0.5
    a_log_a = jnp.log(jax.random.uniform(ks[5], (L, N_HEADS), F32, 1.0, 16.0))
    dt = jnp.exp(jax.random.uniform(ks[6], (L, N_HEADS), F32, float(np.log(1e-3)), float(np.log(1e-1))))
    dt_bias_a = dt + jnp.log(-jnp.expm1(-dt))
    onorm_a = 1.0 + 0.02 * nrm(ks[7], (L, HEAD_DIM), F32)
    onorm_b = 1.0 + 0.02 * nrm(ks[8], (L, HEAD_DIM), F32)
    qnorm_c = 1.0 + 0.02 * nrm(ks[9], (L, HEAD_DIM), F32)
    knorm_c = 1.0 + 0.02 * nrm(ks[10], (L, HEAD_DIM), F32)
    fbias_c = jax.random.uniform(ks[11], (L, N_HEADS), F32, 1.0, 4.0)
    lower_bound_d = 0.02 * nrm(ks[12], (L, GROUP_WIDTH), F32)
    onorm_d = 1.0 + 0.02 * nrm(ks[13], (L, HEAD_DIM), F32)
    w_out = nrm(ks[14], (L, MIX_WIDTH, D_MODEL), F32) * MIX_WIDTH ** -0.5
    w_up = nrm(ks[15], (L, D_MODEL, 2 * D_FF), F32) * D_MODEL ** -0.5
    conv_ffn = nrm(ks[16], (L, FFN_CONV, 2 * D_FF), F32) * FFN_CONV ** -0.5
    w_down = nrm(ks[17], (L, D_FF, D_MODEL), F32) * D_FF ** -0.5
    return {'x': x, 'norm_mix': norm_mix, 'norm_ffn': norm_ffn, 'w_in': w_in, 'conv_qkv_a': conv_qkv_a, 'a_log_a': a_log_a, 'dt_bias_a': dt_bias_a, 'onorm_a': onorm_a, 'onorm_b': onorm_b, 'qnorm_c': qnorm_c, 'knorm_c': knorm_c, 'fbias_c': fbias_c, 'lower_bound_d': lower_bound_d, 'onorm_d': onorm_d, 'w_out': w_out, 'w_up': w_up, 'conv_ffn': conv_ffn, 'w_down': w_down}


def reference(x, norm_mix, norm_ffn, w_in, conv_qkv_a, a_log_a, dt_bias_a, onorm_a, onorm_b, qnorm_c, knorm_c, fbias_c, lower_bound_d, onorm_d, w_out, w_up, conv_ffn, w_down):
    seq = x.shape[1]
    inv_freq = ROPE_BASE ** (-jnp.arange(0, HEAD_DIM, 2, dtype=F32) / HEAD_DIM)
    ang = jnp.arange(seq, dtype=F32)[:, None] * inv_freq[None, :]
    cos, sin = jnp.cos(ang), jnp.sin(ang)
    lbs = jax.nn.softmax(lower_bound_d.astype(F32), axis=0)
    lbs = jnp.cumsum(lbs, axis=0) - lbs[0]
    offsets = np.cumsum(PROJ_SIZES)[:-1].tolist()
    h = x
    for l in range(DEPTH):
        u = _rmsnorm(h, norm_mix[l])
        p = u @ w_in[l]
        (qa, ka, va, ga, ba, aa, qb, kb, vb, gb, qc, kc, vc, gc, fc, qd, fd, id_, gd) = jnp.split(p, offsets, axis=-1)
        ya = _gated_deltanet(qa, ka, va, ga, ba, aa, conv_qkv_a[l], a_log_a[l], dt_bias_a[l], onorm_a[l])
        yb = _retention(qb, kb, vb, gb, onorm_b[l], cos, sin)
        yc = _forgetting_attention(qc, kc, vc, gc, fc, qnorm_c[l], knorm_c[l], fbias_c[l])
        yd = _hgrn2(qd, fd, id_, gd, lbs[l], onorm_d[l])
        mix = jnp.concatenate([ya, yb, yc, yd], axis=-1).astype(h.dtype)
        h = h + mix @ w_out[l]
        u = _rmsnorm(h, norm_ffn[l])
        up = _causal_dwconv(u @ w_up[l], conv_ffn[l])
        g_ff, v_ff = jnp.split(up, 2, axis=-1)
        h = h + (jax.nn.silu(g_ff) * v_ff) @ w_down[l]
    return h
```

```python
import contextlib
import numpy as np
import concourse.bass as bass
import concourse.mybir as mybir
from concourse.bass_utils import run_bass_kernel_spmd

F32 = mybir.dt.float32
BF16 = mybir.dt.bfloat16
AF = mybir.ActivationFunctionType
ALU = mybir.AluOpType
AX = mybir.AxisListType

EPOCH = 20000
ENGS = ("pe", "act", "dve", "pool", "sp")


class Res:
    __slots__ = ("name", "w", "r", "excl")

    def __init__(self, name="", excl=False):
        self.name = name
        self.w = None
        self.r = []
        self.excl = excl


class V:
    __slots__ = ("ap", "res")

    def __init__(self, ap, res):
        self.ap = ap
        self.res = tuple(res)

    def __getitem__(self, idx):
        return V(self.ap[idx], self.res)

    def re(self, pat, **kw):
        return V(self.ap.rearrange(pat, **kw), self.res)

    def with_res(self, *res):
        return V(self.ap, res)


class Tok:
    __slots__ = ("eng", "idx", "ordn", "dma", "sig")

    def __init__(self, eng, idx, ordn, dma, sig=True):
        self.eng, self.idx, self.ordn, self.dma, self.sig = eng, idx, ordn, dma, sig


class Prog:
    def __init__(self, nc):
        self.nc = nc
        self.ops = {e: [] for e in ENGS}
        self.sigc = {e: 0 for e in ENGS}
        self.seen = {e: {} for e in ENGS}
        self.dma_cnt = {}
        self.n = 0

    def op(self, eng, fn, reads=(), writes=(), sig=True, dma=None):
        lst = self.ops[eng]
        idx = len(lst)
        waits = []
        seen = self.seen[eng]

        def need(t, kind):
            if t.dma is not None:
                key, val = t.dma
                val = self.dma_cnt[key]
                if seen.get(key, 0) < val:
                    seen[key] = val
                    waits.append((key, val))
                return
            if t.eng == eng:
                if not t.sig:
                    return
                o = t.ordn
            else:
                o = t.ordn
                if o is None:
                    raise RuntimeError("dep on non-signalling op")
                if o > self.sigc[t.eng]:
                    raise RuntimeError("dep on a signal not yet emitted (%s->%s)" % (t.eng, eng))
            if seen.get(t.eng, 0) < o:
                seen[t.eng] = o
                waits.append((t.eng, o))

        for r in reads:
            for rr in r.res:
                if rr.w is not None:
                    need(rr.w, "raw")
                if rr.excl:
                    for t in rr.r:
                        if t.eng != eng:
                            need(t, "rar")
        for r in writes:
            for rr in r.res:
                if rr.w is not None:
                    need(rr.w, "waw")
                for t in rr.r:
                    need(t, "war")
        if dma is not None:
            self.dma_cnt[dma] = self.dma_cnt.get(dma, 0) + 16
            tok = Tok(eng, idx, None, (dma, self.dma_cnt[dma]))
            sig = False
        else:
            if sig:
                self.sigc[eng] += 1
                tok = Tok(eng, idx, self.sigc[eng], None)
            else:
                tok = Tok(eng, idx, self.sigc[eng] + 1, None, False)
        for r in reads:
            for rr in r.res:
                rr.r.append(tok)
        for r in writes:
            for rr in r.res:
                rr.w = tok
                rr.r = []
        lst.append((fn, waits, sig, dma))
        self.n += 1
        return tok

    def barrier(self):
        for e in ENGS:
            waits = []
            seen = self.seen[e]
            for e2 in ENGS:
                if e2 != e and self.sigc[e2] > seen.get(e2, 0):
                    seen[e2] = self.sigc[e2]
                    waits.append((e2, self.sigc[e2]))
            for k, v in self.dma_cnt.items():
                if seen.get(k, 0) < v:
                    seen[k] = v
                    waits.append((k, v))
            if waits:
                self.ops[e].append((None, waits, False, None))

    def emit(self, stack):
        nc = self.nc
        sems = {}

        def getsem(key, ep=0):
            k = (key, ep)
            if k not in sems:
                sems[k] = stack.enter_context(nc.semaphore("s_%s_%d" % (str(key), ep)))
            return sems[k]

        for e in ENGS:
            for ep in range((self.sigc[e] + EPOCH - 1) // EPOCH + 1):
                getsem(e, ep)
        for k in self.dma_cnt:
            getsem(k, 0)

        def run(engname, eng):
            cnt = 0
            for fn, waits, sig, dma in self.ops[engname]:
                for key, val in waits:
                    if key in ENGS:
                        ep, v = (val - 1) // EPOCH, (val - 1) % EPOCH + 1
                        eng.wait_ge(getsem(key, ep), v)
                    else:
                        eng.wait_ge(getsem(key, 0), val)
                if fn is None:
                    continue
                ins = fn(eng)
                if dma is not None:
                    ins.then_inc(getsem(dma, 0), 16)
                elif sig:
                    cnt += 1
                    ins.then_inc(getsem(engname, (cnt - 1) // EPOCH), 1)

        with nc.Block() as block:
            @block.tensor
            def _(eng):
                run("pe", eng)

            @block.scalar
            def _(eng):
                run("act", eng)

            @block.vector
            def _(eng):
                run("dve", eng)

            @block.gpsimd
            def _(eng):
                run("pool", eng)

            @block.sync
            def _(eng):
                run("sp", eng)

    def mm(self, out, lhsT, rhs, start=True, stop=True, sig=False):
        return self.op("pe", lambda e: e.matmul(out.ap, lhsT.ap, rhs.ap, start=start, stop=stop),
                       reads=(lhsT, rhs), writes=(out,), sig=sig)

    def tr(self, out, in_, ident, sig=True):
        return self.op("pe", lambda e: e.transpose(out.ap, in_.ap, ident.ap),
                       reads=(in_, ident), writes=(out,), sig=sig)

    def act(self, out, in_, func, bias=None, scale=1.0, accum=None, eng="act"):
        reads = [in_]
        kw = {}
        if bias is not None:
            if isinstance(bias, V):
                reads.append(bias)
                kw["bias"] = bias.ap
            else:
                kw["bias"] = bias
        if isinstance(scale, V):
            reads.append(scale)
            kw["scale"] = scale.ap
        else:
            kw["scale"] = scale
        writes = [out]
        if accum is not None:
            writes.append(accum)
            kw["accum_out"] = accum.ap
        return self.op("act", lambda e: e.activation(out.ap, in_.ap, func, **kw), reads=reads, writes=writes)

    def tt(self, eng, out, in0, in1, op):
        return self.op(eng, lambda e: e.tensor_tensor(out.ap, in0.ap, in1.ap, op), reads=(in0, in1), writes=(out,))

    def ts(self, eng, out, in0, s1, op0, s2=None, op1=None):
        reads = [in0]
        a1 = s1
        if isinstance(s1, V):
            reads.append(s1)
            a1 = s1.ap
        a2 = s2
        if isinstance(s2, V):
            reads.append(s2)
            a2 = s2.ap
        if op1 is None:
            return self.op(eng, lambda e: e.tensor_scalar(out.ap, in0.ap, a1, None, op0), reads=reads, writes=(out,))
        return self.op(eng, lambda e: e.tensor_scalar(out.ap, in0.ap, a1, a2, op0, op1), reads=reads, writes=(out,))

    def stt(self, eng, out, in0, s, in1, op0, op1):
        reads = [in0, in1]
        a = s
        if isinstance(s, V):
            reads.append(s)
            a = s.ap
        return self.op(eng, lambda e: e.scalar_tensor_tensor(out.ap, in0.ap, a, in1.ap, op0, op1), reads=reads, writes=(out,))

    def copy(self, eng, out, in_):
        if eng == "act":
            return self.op(eng, lambda e: e.copy(out.ap, in_.ap), reads=(in_,), writes=(out,))
        return self.op(eng, lambda e: e.tensor_copy(out.ap, in_.ap), reads=(in_,), writes=(out,))

    def memset(self, eng, out, val):
        return self.op(eng, lambda e: e.memset(out.ap, val), writes=(out,))

    def dma(self, eng, out, in_, key):
        return self.op(eng, lambda e: e.dma_start(out=out.ap, in_=in_.ap), reads=(in_,), writes=(out,), dma=key)


class Alloc:
    def __init__(self, nc, base=16640, limit=229000):
        self.nc, self.base, self.off, self.limit = nc, base, base, limit
        self.cnt = 0

    def mark(self):
        return self.off

    def reset(self, m):
        self.off = m

    def sb(self, shape, dtype, nres=1):
        nbytes = int(np.prod(shape[1:])) * (4 if dtype == F32 else 2)
        nbytes = (nbytes + 63) // 64 * 64
        if self.off + nbytes > self.limit:
            raise RuntimeError("SBUF overflow: %d + %d" % (self.off, nbytes))
        self.cnt += 1
        t = self.nc.alloc_sbuf_tensor_at("t%d" % self.cnt, list(shape), dtype, offset=self.off)
        self.off += nbytes
        return V(t[tuple(slice(None) for _ in shape)], [Res("t%d" % self.cnt) for _ in range(nres)])


D = 1024
NCH = 8
DFF = 2816
NF = 44
NJ = 22
EPS = 1e-6
POOLENG = "dve"
TT3 = 256


class Ctx:
    pass


def rmsnorm_tile(P, cx, ht, sq, uT, ps, rs, TT, ei, ulo=None, uf=None):
    P.act(sq, ht, AF.Square)
    for c in range(NCH):
        P.mm(ps[:, 0:TT], cx.ones_bf, sq[:, c, :], start=(c == 0), stop=(c == NCH - 1), sig=(c == NCH - 1))
    P.act(rs[:, 0:TT], ps[:, 0:TT], AF.Sqrt, bias=cx.epsc, scale=1.0 / D)
    P.op("dve", lambda e: e.reciprocal(rs.ap[:, 0:TT], rs.ap[:, 0:TT]), reads=(rs,), writes=(rs,))
    for c in range(NCH):
        if ulo is None:
            P.tt("dve" if c % 2 == 0 else POOLENG, uT[:, c, :], ht[:, c, :], rs[:, 0:TT], ALU.mult)
        else:
            f = uf[c % 2]
            P.tt("dve", f, ht[:, c, :], rs[:, 0:TT], ALU.mult)
            P.copy("act", uT[:, c, :], f)
            P.tt("dve", ulo[:, c, :], f, uT[:, c, :], ALU.subtract)


def load_weights_cast(P, cx, dst, src_ap_fn, nparts, stage, scale_fn, k0=0):
    pass


def pass3(P, A, cx, l, L, S, h_in, h_out):
    TT = TT3
    m = A.mark()
    ps = cx.ps
    ht = A.sb([128, NCH, TT], F32)
    X = [A.sb([128, TT + 2], F32) for _ in range(4)]
    Y = [A.sb([128, TT], F32) for _ in range(4)]
    SA = [A.sb([128, TT], F32) for _ in range(2)]
    rs = A.sb([128, TT], F32)
    hres = [A.sb([128, TT], F32) for _ in range(3)]
    carry = [A.sb([128, 2], F32) for _ in range(NF)]
    sqb = A.sb([128, NCH, TT], BF16)
    uT = A.sb([128, NCH, TT], BF16)
    for fc in range(NF):
        P.memset("dve", carry[fc], 0.0)
    nt = S // TT
    hin_v = h_in.re("(c p) s -> p c s", p=128)
    hout_v = h_out.re("(c p) s -> p c s", p=128)
    P.dma("sp", ht, hin_v[:, :, 0:TT], "ht")

    def ffn_tile(t, up_group, down_group, Gbuf, prep):
        t0 = t * TT
        rmsnorm_tile(P, cx, ht, sqb, uT, ps[7], rs, TT, 0)
        prep()
        k = 0
        for j in range(NJ):
            ys = []
            for half in range(2):
                fc = j + half * NJ
                pb = ps[k % 4]
                up_group(pb, fc)
                x = X[k % 4]
                y = Y[k % 4]
                P.copy("dve", x[:, 0:2], carry[fc])
                P.act(x[:, 2:TT + 2], pb[:, 0:TT], AF.Copy)
                cwb = (l * NF + fc) * 3
                P.act(y, x[:, 0:TT], AF.Copy, scale=cx.cw[:, cwb:cwb + 1])
                P.stt("dve", y, x[:, 1:TT + 1], cx.cw[:, cwb + 1:cwb + 2], y, ALU.mult, ALU.add)
                P.stt("dve", y, x[:, 2:TT + 2], cx.cw[:, cwb + 2:cwb + 3], y, ALU.mult, ALU.add)
                P.copy("dve", carry[fc], x[:, TT:TT + 2])
                ys.append(y)
                k += 1
            sa = SA[j % 2]
            P.act(sa, ys[0], AF.Silu)
            P.tt("dve", Gbuf[:, j, :], sa, ys[1], ALU.mult)
        if t + 1 < nt:
            P.dma("sp", ht, hin_v[:, :, t0 + TT:t0 + 2 * TT], "ht")
        for dc in range(NCH):
            hr = hres[dc % 3]
            P.dma("sp", hr, hin_v[:, dc, t0:t0 + TT], "hres%d" % (dc % 3))
            pb = ps[4 + dc % 2]
            down_group(pb, dc)
            P.tt("dve", hr, hr, pb[:, 0:TT], ALU.add)
            P.dma("sp", hout_v[:, dc, t0:t0 + TT], hr, "hst%d" % (dc % 3))

    m2 = A.mark()
    ufull = A.sb([128, NCH, TT], F32)
    G32 = A.sb([128, NJ, TT], F32)
    wtu = [A.sb([128, NCH, 128], F32) for _ in range(2)]
    wtd = [A.sb([128, NJ, 128], F32) for _ in range(2)]
    cnt = [0, 0]

    def prep0():
        for c in range(NCH):
            P.stt("dve", ufull[:, c, :], ht[:, c, :], cx.gffn[:, l * NCH + c:l * NCH + c + 1], rs[:, 0:TT], ALU.mult, ALU.mult)

    def up0(pb, fc):
        wt = wtu[cnt[0] % 2]
        P.dma("sp", wt, V(cx.d_wup[l, :, fc * 128:(fc + 1) * 128].rearrange("(c p) m -> p c m", p=128), [cx.r_w]), "wu%d" % (cnt[0] % 2))
        cnt[0] += 1
        for c in range(NCH):
            P.mm(pb[:, 0:TT], wt[:, c, :], ufull[:, c, :], start=(c == 0), stop=(c == NCH - 1), sig=(c == NCH - 1))

    def down0(pb, dc):
        wt = wtd[cnt[1] % 2]
        P.dma("sp", wt, V(cx.d_wdn[l, :, dc * 128:(dc + 1) * 128].rearrange("(j p) m -> p j m", p=128), [cx.r_w]), "wd%d" % (cnt[1] % 2))
        cnt[1] += 1
        for j in range(NJ):
            P.mm(pb[:, 0:TT], wt[:, j, :], G32[:, j, :], start=(j == 0), stop=(j == NJ - 1), sig=(j == NJ - 1))

    ffn_tile(0, up0, down0, G32, prep0)
    A.reset(m2)
    if nt > 1:
        Wup = A.sb([128, NCH, 2 * DFF], BF16)
        Wdn = A.sb([128, NJ, D], BF16)
        stage = [A.sb([128, 1408], F32) for _ in range(2)]
        G = A.sb([128, NJ, TT], BF16)
        wi = 0
        for c in range(NCH):
            for hf in range(4):
                st = stage[wi % 2]
                P.dma("sp", st, V(cx.d_wup[l, c * 128:(c + 1) * 128, hf * 1408:(hf + 1) * 1408], [cx.r_w]), "stg%d" % (wi % 2))
                dst = Wup[:, c, hf * 1408:(hf + 1) * 1408]
                g = cx.gffn[:, l * NCH + c:l * NCH + c + 1]
                if wi % 2 == 0:
                    P.act(dst, st, AF.Copy, scale=g)
                else:
                    P.ts("dve", dst, st, g, ALU.mult)
                wi += 1
        for j in range(NJ):
            st = stage[wi % 2]
            P.dma("sp", st[:, 0:1024], V(cx.d_wdn[l, j * 128:(j + 1) * 128, :], [cx.r_w]), "stg%d" % (wi % 2))
            if wi % 2 == 0:
                P.copy("act", Wdn[:, j, :], st[:, 0:1024])
            else:
                P.copy("dve", Wdn[:, j, :], st[:, 0:1024])
            wi += 1

        def up1(pb, fc):
            for c in range(NCH):
                P.mm(pb[:, 0:TT], Wup[:, c, fc * 128:(fc + 1) * 128], uT[:, c, :], start=(c == 0), stop=(c == NCH - 1), sig=(c == NCH - 1))

        def down1(pb, dc):
            for j in range(NJ):
                P.mm(pb[:, 0:TT], Wdn[:, j, dc * 128:(dc + 1) * 128], G[:, j, :], start=(j == 0), stop=(j == NJ - 1), sig=(j == NJ - 1))

        for t in range(1, nt):
            ffn_tile(t, up1, down1, G, lambda: None)
    P.barrier()
    A.reset(m)


T = 128
NEG = -30000.0
WC = 4624
C_QA, C_KA, C_VA, C_GA = 0, 256, 512, 768
C_QB, C_QBS, C_KB, C_KBS, C_GB = 1024, 1280, 1536, 1792, 2048
C_QC, C_KC = 2304, 2560
C_QD, C_FD, C_GD = 2816, 3072, 3328
C_VB, C_VC, C_ID, C_GC = 3584, 3840, 4096, 4352
C_BA, C_AA, C_FC = 4608, 4612, 4616
LO_RANGES = ()


def lo_col(col):
    for a0, a1, l0 in LO_RANGES:
        if a0 <= col < a1:
            return l0 + col - a0
    return None


class Stream:
    def __init__(self, tmps, pss):
        self.tmps, self.pss, self.ti, self.pi = tmps, pss, 0, 0

    def tmp(self):
        self.ti += 1
        return self.tmps[self.ti % len(self.tmps)]

    def ps(self):
        self.pi += 1
        return self.pss[self.pi % len(self.pss)]


def run_streams(gens):
    gens = list(gens)
    while gens:
        nxt = []
        for g in gens:
            try:
                next(g)
                nxt.append(g)
            except StopIteration:
                pass
        gens = nxt


def pass1(P, A, cx, l, L, S, h_in):
    m = A.mark()
    W = A.sb([128, NCH, WC], BF16)
    ufull = A.sb([128, NCH, T], F32)
    wring = [A.sb([128, NCH, 256], F32) for _ in range(2)]
    cx.wri = 0
    NRING = 104
    ringmem = A.sb([128, NRING * T], F32, nres=NRING)
    ring = [V(ringmem.ap[:, i * T:(i + 1) * T], [ringmem.res[i]]) for i in range(NRING)]
    stage = [V(ringmem.ap[:, k * 1280:k * 1280 + 1156], ringmem.res[10 * k:10 * k + 10]) for k in range(2)]
    ps = cx.ps
    pslot = [V(ps[i // 4].ap[:, (i % 4) * T:(i % 4 + 1) * T], ps[i // 4].res) for i in range(32)]
    main = Stream(ring[80:104], [pslot[29], pslot[30], pslot[31]])
    ps256 = [V(ps[6].ap[:, 0:256], ps[6].res), V(ps[6].ap[:, 256:512], ps[6].res)]
    cx.p256 = 0

    def streams(n):
        nt_, np_ = 80 // n, 24 // n
        return [Stream(ring[i * nt_:(i + 1) * nt_], pslot[i * np_:(i + 1) * np_]) for i in range(n)]
    wi = 0
    PW = 1156
    for c in range(NCH):
        for hf in range(4):
            st = stage[wi % 2]
            c0 = hf * PW
            P.dma("sp", st, V(cx.d_win[l, c * 128:(c + 1) * 128, c0:c0 + PW], [cx.r_w]), "stg%d" % (wi % 2))
            dst = W[:, c, c0:c0 + PW]
            g = cx.gmix[:, l * NCH + c:l * NCH + c + 1]
            if wi % 2 == 0:
                P.act(dst, st, AF.Copy, scale=g)
            else:
                P.ts("dve", dst, st, g, ALU.mult)
            wi += 1
    pp = A.sb([128, 24], F32)
    P.dma("sp", pp, cx.d_pp[:, l * 24:(l + 1) * 24], "pp")
    p4 = A.sb([4, 8], F32)
    P.dma("sp", p4, cx.d_p4[:, l * 8:(l + 1) * 8], "pp")
    pq = A.sb([128, 16], F32)
    P.dma("sp", pq, cx.d_pq[:, l * 16:(l + 1) * 16], "pp")
    nega = A.sb([4, 1], F32)
    P.act(nega, p4[:, 3:4], AF.Exp)
    P.ts("dve", nega, nega, -1.0, ALU.mult)
    oml = A.sb([128, 4], F32)
    lbv = A.sb([128, 2], F32)
    for fc_ in range(2):
        P.copy("dve", lbv[:, fc_:fc_ + 1], cx.lbs[:, fc_ * L + l:fc_ * L + l + 1])
    P.ts("dve", oml[:, 0:2], lbv, -1.0, ALU.mult, 1.0, ALU.add)
    P.ts("dve", oml[:, 2:4], oml[:, 0:2], -1.0, ALU.mult)
    SA = [A.sb([128, 64], F32) for _ in range(2)]
    SB = [A.sb([128, 64], F32) for _ in range(2)]
    SD = [A.sb([128, 64], F32) for _ in range(2)]
    for s_ in SA + SB + SD:
        P.memset("dve", s_, 0.0)
    ccar = A.sb([4, 1], F32)
    P.memset("dve", ccar, 0.0)
    cvx = [A.sb([128, T + 3], F32) for _ in range(6)]
    for b_ in cvx:
        P.memset("dve", b_, 0.0)
    ht = A.sb([128, NCH, T], F32)
    sqb = A.sb([128, NCH, T], BF16)
    uT = A.sb([128, NCH, T], BF16)
    rs = A.sb([128, T], F32)
    blk = {k: A.sb([128, T], F32) for k in ["qa0", "qa1", "ka0", "ka1", "va0", "va1", "vtokA0", "vtokA1", "ktokA0", "ktokA1",
                                             "gt8", "egend", "betat", "bexp", "oA0", "oA1"]}
    vtokB = A.sb([128, 256], F32)
    vtokD = A.sb([128, 256], F32)
    g8 = A.sb([8, T], F32)
    gneg = A.sb([8, T], F32)
    ropeb = [A.sb([128, T], F32) for _ in range(4)]
    pq2s = A.sb([128, 1], F32)
    P.ts("dve", pq2s, pq[:, 2:3], 0.125, ALU.mult)
    nfb = A.sb([4, 1], F32)
    P.ts("dve", nfb, p4[:, 2:3], -1.0, ALU.mult)
    c4 = {k: A.sb([4, T], F32) for k in ["c", "r1", "r2", "sp"]}
    c3b = A.sb([4, 3, T], BF16)
    c3n = A.sb([4, 3, T], BF16)
    obf = [A.sb([128, T], BF16) for _ in range(8)]
    vcb = A.sb([128, 256], BF16)
    v32 = A.sb([128, 256], F32)
    g32 = A.sb([128, 256], F32)
    gcb = A.sb([128, 256], BF16)
    cx.oi = 0
    cx.prec = False
    hin_v = h_in.re("(c p) s -> p c s", p=128)
    nb = S // T

    def wload(col, M):
        cx.wri += 1
        wt = wring[cx.wri % 2]
        src = cx.d_win[l, :, col:col + M].rearrange("(c p) m -> p c m", p=128)
        P.dma("sp", wt[:, :, 0:M], V(src, [cx.r_w]), "wr%d" % (cx.wri % 2))
        return wt

    def proj_fm(st, col, M=128):
        pb = st.ps()
        if cx.prec:
            wt = wload(col, M)
            for c in range(NCH):
                P.mm(pb[0:M, :], wt[:, c, 0:M], ufull[:, c, :], start=(c == 0), stop=(c == NCH - 1), sig=(c == NCH - 1))
        else:
            for c in range(NCH):
                P.mm(pb[0:M, :], W[:, c, col:col + M], uT[:, c, :], start=(c == 0), stop=(c == NCH - 1), sig=(c == NCH - 1))
        return pb[0:M, :]

    def proj_tm(col, N):
        if N > 128:
            cx.p256 += 1
            pb = ps256[cx.p256 % 2]
        else:
            pb = main.ps()
        if cx.prec:
            wt = wload(col, N)
            for c in range(NCH):
                P.mm(pb[:, 0:N], ufull[:, c, :], wt[:, c, 0:N], start=(c == 0), stop=(c == NCH - 1), sig=(c == NCH - 1))
        else:
            for c in range(NCH):
                P.mm(pb[:, 0:N], uT[:, c, :], W[:, c, col:col + N], start=(c == 0), stop=(c == NCH - 1), sig=(c == NCH - 1))
        return pb[:, 0:N]

    def headnorm(st, src, kind, gain, sum_scale=1.0):
        if kind == "ln":
            mp = st.ps()
            P.mm(mp, cx.mavg, src, True, True, sig=True)
            yield
            mean = st.tmp()
            P.copy("act", mean, mp)
            yield
            cen = st.tmp()
            P.tt("dve", cen, src, mean, ALU.subtract)
            src = cen
            yield
        sq = st.tmp()
        P.act(sq, src, AF.Square)
        yield
        e2 = st.ps()
        P.mm(e2, cx.mavg, sq, True, True, sig=True)
        yield
        r = st.tmp()
        P.act(r, e2, AF.Sqrt, bias=cx.epsc, scale=sum_scale)
        yield
        P.op("dve", lambda e: e.reciprocal(r.ap, r.ap), reads=(r,), writes=(r,))
        yield
        y = st.tmp()
        P.stt("dve", y, src, gain, r, ALU.mult, ALU.mult)
        yield
        return y

    def emit_mix(st, y, gate_col, fc, chunk, t0, func=AF.Silu):
        gate_ps = proj_fm(st, gate_col + fc * 128)
        yield
        sg = st.tmp()
        P.act(sg, gate_ps, func)
        yield
        ob = obf[cx.oi % 8]
        cx.oi += 1
        P.tt("dve", ob, y, sg, ALU.mult)
        P.dma("sp", cx.d_mixT[chunk * 128:(chunk + 1) * 128, t0:t0 + T], ob, "mx%d" % (cx.oi % 8))
        if cx.prec:
            o32 = st.tmp()
            P.tt("dve", o32, y, sg, ALU.mult)
            P.dma("sp", cx.d_mix0[chunk * 128:(chunk + 1) * 128, :], o32, "m0")
        yield

    def gla(st, qT, kT, lfT, vtok, St, fc, kind, gain, gate_col, chunk, t0):
        bc = st.tmp()
        for ci in range(2):
            P.op("dve", lambda e, ci=ci: e.tensor_tensor_scan(bc.ap[:, ci * 64:(ci + 1) * 64], cx.onesf.ap[:, 0:64], lfT.ap[:, ci * 64:(ci + 1) * 64], 0.0, ALU.mult, ALU.add),
                 reads=(cx.onesf, lfT), writes=(bc,))
        yield
        eb = st.tmp()
        P.act(eb, bc, AF.Exp)
        enb = st.tmp()
        P.act(enb, bc, AF.Exp, scale=-1.0)
        ke = st.tmp()
        for ci in range(2):
            P.act(ke[:, ci * 64:(ci + 1) * 64], bc[:, ci * 64:(ci + 1) * 64], AF.Exp, bias=bc[:, ci * 64 + 63:ci * 64 + 64], scale=-1.0)
        yield
        qt = st.tmp()
        P.tt("dve", qt, qT, eb, ALU.mult)
        kt = st.tmp()
        P.tt("dve", kt, kT, enb, ALU.mult)
        kend = st.tmp()
        P.tt("dve", kend, kT, ke, ALU.mult)
        yield
        kp = st.ps()
        P.tr(kp, kend, cx.ident)
        yield
        kendTok = st.tmp()
        P.copy("act", kendTok, kp)
        yield
        o = st.tmp()
        for hh in range(2):
            b = 64 * hh
            h = 2 * fc + hh
            sc = st.ps()
            P.mm(sc, kt[b:b + 64, :], qt[b:b + 64, :], True, True, sig=True)
            u0 = st.ps()
            P.mm(u0[b:b + 64, 0:64], kendTok[0:64, b:b + 64], vtok[0:64, h * 64:(h + 1) * 64], True, True, sig=True)
            yield
            PT = st.tmp()
            P.tt("dve", PT, sc, cx.mask_bd, ALU.mult)
            yield
            oa = st.ps()
            P.mm(oa[b:b + 64, :], vtok[:, h * 64:(h + 1) * 64], PT, True, False)
            P.mm(oa[b:b + 64, 0:64], St[b:b + 64, :], qt[b:b + 64, 0:64], False, True, sig=True)
            P.stt("dve", St[b:b + 64, :], St[b:b + 64, :], eb[b:b + 64, 63:64], u0[b:b + 64, 0:64], ALU.mult, ALU.add)
            yield
            ob_ = st.ps()
            P.mm(ob_[b:b + 64, 0:64], St[b:b + 64, :], qt[b:b + 64, 64:128], True, True, sig=True)
            u1 = st.ps()
            P.mm(u1[b:b + 64, 0:64], kendTok[64:128, b:b + 64], vtok[64:128, h * 64:(h + 1) * 64], True, True, sig=True)
            P.copy("act", o[b:b + 64, :], oa[b:b + 64, :])
            yield
            P.tt("dve", o[b:b + 64, 64:128], o[b:b + 64, 64:128], ob_[b:b + 64, 0:64], ALU.add)
            P.stt("dve", St[b:b + 64, :], St[b:b + 64, :], eb[b:b + 64, 127:128], u1[b:b + 64, 0:64], ALU.mult, ALU.add)
            yield
        y = yield from headnorm(st, o, kind, gain)
        yield from emit_mix(st, y, gate_col, fc, chunk, t0)

    def stream_B(st, fc, t0):
        out = {}
        for nm, c0, c1, ct in (("q", C_QB, C_QBS, 0), ("k", C_KB, C_KBS, 2)):
            a_ = proj_fm(st, c0 + fc * 128)
            b_ = proj_fm(st, c1 + fc * 128)
            yield
            t1 = st.tmp()
            P.tt("dve", t1, a_, ropeb[ct], ALU.mult)
            t2 = st.tmp()
            P.tt("dve", t2, b_, ropeb[ct + 1], ALU.mult)
            yield
            r_ = st.tmp()
            P.tt("dve", r_, t1, t2, ALU.add)
            out[nm] = r_
            yield
        yield from gla(st, out["q"], out["k"], cx.lgB[fc], vtokB, SB[fc], fc, "ln", pq[:, 1:2], C_GB, 2 + fc, t0)

    def stream_D(st, fc, t0):
        qp = proj_fm(st, C_QD + fc * 128)
        fp_ = proj_fm(st, C_FD + fc * 128)
        yield
        qd = st.tmp()
        P.act(qd, qp, AF.Silu)
        sg = st.tmp()
        P.act(sg, fp_, AF.Sigmoid)
        yield
        fg = st.tmp()
        P.ts("dve", fg, sg, oml[:, fc:fc + 1], ALU.mult, lbv[:, fc:fc + 1], ALU.add)
        kd = st.tmp()
        P.ts("dve", kd, sg, oml[:, 2 + fc:3 + fc], ALU.mult, oml[:, fc:fc + 1], ALU.add)
        yield
        lf = st.tmp()
        P.act(lf, fg, AF.Ln)
        yield
        yield from gla(st, qd, kd, lf, vtokD, SD[fc], fc, "rms", pq[:, 4:5], C_GD, 6 + fc, t0)

    def stream_Cqk(st, fc, which, col, gain_, dst, t0):
        pp_ = proj_fm(st, col + fc * 128)
        yield
        o = st.tmp()
        P.copy("act", o, pp_)
        yield
        y = yield from headnorm(st, o, "rms", gain_)
        ob = obf[cx.oi % 8]
        cx.oi += 1
        P.copy("act", ob, y)
        for hh in range(2):
            P.dma("sp", dst[2 * fc + hh, 0:64, t0:t0 + T], ob[64 * hh:64 * hh + 64, :], "cq%d" % (cx.oi % 8))
            if cx.prec:
                d0 = cx.d_q0 if which == "q" else cx.d_k0
                P.dma("sp", d0[2 * fc + hh, :, :], y[64 * hh:64 * hh + 64, :], "m0")
        yield

    def stream_Aconv(st, gi, nm, col, fc):
        xb = cvx[gi * 2 + fc]
        pp_ = proj_fm(st, col + fc * 128)
        P.copy("dve", xb[:, 0:3], xb[:, T:T + 3])
        yield
        P.copy("act", xb[:, 3:T + 3], pp_)
        yield
        y = st.tmp()
        cb = gi * 8 + fc * 4
        P.act(y, xb[:, 0:T], AF.Copy, scale=pp[:, cb:cb + 1])
        yield
        for k_ in range(1, 4):
            P.stt("dve", y, xb[:, k_:k_ + T], pp[:, cb + k_:cb + k_ + 1], y, ALU.mult, ALU.add)
            yield
        if nm == "va":
            P.act(blk["va%d" % fc], y, AF.Silu)
            yield
            src = blk["va%d" % fc]
        else:
            z = st.tmp()
            P.act(z, y, AF.Silu)
            yield
            yn = yield from headnorm(st, z, "rms", cx.c0125 if nm == "qa" else cx.onec, sum_scale=64.0)
            P.copy("act", blk[nm + str(fc)], yn)
            yield
            src = blk[nm + str(fc)]
        if nm != "qa":
            tp = st.ps()
            P.tr(tp, src, cx.ident)
            yield
            P.copy("act", blk[("ktokA" if nm == "ka" else "vtokA") + str(fc)], tp)
            yield

    def stream_Ahead(st, h):
        fc, hh = h // 2, h % 2
        b = 64 * hh
        kT = blk["ka%d" % fc]
        qT = blk["qa%d" % fc]
        ktok = blk["ktokA%d" % fc][:, b:b + 64]
        vtok = blk["vtokA%d" % fc][:, b:b + 64]
        St = SA[fc]
        kk = st.ps()
        P.mm(kk, kT[b:b + 64, :], kT[b:b + 64, :], True, True, sig=True)
        gb = st.ps()
        P.mm(gb, cx.sel4[:, h * 128:(h + 1) * 128], gneg[0:4, :], True, False)
        P.mm(gb, cx.ident, cx.mneg_strict, False, True, sig=True)
        x = st.tmp()
        P.act(x[:, 0:64], vtok, AF.Copy, scale=blk["betat"][:, h:h + 1])
        P.act(x[:, 64:128], ktok, AF.Copy, scale=blk["bexp"][:, 4 + h:5 + h])
        yield
        gam = st.tmp()
        P.act(gam, gb, AF.Exp, bias=blk["gt8"][:, h:h + 1], scale=1.0)
        yield
        Am = st.tmp()
        P.stt("dve", Am, kk, blk["betat"][:, h:h + 1], gam, ALU.mult, ALU.mult)
        yield
        atp = st.ps()
        P.tr(atp, Am, cx.ident)
        yield
        AT = st.tmp()
        P.copy("act", AT, atp)
        yield
        xp = st.ps()
        P.mm(xp, AT, x, True, True, sig=True)
        p2 = st.ps()
        P.mm(p2, Am, AT, True, True, sig=True)
        p1 = st.ps()
        P.mm(p1, AT, Am, True, True, sig=True)
        yield
        Pm, PTm = Am, AT
        for it in range(6):
            x2 = st.tmp()
            P.tt("dve", x2, x, xp, ALU.subtract if it == 0 else ALU.add)
            x = x2
            nPT = st.tmp()
            P.copy("act", nPT, p2)
            if it < 5:
                nP = st.tmp()
                P.copy("dve", nP, p1)
            else:
                nP = None
            yield
            xp = st.ps()
            P.mm(xp, nPT, x, True, True, sig=True)
            if it < 5:
                p2 = st.ps()
                P.mm(p2, nP, nPT, True, True, sig=True)
                if it < 4:
                    p1 = st.ps()
                    P.mm(p1, nPT, nP, True, True, sig=True)
            yield
            Pm, PTm = nP, nPT
        x2 = st.tmp()
        P.tt("dve", x2, x, xp, ALU.add)
        x = x2
        yield
        wtp = st.ps()
        if hh == 0:
            P.tr(wtp[0:64, :], x[:, 64:128], cx.ident)
        else:
            P.tr(wtp, x, cx.ident)
        kq = st.ps()
        P.mm(kq, kT[b:b + 64, :], qT[b:b + 64, :], True, True, sig=True)
        gb2a = st.ps()
        P.mm(gb2a, cx.sel4[:, h * 128:(h + 1) * 128], g8[0:4, :], True, True, sig=True)
        gb2 = st.ps()
        P.mm(gb2, cx.sel4[:, h * 128:(h + 1) * 128], g8[0:4, :], True, False)
        P.mm(gb2, cx.ident, cx.mneg_tril, False, True, sig=True)
        yield
        wT = st.tmp()
        P.copy("act", wT[b:b + 64, :], wtp[b:b + 64, :])
        egbc = st.tmp()
        P.act(egbc, gb2a, AF.Exp)
        gam2 = st.tmp()
        P.act(gam2, gb2, AF.Exp, bias=blk["gt8"][:, 4 + h:5 + h], scale=1.0)
        ed = st.tmp()
        P.act(ed[:, 0:1], blk["gt8"][:, h:h + 1], AF.Exp, bias=blk["egend"][:, h:h + 1], scale=-1.0)
        yield
        ws = st.ps()
        P.mm(ws[:, 0:64], wT[b:b + 64, :], St[b:b + 64, :], True, True, sig=True)
        PT = st.tmp()
        P.tt("dve", PT, kq, gam2, ALU.mult)
        qg = st.tmp()
        P.tt("dve", qg[b:b + 64, :], qT[b:b + 64, :], egbc[b:b + 64, :], ALU.mult)
        kd = st.tmp()
        P.ts("dve", kd[:, 0:64], ktok, ed[:, 0:1], ALU.mult)
        yield
        vnew = st.tmp()
        P.tt("dve", vnew[:, 0:64], x[:, 0:64], ws[:, 0:64], ALU.subtract)
        yield
        op_ = st.ps()
        P.mm(op_[b:b + 64, :], St[b:b + 64, :], qg[b:b + 64, :], True, False)
        P.mm(op_[b:b + 64, :], vnew[:, 0:64], PT, False, True, sig=True)
        up = st.ps()
        P.mm(up[b:b + 64, 0:64], kd[:, 0:64], vnew[:, 0:64], True, True, sig=True)
        yield
        P.copy("act", blk["oA%d" % fc][b:b + 64, :], op_[b:b + 64, :])
        P.stt("dve", St[b:b + 64, :], St[b:b + 64, :], blk["egend"][b:b + 64, 4 + h:5 + h], up[b:b + 64, 0:64], ALU.mult, ALU.add)
        yield

    def stream_Afin(st, fc, t0):
        y = yield from headnorm(st, blk["oA%d" % fc], "rms", pq[:, 0:1])
        yield from emit_mix(st, y, C_GA, fc, fc, t0)

    P.dma("sp", ht, hin_v[:, :, 0:T], "ht")
    for bi in range(nb):
        t0 = bi * T
        prec = bi == 0
        cx.prec = prec
        rmsnorm_tile(P, cx, ht, sqb, uT, ps[7], rs, T, 0)
        if prec:
            for c in range(NCH):
                P.stt("dve", ufull[:, c, :], ht[:, c, :], cx.gmix[:, l * NCH + c:l * NCH + c + 1], rs[:, 0:T], ALU.mult, ALU.mult)
        if bi + 1 < nb:
            P.dma("sp", ht, hin_v[:, :, t0 + T:t0 + 2 * T], "ht")
        for i_, nm in enumerate(["cosq", "sinq", "cosk", "sink"]):
            P.dma("sp", ropeb[i_], cx.d_rope[nm][:, t0:t0 + T], "rope%d" % i_)
        vp = proj_tm(C_VC, 256)
        P.copy("act", vcb, vp)
        P.dma("sp", cx.d_vc[t0:t0 + T, :], vcb, "vc")
        if prec:
            P.copy("act", v32, vp)
            P.dma("sp", cx.d_v0, v32, "m0")
        gp = proj_tm(C_GC, 256)
        P.act(gcb, gp, AF.Sigmoid)
        P.dma("sp", cx.d_gc[t0:t0 + T, :], gcb, "gc")
        if prec:
            P.act(g32, gp, AF.Sigmoid)
            P.dma("sp", cx.d_g0, g32, "m0")
        vp = proj_tm(C_VB, 256)
        P.copy("act", vtokB, vp)
        vp = proj_tm(C_ID, 256)
        P.copy("act", vtokD, vp)
        bp = proj_tm(C_BA, 4)
        P.act(blk["betat"][:, 0:4], bp[:, 0:4], AF.Sigmoid)
        fp = proj_fm(main, C_FC, 4)
        P.act(c4["sp"], fp, AF.Exp, bias=nfb[:, 0:1], scale=-1.0)
        P.act(c4["sp"], c4["sp"], AF.Ln, bias=1.0, scale=1.0)
        P.op("dve", lambda e: e.tensor_tensor_scan(c4["r1"].ap, cx.onesf.ap[0:4, :], c4["sp"].ap, 0.0, ALU.mult, ALU.subtract),
             reads=(cx.onesf, c4["sp"]), writes=(c4["r1"],))
        P.ts("dve", c4["c"], c4["r1"], ccar[:, 0:1], ALU.add)
        P.copy("dve", ccar, c4["c"][:, T - 1:T])
        if prec:
            P.dma("sp", cx.d_c0[0], c4["c"], "m0")
            P.ts("dve", c4["sp"], c4["c"], -1.0, ALU.mult)
            P.dma("sp", cx.d_c0[1], c4["sp"], "m0")
        P.copy("dve", c3b[:, 0, :], c4["c"])
        P.tt("dve", c4["r1"], c4["c"], c3b[:, 0, :], ALU.subtract)
        P.copy("dve", c3b[:, 1, :], c4["r1"])
        P.tt("dve", c4["r2"], c4["r1"], c3b[:, 1, :], ALU.subtract)
        P.copy("dve", c3b[:, 2, :], c4["r2"])
        P.ts("dve", c3n, c3b, -1.0, ALU.mult)
        P.dma("sp", cx.d_qaug[:, 64:67, t0:t0 + T], c3b, "c4q")
        P.dma("sp", cx.d_kaug[:, 67:70, t0:t0 + T], c3n, "c4k")
        ap_ = proj_fm(main, C_AA, 4)
        P.act(c4["r1"], ap_, AF.Exp, bias=p4[:, 1:2], scale=1.0)
        P.act(c4["r1"], c4["r1"], AF.Ln, bias=1.0, scale=1.0)
        P.ts("dve", c4["r2"], c4["r1"], nega[:, 0:1], ALU.mult)
        P.op("dve", lambda e: e.tensor_tensor_scan(g8.ap[0:4, :], cx.onesf.ap[0:4, :], c4["r2"].ap, 0.0, ALU.mult, ALU.add),
             reads=(cx.onesf, c4["r2"]), writes=(g8,))
        P.ts("dve", gneg[0:4, :], g8[0:4, :], -1.0, ALU.mult)
        tp = main.ps()
        P.tr(tp[:, 0:4], g8[0:4, :], cx.ident[0:4, 0:4])
        P.copy("act", blk["gt8"][:, 0:4], tp[:, 0:4])
        P.ts("dve", blk["gt8"][:, 4:8], blk["gt8"][:, 0:4], -1.0, ALU.mult)
        P.act(blk["bexp"][:, 0:4], blk["gt8"][:, 0:4], AF.Exp)
        P.tt("dve", blk["bexp"][:, 4:8], blk["bexp"][:, 0:4], blk["betat"][:, 0:4], ALU.mult)
        tp2 = main.ps()
        P.tr(tp2[:, 0:4], c4["r2"], cx.ident[0:4, 0:4])
        latok = main.tmp()
        P.copy("act", latok[:, 0:4], tp2[:, 0:4])
        gep = main.ps()
        P.mm(gep[:, 0:4], cx.onesf, latok[:, 0:4], True, True, sig=True)
        P.copy("act", blk["egend"][:, 0:4], gep[:, 0:4])
        P.act(blk["egend"][:, 4:8], gep[:, 0:4], AF.Exp)
        s4 = streams(4)
        if 'C' in cx.en:
          run_streams([stream_Cqk(s4[0], 0, "q", C_QC, pq2s[:, 0:1], cx.d_qaug, t0), stream_Cqk(s4[1], 0, "k", C_KC, pq[:, 3:4], cx.d_kaug, t0),
                     stream_Cqk(s4[2], 1, "q", C_QC, pq2s[:, 0:1], cx.d_qaug, t0), stream_Cqk(s4[3], 1, "k", C_KC, pq[:, 3:4], cx.d_kaug, t0)])
        s4 = streams(4)
        if 'B' in cx.en:
          run_streams([stream_B(s4[0], 0, t0), stream_B(s4[1], 1, t0), stream_D(s4[2], 0, t0), stream_D(s4[3], 1, t0)])
        s6 = streams(6)
        if 'A' in cx.en or 'X' in cx.en:
          run_streams([stream_Aconv(s6[gi * 2 + fc], gi, nm, col, fc) for gi, (nm, col) in enumerate((("qa", C_QA), ("ka", C_KA), ("va", C_VA))) for fc in range(2)])
        s4 = streams(4)
        if 'A' in cx.en or 'Y' in cx.en:
          run_streams([stream_Ahead(s4[h], h) for h in range(4)])
        s4 = streams(4)
        if 'A' in cx.en or 'Z' in cx.en:
          run_streams([stream_Afin(s4[fc], fc, t0) for fc in range(2)])
    P.barrier()
    A.reset(m)


def pass2(P, A, cx, l, L, S, h_in, h_out):
    m = A.mark()
    nb = S // T
    Wout = A.sb([128, NCH, D], BF16)
    stage = [A.sb([128, 1024], F32) for _ in range(2)]
    Kc = A.sb([70, 4, S], BF16)
    Vc = A.sb([128, nb, 4, 66], BF16)
    ps = cx.ps
    for c in range(NCH):
        st = stage[c % 2]
        P.dma("sp", st, V(cx.d_wout[l, c * 128:(c + 1) * 128, :], [cx.r_w]), "stg%d" % (c % 2))
        if c % 2 == 0:
            P.copy("act", Wout[:, c, :], st)
        else:
            P.copy("dve", Wout[:, c, :], st)
    P.memset("dve", Vc, 1.0)
    P.memset("dve", Kc[64:70, :, :], 1.0)
    for h in range(4):
        P.dma("sp", Kc[0:64, h, :], cx.d_kaug[h, 0:64, :], "kc")
        P.dma("sp", Kc[67:70, h, :], cx.d_kaug[h, 67:70, :], "kc")
    vsrc = cx.d_vc.re("(n p) (h d) -> p n h d", p=128, h=4)
    for h in range(4):
        P.dma("sp", Vc[:, :, h, 0:64], vsrc[:, :, h, :], "vcl")
    qa = [A.sb([70, 4, T], BF16) for _ in range(2)]
    for q_ in qa:
        P.memset("dve", q_[64:70, :, :], 1.0)
    gq = [A.sb([128, 256], BF16) for _ in range(2)]
    mx = [A.sb([128, NCH, T], BF16) for _ in range(2)]
    ht = [A.sb([128, NCH, T], F32) for _ in range(2)]
    pT = [A.sb([128, T], BF16) for _ in range(4)]
    oc = A.sb([128, 256], F32)
    rl = A.sb([128, 4], F32)
    hin_v = h_in.re("(c p) s -> p c s", p=128)
    hout_v = h_out.re("(c p) s -> p c s", p=128)
    mixv = cx.d_mixT.re("(c p) s -> p c s", p=128)
    qsrc = cx.d_qaug.re("h r s -> r h s")
    Qa0 = A.sb([66, 4, T], F32)
    Ka0 = A.sb([66, 4, T], F32)
    V0 = A.sb([128, 4, 66], F32)
    gq0 = A.sb([128, 256], F32)
    mx0 = A.sb([128, NCH, T], F32)
    pT0 = [A.sb([128, T], F32) for _ in range(2)]
    wt0 = [A.sb([128, NCH, 128], F32) for _ in range(2)]
    qa5 = [A.sb([70, 4, 4 * T], BF16) for _ in range(2)]
    for q_ in qa5:
        P.memset("dve", q_[64:70, :, :], 1.0)
    pT5 = [A.sb([128, 4 * T], BF16) for _ in range(3)]
    cx.kk = 0

    def load_tile_inputs(sl, t0):
        P.dma("sp", gq[sl], cx.d_gc[t0:t0 + T, :], "gq%d" % sl)
        P.dma("sp", mx[sl][:, 0:4, :], mixv[:, 0:4, t0:t0 + T], "mx%d" % sl)
        P.dma("sp", mx[sl][:, 6:8, :], mixv[:, 6:8, t0:t0 + T], "mx%d" % sl)
        P.dma("sp", ht[sl], hin_v[:, :, t0:t0 + T], "ht%d" % sl)

    def epilogue(o_ps, sl, t0, tpb, pbs):
        P.op("dve", lambda e: e.reciprocal(rl.ap, o_ps.ap[:, 0:260].rearrange("p (h d) -> p h d", h=4)[:, :, 64]), reads=(o_ps,), writes=(rl,))
        for h in range(4):
            P.ts("dve", oc[:, h * 64:(h + 1) * 64], o_ps[:, h * 65:h * 65 + 64], rl[:, h:h + 1], ALU.mult)
        P.tt("dve", oc, oc, gq[sl], ALU.mult)
        for fc in range(2):
            tp = tpb[fc][:, 0:T]
            P.tr(tp, oc[:, fc * 128:(fc + 1) * 128], cx.ident)
            P.copy("act", mx[sl][:, 4 + fc, :], tp)
        for dc in range(NCH):
            pb = pbs[cx.kk % len(pbs)][:, 0:T]
            cx.kk += 1
            for kc in range(NCH):
                P.mm(pb, Wout[:, kc, dc * 128:(dc + 1) * 128], mx[sl][:, kc, :], kc == 0, kc == NCH - 1, sig=(kc == NCH - 1))
            P.tt("dve", ht[sl][:, dc, :], ht[sl][:, dc, :], pb, ALU.add)
        P.dma("sp", hout_v[:, :, t0:t0 + T], ht[sl], "hs%d" % sl)

    k = 0
    for qt in range(nb):
        t0 = qt * T
        sl = qt % 2
        if qt == 0:
            P.memset("dve", Qa0[64:66, :, :], 1.0)
            P.memset("dve", Ka0[64:66, :, :], 1.0)
            P.memset("dve", V0, 1.0)
            P.dma("sp", Qa0[0:64, :, :], cx.d_q0.re("h d t -> d h t"), "p0a")
            P.dma("sp", Ka0[0:64, :, :], cx.d_k0.re("h d t -> d h t"), "p0b")
            P.dma("sp", Qa0[64:65, :, :], cx.d_c0[0:1, :, :], "p0c")
            P.dma("sp", Ka0[65:66, :, :], cx.d_c0[1:2, :, :], "p0d")
            P.dma("sp", V0[:, :, 0:64], cx.d_v0.re("t (h d) -> t h d", h=4), "p0e")
            P.dma("sp", gq0, cx.d_g0, "p0f")
            m0v = cx.d_mix0.re("(c p) t -> p c t", p=128)
            P.dma("sp", mx0[:, 0:4, :], m0v[:, 0:4, :], "p0g")
            P.dma("sp", mx0[:, 6:8, :], m0v[:, 6:8, :], "p0h")
            P.dma("sp", ht[sl], hin_v[:, :, t0:t0 + T], "ht%d" % sl)
            o_ps = ps[6]
            for h in range(4):
                sp_ = ps[k % 4][:, 0:T]
                P.mm(sp_, Ka0[:, h, :], Qa0[:, h, :], True, False)
                P.mm(sp_, cx.ident, cx.mneg_tril, False, True, sig=True)
                pt = pT0[h % 2]
                P.act(pt, sp_, AF.Exp)
                P.mm(o_ps[:, h * 65:(h + 1) * 65], pt, V0[:, h, 0:65], True, True, sig=True)
                k += 1
            P.op("dve", lambda e: e.reciprocal(rl.ap, o_ps.ap[:, 0:260].rearrange("p (h d) -> p h d", h=4)[:, :, 64]), reads=(o_ps,), writes=(rl,))
            for h in range(4):
                P.ts("dve", oc[:, h * 64:(h + 1) * 64], o_ps[:, h * 65:h * 65 + 64], rl[:, h:h + 1], ALU.mult)
            P.tt("dve", oc, oc, gq0, ALU.mult)
            for fc in range(2):
                tp = ps[4 + fc][:, 0:T]
                P.tr(tp, oc[:, fc * 128:(fc + 1) * 128], cx.ident)
                P.copy("act", mx0[:, 4 + fc, :], tp)
            for dc in range(NCH):
                wt = wt0[dc % 2]
                P.dma("sp", wt, V(cx.d_wout[l, :, dc * 128:(dc + 1) * 128].rearrange("(c p) m -> p c m", p=128), [cx.r_w]), "w0%d" % (dc % 2))
                pb = ps[k % 4][:, 0:T]
                k += 1
                for kc in range(NCH):
                    P.mm(pb, wt[:, kc, :], mx0[:, kc, :], kc == 0, kc == NCH - 1, sig=(kc == NCH - 1))
                P.tt("dve", ht[sl][:, dc, :], ht[sl][:, dc, :], pb, ALU.add)
            P.dma("sp", hout_v[:, :, t0:t0 + T], ht[sl], "hs%d" % sl)
            continue
        if qt >= 4 and nb % 4 == 0:
            if qt % 4 != 0:
                continue
            q0 = qt
            s5 = (qt // 4) % 2
            P.dma("sp", qa5[s5][0:67, :, :], qsrc[0:67, :, t0:t0 + 4 * T], "qa5%d" % s5)
            for i in range(4):
                sli = (q0 + i) % 2
                ti = t0 + i * T
                if i < 2:
                    load_tile_inputs(sli, ti)
            obank = [ps[4 + i] for i in range(4)]
            for h in range(4):
                for kb in range(q0 + 4):
                    j = kb - q0
                    c0 = 128 * j if j >= 0 else 0
                    sp_ = ps[k % 3]
                    P.mm(sp_[:, c0:512], Kc[:, h, kb * T:(kb + 1) * T], qa5[s5][:, h, c0:512], True, j < 0, sig=(j < 0))
                    if j >= 0:
                        P.mm(sp_[:, c0:c0 + T], cx.ident_bf, cx.mneg_tril_bf, False, True, sig=True)
                    pt = pT5[k % 3]
                    P.act(pt[:, c0:512], sp_[:, c0:512], AF.Exp)
                    for i in range(max(j, 0), 4):
                        last = kb == q0 + i
                        P.mm(obank[i][:, h * 65:(h + 1) * 65], pt[:, i * T:(i + 1) * T], Vc[:, kb, h, 0:65], kb == 0, last, sig=last)
                    k += 1
            for i in range(4):
                sli = (q0 + i) % 2
                ti = t0 + i * T
                if i >= 2:
                    load_tile_inputs(sli, ti)
                epilogue(obank[i], sli, ti, [ps[3], ps[3]], ps[0:3])
            continue
        P.dma("sp", qa[sl][0:67, :, :], qsrc[0:67, :, t0:t0 + T], "qa%d" % sl)
        load_tile_inputs(sl, t0)
        o_ps = ps[6]
        for h in range(4):
            for kb in range(qt + 1):
                sp_ = ps[k % 4][:, 0:T]
                diag = kb == qt
                P.mm(sp_, Kc[:, h, kb * T:(kb + 1) * T], qa[sl][:, h, :], True, not diag, sig=not diag)
                if diag:
                    P.mm(sp_, cx.ident_bf, cx.mneg_tril_bf, False, True, sig=True)
                pt = pT[k % 4]
                P.act(pt, sp_, AF.Exp)
                P.mm(o_ps[:, h * 65:(h + 1) * 65], pt, Vc[:, kb, h, 0:65], kb == 0, kb == qt, sig=(kb == qt))
                k += 1
        epilogue(o_ps, sl, t0, [ps[4], ps[5]], ps[0:4])
    P.barrier()
    A.reset(m)


def setup_consts(P, A, cx, L):
    def cst(name, shape, dt=F32):
        t = A.sb(shape, F32)
        P.dma("sp", t, cx.dc[name], "const")
        return t
    cx.ones_bf = A.sb([128, 128], BF16)
    P.memset("dve", cx.ones_bf, 1.0)
    cx.onesf = A.sb([128, 128], F32)
    P.memset("dve", cx.onesf, 1.0)
    cx.ones4b = A.sb([4, T], BF16)
    P.memset("dve", cx.ones4b, 1.0)
    cx.epsc = A.sb([128, 1], F32)
    P.memset("dve", cx.epsc, EPS)
    cx.onec = A.sb([128, 1], F32)
    P.memset("dve", cx.onec, 1.0)
    cx.c0125 = A.sb([128, 1], F32)
    P.memset("dve", cx.c0125, 0.125)
    cx.gffn = cst("gffn", [128, L * NCH])
    cx.gmix = cst("gmix", [128, L * NCH])
    cx.cw = cst("cw", [128, L * NF * 3])
    cx.ident = cst("ident", [128, 128])
    cx.mavg = cst("mavg", [128, 128])
    cx.mask_bd = cst("mask_bd", [128, 128])
    cx.mneg_strict = cst("mneg_strict", [128, 128])
    cx.mneg_tril = cst("mneg_tril", [128, 128])
    cx.sel4 = cst("sel4", [4, 512])
    cx.lgB = [cst("lgB%d" % i, [128, T]) for i in range(2)]
    lbd = cst("lbd", [128, 2 * L])
    P.barrier()
    cx.ident_bf = A.sb([128, 128], BF16)
    P.copy("dve", cx.ident_bf, cx.ident)
    cx.mneg_tril_bf = A.sb([128, 128], BF16)
    P.copy("dve", cx.mneg_tril_bf, cx.mneg_tril)
    e = A.sb([128, 2 * L], F32)
    P.act(e, lbd, AF.Exp)
    cx.lbs = A.sb([128, 2 * L], F32)
    ssum = A.sb([128, 2], F32)
    for fc in range(2):
        P.copy("dve", ssum[:, fc:fc + 1], e[:, fc * L:fc * L + 1])
        for l in range(1, L):
            P.tt("dve", ssum[:, fc:fc + 1], ssum[:, fc:fc + 1], e[:, fc * L + l:fc * L + l + 1], ALU.add)
    P.op("dve", lambda en: en.reciprocal(ssum.ap, ssum.ap), reads=(ssum,), writes=(ssum,))
    for fc in range(2):
        P.memset("dve", cx.lbs[:, fc * L:fc * L + 1], 0.0)
        for l in range(1, L):
            P.stt("dve", cx.lbs[:, fc * L + l:fc * L + l + 1], e[:, fc * L + l:fc * L + l + 1], ssum[:, fc:fc + 1],
                  cx.lbs[:, fc * L + l - 1:fc * L + l], ALU.mult, ALU.add)


CONST_SHAPES = None


def build(S, L, run_layers=None, dbg=None, en="ABCD"):
    nc = bass.Bass("TRN2", target_bir_lowering=False)
    cx = Ctx()
    cx.en = en
    P = Prog(nc)
    st = contextlib.ExitStack()
    with st:
        def din(name, shape):
            return nc.dram_tensor(name, list(shape), F32, kind="ExternalInput").ap()
        cx.d_xT = V(din("xT", [D, S]), [Res()])
        cx.dc = {}
        for name, shape in (("gffn", [128, L * NCH]), ("gmix", [128, L * NCH]), ("cw", [128, L * NF * 3]), ("ident", [128, 128]),
                            ("mavg", [128, 128]), ("mask_bd", [128, 128]), ("mneg_strict", [128, 128]), ("mneg_tril", [128, 128]),
                            ("sel4", [4, 512]), ("lgB0", [128, T]), ("lgB1", [128, T]), ("lbd", [128, 2 * L])):
            cx.dc[name] = V(din(name, shape), [Res()])
        cx.d_pp = V(din("pp", [128, L * 24]), [Res()])
        cx.d_pq = V(din("pq", [128, L * 16]), [Res()])
        cx.d_p4 = V(din("p4", [4, L * 8]), [Res()])
        cx.d_rope = {nm: V(din(nm, [128, S]), [Res()]) for nm in ("cosq", "sinq", "cosk", "sink")}
        cx.d_win = din("w_in", [L, D, WC])
        cx.d_wout = din("w_out", [L, D, D])
        cx.d_wup = din("w_up", [L, D, 2 * DFF])
        cx.d_wdn = din("w_down", [L, DFF, D])
        cx.r_w = Res()
        yT = V(nc.dram_tensor("yT", [D, S], F32, kind="ExternalOutput").ap(), [Res()])
        hA = V(nc.dram_tensor("hA", [D, S], F32, kind="Internal").ap(), [Res()])
        hB = V(nc.dram_tensor("hB", [D, S], F32, kind="Internal").ap(), [Res()])
        cx.d_mixT = V(nc.dram_tensor("mixT", [D, S], BF16, kind="Internal").ap(), [Res()])
        cx.d_qaug = V(nc.dram_tensor("qaug", [4, 70, S], BF16, kind="Internal").ap(), [Res()])
        cx.d_kaug = V(nc.dram_tensor("kaug", [4, 70, S], BF16, kind="Internal").ap(), [Res()])
        cx.d_vc = V(nc.dram_tensor("vcs", [S, 256], BF16, kind="Internal").ap(), [Res()])
        cx.d_gc = V(nc.dram_tensor("gcs", [S, 256], BF16, kind="Internal").ap(), [Res()])
        cx.d_mix0 = V(nc.dram_tensor("mix0", [D, T], F32, kind="Internal").ap(), [Res()])
        cx.d_q0 = V(nc.dram_tensor("q0s", [4, 64, T], F32, kind="Internal").ap(), [Res()])
        cx.d_k0 = V(nc.dram_tensor("k0s", [4, 64, T], F32, kind="Internal").ap(), [Res()])
        cx.d_c0 = V(nc.dram_tensor("c0s", [2, 4, T], F32, kind="Internal").ap(), [Res()])
        cx.d_v0 = V(nc.dram_tensor("v0s", [T, 256], F32, kind="Internal").ap(), [Res()])
        cx.d_g0 = V(nc.dram_tensor("g0s", [T, 256], F32, kind="Internal").ap(), [Res()])
        A = Alloc(nc)
        cx.ps = [V(st.enter_context(nc.psum_tensor("ps%d" % i, [128, 512], F32))[:, :], [Res("ps%d" % i, excl=True)]) for i in range(8)]
        setup_consts(P, A, cx, L)
        P.barrier()
        layers = list(range(L)) if run_layers is None else run_layers
        h_cur = cx.d_xT
        for i, l in enumerate(layers):
            last = i == len(layers) - 1
            if dbg != "nop1":
                pass1(P, A, cx, l, L, S, h_cur)
            if dbg == "p1":
                break
            if dbg == "p2" or dbg == "nop1":
                pass2(P, A, cx, l, L, S, h_cur, yT)
                break
            pass2(P, A, cx, l, L, S, h_cur, hB)
            pass3(P, A, cx, l, L, S, hB, yT if last else hA)
            h_cur = hA
        P.barrier()
        P.emit(st)
    return nc


def host_inputs(inp, S, L):
    f32 = np.float32
    GW = 256
    offs = np.cumsum([0, GW, GW, GW, GW, 4, 4, GW, GW, GW, GW, GW, GW, GW, GW, 4, GW, GW, GW, GW])
    (o_qa, o_ka, o_va, o_ga, o_ba, o_aa, o_qb, o_kb, o_vb, o_gb, o_qc, o_kc, o_vc, o_gc, o_fc, o_qd, o_fd, o_id, o_gd) = offs[:-1]
    sw = np.arange(256).reshape(4, 64)
    sw = np.concatenate([sw[:, 32:], sw[:, :32]], axis=1).reshape(-1)
    cols = np.concatenate([
        o_qa + np.arange(256), o_ka + np.arange(256), o_va + np.arange(256), o_ga + np.arange(256),
        o_qb + np.arange(256), o_qb + sw, o_kb + np.arange(256), o_kb + sw, o_gb + np.arange(256),
        o_qc + np.arange(256), o_kc + np.arange(256),
        o_qd + np.arange(256), o_fd + np.arange(256), o_gd + np.arange(256),
        o_vb + np.arange(256), o_vc + np.arange(256), o_id + np.arange(256), o_gc + np.arange(256),
        o_ba + np.arange(4), o_aa + np.arange(4), o_fc + np.arange(4), np.zeros(4, np.int64)])
    assert cols.shape[0] == WC
    w_in = np.ascontiguousarray(np.asarray(inp["w_in"], f32)[:, :, cols])

    def lay_g(g):
        return np.ascontiguousarray(np.asarray(g, f32).reshape(L, NCH, 128).transpose(2, 0, 1).reshape(128, L * NCH))
    cw = np.ascontiguousarray(np.asarray(inp["conv_ffn"], f32).reshape(L, 3, NF, 128).transpose(3, 0, 2, 1).reshape(128, L * NF * 3))
    pp = np.ascontiguousarray(np.asarray(inp["conv_qkv_a"], f32).reshape(L, 4, 3, 2, 128).transpose(4, 0, 2, 3, 1).reshape(128, L * 24))
    pq = np.zeros((128, L, 16), f32)
    for j, nm in ((0, "onorm_a"), (1, "onorm_b"), (2, "qnorm_c"), (3, "knorm_c"), (4, "onorm_d")):
        v = np.asarray(inp[nm], f32)
        pq[:, :, j] = np.concatenate([v, v], axis=1).T
    p4 = np.zeros((4, L, 8), f32)
    p4[:, :, 1] = np.asarray(inp["dt_bias_a"], f32).T
    p4[:, :, 2] = np.asarray(inp["fbias_c"], f32).T
    p4[:, :, 3] = np.asarray(inp["a_log_a"], f32).T
    lbd = np.ascontiguousarray(np.asarray(inp["lower_bound_d"], f32).reshape(L, 2, 128).transpose(2, 1, 0).reshape(128, 2 * L))
    ar = np.arange(128)
    ident = np.eye(128, dtype=f32)
    mavg = (ar[:, None] // 64 == ar[None, :] // 64).astype(f32) / 64.0
    mask_bd = ((ar[:, None] // 64 == ar[None, :] // 64) & (ar[:, None] <= ar[None, :])).astype(f32)
    mneg_strict = np.where(ar[None, :] < ar[:, None], 0.0, NEG).astype(f32)
    mneg_tril = np.where(ar[:, None] <= ar[None, :], 0.0, NEG).astype(f32)
    sel4 = np.zeros((4, 4, 128), f32)
    for h in range(4):
        sel4[h, h, :] = 1.0
    sel4 = np.ascontiguousarray(sel4.transpose(1, 0, 2).reshape(4, 512))
    lgh = np.log1p(-np.exp2(-5.0 - np.arange(4, dtype=f32))).astype(f32)
    lgB = [np.ascontiguousarray(np.repeat(lgh[2 * fc + ar // 64][:, None], T, axis=1).astype(f32)) for fc in range(2)]
    inv_freq = (10000.0 ** (-np.arange(0, 64, 2, dtype=f32) / 64)).astype(f32)
    ang = (np.arange(S, dtype=f32)[:, None] * inv_freq[None, :]).astype(f32)
    cos, sin = np.cos(ang).astype(f32), np.sin(ang).astype(f32)
    d = ar % 64
    cosq = np.ascontiguousarray(cos[:, d % 32].T)
    sinq = np.ascontiguousarray((sin[:, d % 32] * np.where(d < 32, -1.0, 1.0)[None, :]).T.astype(f32))
    out = {
        "xT": None, "gffn": lay_g(inp["norm_ffn"]), "gmix": lay_g(inp["norm_mix"]), "cw": cw, "ident": ident, "mavg": mavg,
        "mask_bd": mask_bd, "mneg_strict": mneg_strict, "mneg_tril": mneg_tril, "sel4": sel4, "lgB0": lgB[0], "lgB1": lgB[1], "lbd": lbd,
        "pp": pp, "pq": np.ascontiguousarray(pq.reshape(128, L * 16)), "p4": np.ascontiguousarray(p4.reshape(4, L * 8)),
        "cosq": cosq, "sinq": sinq, "cosk": (cosq * f32(0.125)).astype(f32), "sink": (sinq * f32(0.125)).astype(f32),
        "w_in": w_in, "w_out": np.asarray(inp["w_out"], f32), "w_up": np.asarray(inp["w_up"], f32), "w_down": np.asarray(inp["w_down"], f32),
    }
    return out


_NC_CACHE = {}


def kernel(**inputs):
    x = np.asarray(inputs["x"], np.float32)
    B, S, _ = x.shape
    L = np.asarray(inputs["w_in"]).shape[0]
    key = (S, L)
    if key not in _NC_CACHE:
        _NC_CACHE[key] = build(S, L)
    nc = _NC_CACHE[key]
    base = host_inputs(inputs, S, L)
    ncores = B
    in_maps = []
    for c in range(ncores):
        d = dict(base)
        d["xT"] = np.ascontiguousarray(x[c % B].T)
        in_maps.append(d)
    res = run_bass_kernel_spmd(nc, in_maps, core_ids=list(range(ncores)))
    out = np.stack([np.ascontiguousarray(res.results[b]["yT"].T) for b in range(B)], axis=0)
    return out.astype(np.float32)
```

```python
import contextlib
import numpy as np
import concourse.bass as bass
import concourse.mybir as mybir
from concourse.bass_utils import run_bass_kernel_spmd

F32 = mybir.dt.float32
BF16 = mybir.dt.bfloat16
AF = mybir.ActivationFunctionType
ALU = mybir.AluOpType
AX = mybir.AxisListType

EPOCH = 20000
ENGS = ("pe", "act", "dve", "pool", "sp")


class Res:
    __slots__ = ("name", "w", "r", "excl")

    def __init__(self, name="", excl=False):
        self.name = name
        self.w = None
        self.r = []
        self.excl = excl


class V:
    __slots__ = ("ap", "res")

    def __init__(self, ap, res):
        self.ap = ap
        self.res = tuple(res)

    def __getitem__(self, idx):
        return V(self.ap[idx], self.res)

    def re(self, pat, **kw):
        return V(self.ap.rearrange(pat, **kw), self.res)

    def with_res(self, *res):
        return V(self.ap, res)


class Tok:
    __slots__ = ("eng", "idx", "ordn", "dma", "sig")

    def __init__(self, eng, idx, ordn, dma, sig=True):
        self.eng, self.idx, self.ordn, self.dma, self.sig = eng, idx, ordn, dma, sig


class Prog:
    def __init__(self, nc):
        self.nc = nc
        self.ops = {e: [] for e in ENGS}
        self.sigc = {e: 0 for e in ENGS}
        self.seen = {e: {} for e in ENGS}
        self.dma_cnt = {}
        self.n = 0

    def op(self, eng, fn, reads=(), writes=(), sig=True, dma=None):
        lst = self.ops[eng]
        idx = len(lst)
        waits = []
        seen = self.seen[eng]

        def need(t, kind):
            if t.dma is not None:
                key, val = t.dma
                val = self.dma_cnt[key]
                if seen.get(key, 0) < val:
                    seen[key] = val
                    waits.append((key, val))
                return
            if t.eng == eng:
                if not t.sig:
                    return
                o = t.ordn
            else:
                o = t.ordn
                if o is None:
                    raise RuntimeError("dep on non-signalling op")
                if o > self.sigc[t.eng]:
                    raise RuntimeError("dep on a signal not yet emitted (%s->%s)" % (t.eng, eng))
            if seen.get(t.eng, 0) < o:
                seen[t.eng] = o
                waits.append((t.eng, o))

        for r in reads:
            for rr in r.res:
                if rr.w is not None:
                    need(rr.w, "raw")
                if rr.excl:
                    for t in rr.r:
                        if t.eng != eng:
                            need(t, "rar")
        for r in writes:
            for rr in r.res:
                if rr.w is not None:
                    need(rr.w, "waw")
                for t in rr.r:
                    need(t, "war")
        if dma is not None:
            self.dma_cnt[dma] = self.dma_cnt.get(dma, 0) + 16
            tok = Tok(eng, idx, None, (dma, self.dma_cnt[dma]))
            sig = False
        else:
            if sig:
                self.sigc[eng] += 1
                tok = Tok(eng, idx, self.sigc[eng], None)
            else:
                tok = Tok(eng, idx, self.sigc[eng] + 1, None, False)
        for r in reads:
            for rr in r.res:
                rr.r.append(tok)
        for r in writes:
            for rr in r.res:
                rr.w = tok
                rr.r = []
        lst.append((fn, waits, sig, dma))
        self.n += 1
        return tok

    def barrier(self):
        for e in ENGS:
            waits = []
            seen = self.seen[e]
            for e2 in ENGS:
                if e2 != e and self.sigc[e2] > seen.get(e2, 0):
                    seen[e2] = self.sigc[e2]
                    waits.append((e2, self.sigc[e2]))
            for k, v in self.dma_cnt.items():
                if seen.get(k, 0) < v:
                    seen[k] = v
                    waits.append((k, v))
            if waits:
                self.ops[e].append((None, waits, False, None))

    def emit(self, stack):
        nc = self.nc
        sems = {}

        def getsem(key, ep=0):
            k = (key, ep)
            if k not in sems:
                sems[k] = stack.enter_context(nc.semaphore("s_%s_%d" % (str(key), ep)))
            return sems[k]

        for e in ENGS:
            for ep in range((self.sigc[e] + EPOCH - 1) // EPOCH + 1):
                getsem(e, ep)
        for k in self.dma_cnt:
            getsem(k, 0)

        def run(engname, eng):
            cnt = 0
            for fn, waits, sig, dma in self.ops[engname]:
                for key, val in waits:
                    if key in ENGS:
                        ep, v = (val - 1) // EPOCH, (val - 1) % EPOCH + 1
                        eng.wait_ge(getsem(key, ep), v)
                    else:
                        eng.wait_ge(getsem(key, 0), val)
                if fn is None:
                    continue
                ins = fn(eng)
                if dma is not None:
                    ins.then_inc(getsem(dma, 0), 16)
                elif sig:
                    cnt += 1
                    ins.then_inc(getsem(engname, (cnt - 1) // EPOCH), 1)

        with nc.Block() as block:
            @block.tensor
            def _(eng):
                run("pe", eng)

            @block.scalar
            def _(eng):
                run("act", eng)

            @block.vector
            def _(eng):
                run("dve", eng)

            @block.gpsimd
            def _(eng):
                run("pool", eng)

            @block.sync
            def _(eng):
                run("sp", eng)

    def mm(self, out, lhsT, rhs, start=True, stop=True, sig=False):
        return self.op("pe", lambda e: e.matmul(out.ap, lhsT.ap, rhs.ap, start=start, stop=stop),
                       reads=(lhsT, rhs), writes=(out,), sig=sig)

    def tr(self, out, in_, ident, sig=True):
        return self.op("pe", lambda e: e.transpose(out.ap, in_.ap, ident.ap),
                       reads=(in_, ident), writes=(out,), sig=sig)

    def act(self, out, in_, func, bias=None, scale=1.0, accum=None, eng="act"):
        reads = [in_]
        kw = {}
        if bias is not None:
            if isinstance(bias, V):
                reads.append(bias)
                kw["bias"] = bias.ap
            else:
                kw["bias"] = bias
        if isinstance(scale, V):
            reads.append(scale)
            kw["scale"] = scale.ap
        else:
            kw["scale"] = scale
        writes = [out]
        if accum is not None:
            writes.append(accum)
            kw["accum_out"] = accum.ap
        return self.op("act", lambda e: e.activation(out.ap, in_.ap, func, **kw), reads=reads, writes=writes)

    def tt(self, eng, out, in0, in1, op):
        return self.op(eng, lambda e: e.tensor_tensor(out.ap, in0.ap, in1.ap, op), reads=(in0, in1), writes=(out,))

    def ts(self, eng, out, in0, s1, op0, s2=None, op1=None):
        reads = [in0]
        a1 = s1
        if isinstance(s1, V):
            reads.append(s1)
            a1 = s1.ap
        a2 = s2
        if isinstance(s2, V):
            reads.append(s2)
            a2 = s2.ap
        if op1 is None:
            return self.op(eng, lambda e: e.tensor_scalar(out.ap, in0.ap, a1, None, op0), reads=reads, writes=(out,))
        return self.op(eng, lambda e: e.tensor_scalar(out.ap, in0.ap, a1, a2, op0, op1), reads=reads, writes=(out,))

    def stt(self, eng, out, in0, s, in1, op0, op1):
        reads = [in0, in1]
        a = s
        if isinstance(s, V):
            reads.append(s)
            a = s.ap
        return self.op(eng, lambda e: e.scalar_tensor_tensor(out.ap, in0.ap, a, in1.ap, op0, op1), reads=reads, writes=(out,))

    def copy(self, eng, out, in_):
        if eng == "act":
            return self.op(eng, lambda e: e.copy(out.ap, in_.ap), reads=(in_,), writes=(out,))
        return self.op(eng, lambda e: e.tensor_copy(out.ap, in_.ap), reads=(in_,), writes=(out,))

    def memset(self, eng, out, val):
        return self.op(eng, lambda e: e.memset(out.ap, val), writes=(out,))

    def dma(self, eng, out, in_, key):
        return self.op(eng, lambda e: e.dma_start(out=out.ap, in_=in_.ap), reads=(in_,), writes=(out,), dma=key)


class Alloc:
    def __init__(self, nc, base=16640, limit=229000):
        self.nc, self.base, self.off, self.limit = nc, base, base, limit
        self.cnt = 0

    def mark(self):
        return self.off

    def reset(self, m):
        self.off = m

    def sb(self, shape, dtype, nres=1):
        nbytes = int(np.prod(shape[1:])) * (4 if dtype == F32 else 2)
        nbytes = (nbytes + 63) // 64 * 64
        if self.off + nbytes > self.limit:
            raise RuntimeError("SBUF overflow: %d + %d" % (self.off, nbytes))
        self.cnt += 1
        t = self.nc.alloc_sbuf_tensor_at("t%d" % self.cnt, list(shape), dtype, offset=self.off)
        self.off += nbytes
        return V(t[tuple(slice(None) for _ in shape)], [Res("t%d" % self.cnt) for _ in range(nres)])


D = 1024
NCH = 8
DFF = 2816
NF = 44
NJ = 22
EPS = 1e-6
POOLENG = "dve"
TT3 = 256


class Ctx:
    pass


def rmsnorm_tile(P, cx, ht, sq, uT, ps, rs, TT, ei, ulo=None, uf=None):
    P.act(sq, ht, AF.Square)
    for c in range(NCH):
        P.mm(ps[:, 0:TT], cx.ones_bf, sq[:, c, :], start=(c == 0), stop=(c == NCH - 1), sig=(c == NCH - 1))
    P.act(rs[:, 0:TT], ps[:, 0:TT], AF.Sqrt, bias=cx.epsc, scale=1.0 / D)
    P.op("dve", lambda e: e.reciprocal(rs.ap[:, 0:TT], rs.ap[:, 0:TT]), reads=(rs,), writes=(rs,))
    for c in range(NCH):
        if ulo is None:
            P.tt("dve" if c % 2 == 0 else POOLENG, uT[:, c, :], ht[:, c, :], rs[:, 0:TT], ALU.mult)
        else:
            f = uf[c % 2]
            P.tt("dve", f, ht[:, c, :], rs[:, 0:TT], ALU.mult)
            P.copy("act", uT[:, c, :], f)
            P.tt("dve", ulo[:, c, :], f, uT[:, c, :], ALU.subtract)


def load_weights_cast(P, cx, dst, src_ap_fn, nparts, stage, scale_fn, k0=0):
    pass


def pass3(P, A, cx, l, L, S, h_in, h_out):
    TT = TT3
    m = A.mark()
    ps = cx.ps
    ht = A.sb([128, NCH, TT], F32)
    X = [A.sb([128, TT + 2], F32) for _ in range(4)]
    Y = [A.sb([128, TT], F32) for _ in range(4)]
    SA = [A.sb([128, TT], F32) for _ in range(2)]
    rs = A.sb([128, TT], F32)
    hres = [A.sb([128, TT], F32) for _ in range(3)]
    carry = [A.sb([128, 2], F32) for _ in range(NF)]
    sqb = A.sb([128, NCH, TT], BF16)
    uT = A.sb([128, NCH, TT], BF16)
    for fc in range(NF):
        P.memset("dve", carry[fc], 0.0)
    nt = S // TT
    hin_v = h_in.re("(c p) s -> p c s", p=128)
    hout_v = h_out.re("(c p) s -> p c s", p=128)
    P.dma("sp", ht, hin_v[:, :, 0:TT], "ht")

    def ffn_tile(t, up_group, down_group, Gbuf, prep):
        t0 = t * TT
        rmsnorm_tile(P, cx, ht, sqb, uT, ps[7], rs, TT, 0)
        prep()
        k = 0
        for j in range(NJ):
            ys = []
            for half in range(2):
                fc = j + half * NJ
                pb = ps[k % 4]
                up_group(pb, fc)
                x = X[k % 4]
                y = Y[k % 4]
                P.copy("dve", x[:, 0:2], carry[fc])
                P.act(x[:, 2:TT + 2], pb[:, 0:TT], AF.Copy)
                cwb = (l * NF + fc) * 3
                P.ts("dve", y, x[:, 0:TT], cx.cw[:, cwb:cwb + 1], ALU.mult)
                P.stt("dve", y, x[:, 1:TT + 1], cx.cw[:, cwb + 1:cwb + 2], y, ALU.mult, ALU.add)
                P.stt("dve", y, x[:, 2:TT + 2], cx.cw[:, cwb + 2:cwb + 3], y, ALU.mult, ALU.add)
                P.copy("dve", carry[fc], x[:, TT:TT + 2])
                ys.append(y)
                k += 1
            sa = SA[j % 2]
            P.act(sa, ys[0], AF.Silu)
            P.tt("dve", Gbuf[:, j, :], sa, ys[1], ALU.mult)
        if t + 1 < nt:
            P.dma("sp", ht, hin_v[:, :, t0 + TT:t0 + 2 * TT], "ht")
        for dc in range(NCH):
            hr = hres[dc % 3]
            P.dma("sp", hr, hin_v[:, dc, t0:t0 + TT], "hres%d" % (dc % 3))
            pb = ps[4 + dc % 2]
            down_group(pb, dc)
            P.tt("dve", hr, hr, pb[:, 0:TT], ALU.add)
            P.dma("sp", hout_v[:, dc, t0:t0 + TT], hr, "hst%d" % (dc % 3))

    m2 = A.mark()
    ufull = A.sb([128, NCH, TT], F32)
    G32 = A.sb([128, NJ, TT], F32)
    wtu = [A.sb([128, NCH, 128], F32) for _ in range(2)]
    wtd = [A.sb([128, NJ, 128], F32) for _ in range(2)]
    cnt = [0, 0]

    def prep0():
        for c in range(NCH):
            P.stt("dve", ufull[:, c, :], ht[:, c, :], cx.gffn[:, l * NCH + c:l * NCH + c + 1], rs[:, 0:TT], ALU.mult, ALU.mult)

    def up0(pb, fc):
        wt = wtu[cnt[0] % 2]
        P.dma("sp", wt, V(cx.d_wup[l, :, fc * 128:(fc + 1) * 128].rearrange("(c p) m -> p c m", p=128), [cx.r_w]), "wu%d" % (cnt[0] % 2))
        cnt[0] += 1
        for c in range(NCH):
            P.mm(pb[:, 0:TT], wt[:, c, :], ufull[:, c, :], start=(c == 0), stop=(c == NCH - 1), sig=(c == NCH - 1))

    def down0(pb, dc):
        wt = wtd[cnt[1] % 2]
        P.dma("sp", wt, V(cx.d_wdn[l, :, dc * 128:(dc + 1) * 128].rearrange("(j p) m -> p j m", p=128), [cx.r_w]), "wd%d" % (cnt[1] % 2))
        cnt[1] += 1
        for j in range(NJ):
            P.mm(pb[:, 0:TT], wt[:, j, :], G32[:, j, :], start=(j == 0), stop=(j == NJ - 1), sig=(j == NJ - 1))

    ffn_tile(0, up0, down0, G32, prep0)
    A.reset(m2)
    if nt > 1:
        Wup = A.sb([128, NCH, 2 * DFF], BF16)
        Wdn = A.sb([128, NJ, D], BF16)
        stage = [A.sb([128, 1408], F32) for _ in range(2)]
        G = A.sb([128, NJ, TT], BF16)
        wi = 0
        for c in range(NCH):
            for hf in range(4):
                st = stage[wi % 2]
                P.dma("sp", st, V(cx.d_wup[l, c * 128:(c + 1) * 128, hf * 1408:(hf + 1) * 1408], [cx.r_w]), "stg%d" % (wi % 2))
                dst = Wup[:, c, hf * 1408:(hf + 1) * 1408]
                g = cx.gffn[:, l * NCH + c:l * NCH + c + 1]
                if wi % 2 == 0:
                    P.act(dst, st, AF.Copy, scale=g)
                else:
                    P.ts("dve", dst, st, g, ALU.mult)
                wi += 1
        for j in range(NJ):
            st = stage[wi % 2]
            P.dma("sp", st[:, 0:1024], V(cx.d_wdn[l, j * 128:(j + 1) * 128, :], [cx.r_w]), "stg%d" % (wi % 2))
            if wi % 2 == 0:
                P.copy("act", Wdn[:, j, :], st[:, 0:1024])
            else:
                P.copy("dve", Wdn[:, j, :], st[:, 0:1024])
            wi += 1

        def up1(pb, fc):
            for c in range(NCH):
                P.mm(pb[:, 0:TT], Wup[:, c, fc * 128:(fc + 1) * 128], uT[:, c, :], start=(c == 0), stop=(c == NCH - 1), sig=(c == NCH - 1))

        def down1(pb, dc):
            for j in range(NJ):
                P.mm(pb[:, 0:TT], Wdn[:, j, dc * 128:(dc + 1) * 128], G[:, j, :], start=(j == 0), stop=(j == NJ - 1), sig=(j == NJ - 1))

        for t in range(1, nt):
            ffn_tile(t, up1, down1, G, lambda: None)
    P.barrier()
    A.reset(m)


T = 128
NEG = -30000.0
WC = 4624
C_QA, C_KA, C_VA, C_GA = 0, 256, 512, 768
C_QB, C_QBS, C_KB, C_KBS, C_GB = 1024, 1280, 1536, 1792, 2048
C_QC, C_KC = 2304, 2560
C_QD, C_FD, C_GD = 2816, 3072, 3328
C_VB, C_VC, C_ID, C_GC = 3584, 3840, 4096, 4352
C_BA, C_AA, C_FC = 4608, 4612, 4616
LO_RANGES = ()


def lo_col(col):
    for a0, a1, l0 in LO_RANGES:
        if a0 <= col < a1:
            return l0 + col - a0
    return None


class Stream:
    def __init__(self, tmps, pss):
        self.tmps, self.pss, self.ti, self.pi = tmps, pss, 0, 0

    def tmp(self):
        self.ti += 1
        return self.tmps[self.ti % len(self.tmps)]

    def ps(self):
        self.pi += 1
        return self.pss[self.pi % len(self.pss)]


def run_streams(gens):
    gens = list(gens)
    while gens:
        nxt = []
        for g in gens:
            try:
                next(g)
                nxt.append(g)
            except StopIteration:
                pass
        gens = nxt


def pass1(P, A, cx, l, L, S, h_in):
    m = A.mark()
    W = A.sb([128, NCH, WC], BF16)
    ufull = A.sb([128, NCH, T], F32)
    wring = [A.sb([128, NCH, 256], F32) for _ in range(2)]
    cx.wri = 0
    NRING = 104
    ringmem = A.sb([128, NRING * T], F32, nres=NRING)
    ring = [V(ringmem.ap[:, i * T:(i + 1) * T], [ringmem.res[i]]) for i in range(NRING)]
    stage = [V(ringmem.ap[:, k * 1280:k * 1280 + 1156], ringmem.res[10 * k:10 * k + 10]) for k in range(2)]
    ps = cx.ps
    pslot = [V(ps[i // 4].ap[:, (i % 4) * T:(i % 4 + 1) * T], ps[i // 4].res) for i in range(32)]
    main = Stream(ring[80:104], [pslot[29], pslot[30], pslot[31]])
    ps256 = [V(ps[6].ap[:, 0:256], ps[6].res), V(ps[6].ap[:, 256:512], ps[6].res)]
    cx.p256 = 0

    def streams(n):
        nt_, np_ = 80 // n, 24 // n
        return [Stream(ring[i * nt_:(i + 1) * nt_], pslot[i * np_:(i + 1) * np_]) for i in range(n)]
    wi = 0
    PW = 1156
    for c in range(NCH):
        for hf in range(4):
            st = stage[wi % 2]
            c0 = hf * PW
            P.dma("sp", st, V(cx.d_win[l, c * 128:(c + 1) * 128, c0:c0 + PW], [cx.r_w]), "stg%d" % (wi % 2))
            dst = W[:, c, c0:c0 + PW]
            g = cx.gmix[:, l * NCH + c:l * NCH + c + 1]
            if wi % 2 == 0:
                P.act(dst, st, AF.Copy, scale=g)
            else:
                P.ts("dve", dst, st, g, ALU.mult)
            wi += 1
    pp = A.sb([128, 24], F32)
    P.dma("sp", pp, cx.d_pp[:, l * 24:(l + 1) * 24], "pp")
    p4 = A.sb([4, 8], F32)
    P.dma("sp", p4, cx.d_p4[:, l * 8:(l + 1) * 8], "pp")
    pq = A.sb([128, 16], F32)
    P.dma("sp", pq, cx.d_pq[:, l * 16:(l + 1) * 16], "pp")
    nega = A.sb([4, 1], F32)
    P.act(nega, p4[:, 3:4], AF.Exp)
    P.ts("dve", nega, nega, -1.0, ALU.mult)
    oml = A.sb([128, 4], F32)
    lbv = A.sb([128, 2], F32)
    for fc_ in range(2):
        P.copy("dve", lbv[:, fc_:fc_ + 1], cx.lbs[:, fc_ * L + l:fc_ * L + l + 1])
    P.ts("dve", oml[:, 0:2], lbv, -1.0, ALU.mult, 1.0, ALU.add)
    P.ts("dve", oml[:, 2:4], oml[:, 0:2], -1.0, ALU.mult)
    SA = [A.sb([128, 64], F32) for _ in range(2)]
    SB = [A.sb([128, 64], F32) for _ in range(2)]
    SD = [A.sb([128, 64], F32) for _ in range(2)]
    for s_ in SA + SB + SD:
        P.memset("dve", s_, 0.0)
    ccar = A.sb([4, 1], F32)
    P.memset("dve", ccar, 0.0)
    cvx = [A.sb([128, T + 3], F32) for _ in range(6)]
    for b_ in cvx:
        P.memset("dve", b_, 0.0)
    ht = A.sb([128, NCH, T], F32)
    sqb = A.sb([128, NCH, T], BF16)
    uT = A.sb([128, NCH, T], BF16)
    rs = A.sb([128, T], F32)
    blk = {k: A.sb([128, T], F32) for k in ["qa0", "qa1", "ka0", "ka1", "va0", "va1", "vtokA0", "vtokA1", "ktokA0", "ktokA1",
                                             "gt8", "egend", "betat", "bexp", "oA0", "oA1"]}
    vtokB = A.sb([128, 256], F32)
    vtokD = A.sb([128, 256], F32)
    g8 = A.sb([8, T], F32)
    gneg = A.sb([8, T], F32)
    ropeb = [A.sb([128, T], F32) for _ in range(4)]
    pq2s = A.sb([128, 1], F32)
    P.ts("dve", pq2s, pq[:, 2:3], 0.125, ALU.mult)
    nfb = A.sb([4, 1], F32)
    P.ts("dve", nfb, p4[:, 2:3], -1.0, ALU.mult)
    c4 = {k: A.sb([4, T], F32) for k in ["c", "r1", "r2", "sp"]}
    c3b = A.sb([4, 3, T], BF16)
    c3n = A.sb([4, 3, T], BF16)
    obf = [A.sb([128, T], BF16) for _ in range(8)]
    vcb = A.sb([128, 256], BF16)
    v32 = A.sb([128, 256], F32)
    g32 = A.sb([128, 256], F32)
    gcb = A.sb([128, 256], BF16)
    cx.oi = 0
    cx.prec = False
    hin_v = h_in.re("(c p) s -> p c s", p=128)
    nb = S // T

    def wload(col, M):
        cx.wri += 1
        wt = wring[cx.wri % 2]
        src = cx.d_win[l, :, col:col + M].rearrange("(c p) m -> p c m", p=128)
        P.dma("sp", wt[:, :, 0:M], V(src, [cx.r_w]), "wr%d" % (cx.wri % 2))
        return wt

    def proj_fm(st, col, M=128):
        pb = st.ps()
        if cx.prec:
            wt = wload(col, M)
            for c in range(NCH):
                P.mm(pb[0:M, :], wt[:, c, 0:M], ufull[:, c, :], start=(c == 0), stop=(c == NCH - 1), sig=(c == NCH - 1))
        else:
            for c in range(NCH):
                P.mm(pb[0:M, :], W[:, c, col:col + M], uT[:, c, :], start=(c == 0), stop=(c == NCH - 1), sig=(c == NCH - 1))
        return pb[0:M, :]

    def proj_tm(col, N):
        if N > 128:
            cx.p256 += 1
            pb = ps256[cx.p256 % 2]
        else:
            pb = main.ps()
        if cx.prec:
            wt = wload(col, N)
            for c in range(NCH):
                P.mm(pb[:, 0:N], ufull[:, c, :], wt[:, c, 0:N], start=(c == 0), stop=(c == NCH - 1), sig=(c == NCH - 1))
        else:
            for c in range(NCH):
                P.mm(pb[:, 0:N], uT[:, c, :], W[:, c, col:col + N], start=(c == 0), stop=(c == NCH - 1), sig=(c == NCH - 1))
        return pb[:, 0:N]

    def headnorm(st, src, kind, gain, sum_scale=1.0):
        if kind == "ln":
            mp = st.ps()
            P.mm(mp, cx.mavg, src, True, True, sig=True)
            yield
            mean = st.tmp()
            P.copy("act", mean, mp)
            yield
            cen = st.tmp()
            P.tt("dve", cen, src, mean, ALU.subtract)
            src = cen
            yield
        sq = st.tmp()
        P.tt("dve", sq, src, src, ALU.mult)
        yield
        e2 = st.ps()
        P.mm(e2, cx.mavg, sq, True, True, sig=True)
        yield
        r = st.tmp()
        P.act(r, e2, AF.Sqrt, bias=cx.epsc, scale=sum_scale)
        yield
        P.op("dve", lambda e: e.reciprocal(r.ap, r.ap), reads=(r,), writes=(r,))
        yield
        y = st.tmp()
        P.stt("dve", y, src, gain, r, ALU.mult, ALU.mult)
        yield
        return y

    def emit_mix(st, y, gate_col, fc, chunk, t0, func=AF.Silu):
        gate_ps = proj_fm(st, gate_col + fc * 128)
        yield
        sg = st.tmp()
        P.act(sg, gate_ps, func)
        yield
        ob = obf[cx.oi % 8]
        cx.oi += 1
        P.tt("dve", ob, y, sg, ALU.mult)
        P.dma("sp", cx.d_mixT[chunk * 128:(chunk + 1) * 128, t0:t0 + T], ob, "mx%d" % (cx.oi % 8))
        if cx.prec:
            o32 = st.tmp()
            P.tt("dve", o32, y, sg, ALU.mult)
            P.dma("sp", cx.d_mix0[chunk * 128:(chunk + 1) * 128, :], o32, "m0")
        yield

    def gla(st, qT, kT, lfT, vtok, St, fc, kind, gain, gate_col, chunk, t0):
        bc = st.tmp()
        for ci in range(2):
            P.op("dve", lambda e, ci=ci: e.tensor_tensor_scan(bc.ap[:, ci * 64:(ci + 1) * 64], cx.onesf.ap[:, 0:64], lfT.ap[:, ci * 64:(ci + 1) * 64], 0.0, ALU.mult, ALU.add),
                 reads=(cx.onesf, lfT), writes=(bc,))
        yield
        eb = st.tmp()
        P.act(eb, bc, AF.Exp)
        enb = st.tmp()
        P.act(enb, bc, AF.Exp, scale=-1.0)
        ke = st.tmp()
        for ci in range(2):
            P.act(ke[:, ci * 64:(ci + 1) * 64], bc[:, ci * 64:(ci + 1) * 64], AF.Exp, bias=bc[:, ci * 64 + 63:ci * 64 + 64], scale=-1.0)
        yield
        qt = st.tmp()
        P.tt("dve", qt, qT, eb, ALU.mult)
        kt = st.tmp()
        P.tt("dve", kt, kT, enb, ALU.mult)
        kend = st.tmp()
        P.tt("dve", kend, kT, ke, ALU.mult)
        yield
        kp = st.ps()
        P.tr(kp, kend, cx.ident)
        yield
        kendTok = st.tmp()
        P.copy("act", kendTok, kp)
        yield
        o = st.tmp()
        for hh in range(2):
            b = 64 * hh
            h = 2 * fc + hh
            sc = st.ps()
            P.mm(sc, kt[b:b + 64, :], qt[b:b + 64, :], True, True, sig=True)
            u0 = st.ps()
            P.mm(u0[b:b + 64, 0:64], kendTok[0:64, b:b + 64], vtok[0:64, h * 64:(h + 1) * 64], True, True, sig=True)
            yield
            PT = st.tmp()
            P.tt("dve", PT, sc, cx.mask_bd, ALU.mult)
            yield
            oa = st.ps()
            P.mm(oa[b:b + 64, :], vtok[:, h * 64:(h + 1) * 64], PT, True, False)
            P.mm(oa[b:b + 64, 0:64], St[b:b + 64, :], qt[b:b + 64, 0:64], False, True, sig=True)
            P.stt("dve", St[b:b + 64, :], St[b:b + 64, :], eb[b:b + 64, 63:64], u0[b:b + 64, 0:64], ALU.mult, ALU.add)
            yield
            ob_ = st.ps()
            P.mm(ob_[b:b + 64, 0:64], St[b:b + 64, :], qt[b:b + 64, 64:128], True, True, sig=True)
            u1 = st.ps()
            P.mm(u1[b:b + 64, 0:64], kendTok[64:128, b:b + 64], vtok[64:128, h * 64:(h + 1) * 64], True, True, sig=True)
            P.copy("act", o[b:b + 64, :], oa[b:b + 64, :])
            yield
            P.tt("dve", o[b:b + 64, 64:128], o[b:b + 64, 64:128], ob_[b:b + 64, 0:64], ALU.add)
            P.stt("dve", St[b:b + 64, :], St[b:b + 64, :], eb[b:b + 64, 127:128], u1[b:b + 64, 0:64], ALU.mult, ALU.add)
            yield
        y = yield from headnorm(st, o, kind, gain)
        yield from emit_mix(st, y, gate_col, fc, chunk, t0)

    def stream_B(st, fc, t0):
        out = {}
        for nm, c0, c1, ct in (("q", C_QB, C_QBS, 0), ("k", C_KB, C_KBS, 2)):
            a_ = proj_fm(st, c0 + fc * 128)
            b_ = proj_fm(st, c1 + fc * 128)
            yield
            t1 = st.tmp()
            P.tt("dve", t1, a_, ropeb[ct], ALU.mult)
            t2 = st.tmp()
            P.tt("dve", t2, b_, ropeb[ct + 1], ALU.mult)
            yield
            r_ = st.tmp()
            P.tt("dve", r_, t1, t2, ALU.add)
            out[nm] = r_
            yield
        yield from gla(st, out["q"], out["k"], cx.lgB[fc], vtokB, SB[fc], fc, "ln", pq[:, 1:2], C_GB, 2 + fc, t0)

    def stream_D(st, fc, t0):
        qp = proj_fm(st, C_QD + fc * 128)
        fp_ = proj_fm(st, C_FD + fc * 128)
        yield
        qd = st.tmp()
        P.act(qd, qp, AF.Silu)
        sg = st.tmp()
        P.act(sg, fp_, AF.Sigmoid)
        yield
        fg = st.tmp()
        P.ts("dve", fg, sg, oml[:, fc:fc + 1], ALU.mult, lbv[:, fc:fc + 1], ALU.add)
        kd = st.tmp()
        P.ts("dve", kd, sg, oml[:, 2 + fc:3 + fc], ALU.mult, oml[:, fc:fc + 1], ALU.add)
        yield
        lf = st.tmp()
        P.act(lf, fg, AF.Ln)
        yield
        yield from gla(st, qd, kd, lf, vtokD, SD[fc], fc, "rms", pq[:, 4:5], C_GD, 6 + fc, t0)

    def stream_Cqk(st, fc, which, col, gain_, dst, t0):
        pp_ = proj_fm(st, col + fc * 128)
        yield
        o = st.tmp()
        P.copy("act", o, pp_)
        yield
        y = yield from headnorm(st, o, "rms", gain_)
        ob = obf[cx.oi % 8]
        cx.oi += 1
        P.copy("dve", ob, y)
        for hh in range(2):
            P.dma("sp", dst[2 * fc + hh, 0:64, t0:t0 + T], ob[64 * hh:64 * hh + 64, :], "cq%d" % (cx.oi % 8))
            if cx.prec:
                d0 = cx.d_q0 if which == "q" else cx.d_k0
                P.dma("sp", d0[2 * fc + hh, :, :], y[64 * hh:64 * hh + 64, :], "m0")
        yield

    def stream_Aconv(st, gi, nm, col, fc):
        xb = cvx[gi * 2 + fc]
        pp_ = proj_fm(st, col + fc * 128)
        P.copy("dve", xb[:, 0:3], xb[:, T:T + 3])
        yield
        P.copy("act", xb[:, 3:T + 3], pp_)
        yield
        y = st.tmp()
        cb = gi * 8 + fc * 4
        P.ts("dve", y, xb[:, 0:T], pp[:, cb:cb + 1], ALU.mult)
        yield
        for k_ in range(1, 4):
            P.stt("dve", y, xb[:, k_:k_ + T], pp[:, cb + k_:cb + k_ + 1], y, ALU.mult, ALU.add)
            yield
        if nm == "va":
            P.act(blk["va%d" % fc], y, AF.Silu)
            yield
            src = blk["va%d" % fc]
        else:
            z = st.tmp()
            P.act(z, y, AF.Silu)
            yield
            yn = yield from headnorm(st, z, "rms", cx.c0125 if nm == "qa" else cx.onec, sum_scale=64.0)
            P.copy("act", blk[nm + str(fc)], yn)
            yield
            src = blk[nm + str(fc)]
        if nm != "qa":
            tp = st.ps()
            P.tr(tp, src, cx.ident)
            yield
            P.copy("act", blk[("ktokA" if nm == "ka" else "vtokA") + str(fc)], tp)
            yield

    def stream_Ahead(st, h):
        fc, hh = h // 2, h % 2
        b = 64 * hh
        kT = blk["ka%d" % fc]
        qT = blk["qa%d" % fc]
        ktok = blk["ktokA%d" % fc][:, b:b + 64]
        vtok = blk["vtokA%d" % fc][:, b:b + 64]
        St = SA[fc]
        kk = st.ps()
        P.mm(kk, kT[b:b + 64, :], kT[b:b + 64, :], True, True, sig=True)
        gb = st.ps()
        P.mm(gb, cx.sel4[:, h * 128:(h + 1) * 128], gneg[0:4, :], True, False)
        P.mm(gb, cx.ident, cx.mneg_strict, False, True, sig=True)
        x = st.tmp()
        P.ts("dve", x[:, 0:64], vtok, blk["betat"][:, h:h + 1], ALU.mult)
        P.ts("dve", x[:, 64:128], ktok, blk["bexp"][:, 4 + h:5 + h], ALU.mult)
        yield
        gam = st.tmp()
        P.act(gam, gb, AF.Exp, bias=blk["gt8"][:, h:h + 1], scale=1.0)
        yield
        Am = st.tmp()
        P.stt("dve", Am, kk, blk["betat"][:, h:h + 1], gam, ALU.mult, ALU.mult)
        yield
        atp = st.ps()
        P.tr(atp, Am, cx.ident)
        yield
        AT = st.tmp()
        P.copy("act", AT, atp)
        yield
        xp = st.ps()
        P.mm(xp, AT, x, True, True, sig=True)
        p2 = st.ps()
        P.mm(p2, Am, AT, True, True, sig=True)
        p1 = st.ps()
        P.mm(p1, AT, Am, True, True, sig=True)
        yield
        Pm, PTm = Am, AT
        for it in range(6):
            x2 = st.tmp()
            P.tt("dve", x2, x, xp, ALU.subtract if it == 0 else ALU.add)
            x = x2
            nPT = st.tmp()
            P.copy("act", nPT, p2)
            if it < 5:
                nP = st.tmp()
                P.copy("dve", nP, p1)
            else:
                nP = None
            yield
            xp = st.ps()
            P.mm(xp, nPT, x, True, True, sig=True)
            if it < 5:
                p2 = st.ps()
                P.mm(p2, nP, nPT, True, True, sig=True)
                if it < 4:
                    p1 = st.ps()
                    P.mm(p1, nPT, nP, True, True, sig=True)
            yield
            Pm, PTm = nP, nPT
        x2 = st.tmp()
        P.tt("dve", x2, x, xp, ALU.add)
        x = x2
        yield
        wtp = st.ps()
        if hh == 0:
            P.tr(wtp[0:64, :], x[:, 64:128], cx.ident)
        else:
            P.tr(wtp, x, cx.ident)
        kq = st.ps()
        P.mm(kq, kT[b:b + 64, :], qT[b:b + 64, :], True, True, sig=True)
        gb2a = st.ps()
        P.mm(gb2a, cx.sel4[:, h * 128:(h + 1) * 128], g8[0:4, :], True, True, sig=True)
        gb2 = st.ps()
        P.mm(gb2, cx.sel4[:, h * 128:(h + 1) * 128], g8[0:4, :], True, False)
        P.mm(gb2, cx.ident, cx.mneg_tril, False, True, sig=True)
        yield
        wT = st.tmp()
        P.copy("act", wT[b:b + 64, :], wtp[b:b + 64, :])
        egbc = st.tmp()
        P.act(egbc, gb2a, AF.Exp)
        gam2 = st.tmp()
        P.act(gam2, gb2, AF.Exp, bias=blk["gt8"][:, 4 + h:5 + h], scale=1.0)
        ed = st.tmp()
        P.act(ed[:, 0:1], blk["gt8"][:, h:h + 1], AF.Exp, bias=blk["egend"][:, h:h + 1], scale=-1.0)
        yield
        ws = st.ps()
        P.mm(ws[:, 0:64], wT[b:b + 64, :], St[b:b + 64, :], True, True, sig=True)
        PT = st.tmp()
        P.tt("dve", PT, kq, gam2, ALU.mult)
        qg = st.tmp()
        P.tt("dve", qg[b:b + 64, :], qT[b:b + 64, :], egbc[b:b + 64, :], ALU.mult)
        kd = st.tmp()
        P.ts("dve", kd[:, 0:64], ktok, ed[:, 0:1], ALU.mult)
        yield
        vnew = st.tmp()
        P.tt("dve", vnew[:, 0:64], x[:, 0:64], ws[:, 0:64], ALU.subtract)
        yield
        op_ = st.ps()
        P.mm(op_[b:b + 64, :], St[b:b + 64, :], qg[b:b + 64, :], True, False)
        P.mm(op_[b:b + 64, :], vnew[:, 0:64], PT, False, True, sig=True)
        up = st.ps()
        P.mm(up[b:b + 64, 0:64], kd[:, 0:64], vnew[:, 0:64], True, True, sig=True)
        yield
        P.copy("act", blk["oA%d" % fc][b:b + 64, :], op_[b:b + 64, :])
        P.stt("dve", St[b:b + 64, :], St[b:b + 64, :], blk["egend"][b:b + 64, 4 + h:5 + h], up[b:b + 64, 0:64], ALU.mult, ALU.add)
        yield

    def stream_Afin(st, fc, t0):
        y = yield from headnorm(st, blk["oA%d" % fc], "rms", pq[:, 0:1])
        yield from emit_mix(st, y, C_GA, fc, fc, t0)

    P.dma("sp", ht, hin_v[:, :, 0:T], "ht")
    for bi in range(nb):
        t0 = bi * T
        prec = bi == 0
        cx.prec = prec
        rmsnorm_tile(P, cx, ht, sqb, uT, ps[7], rs, T, 0)
        if prec:
            for c in range(NCH):
                P.stt("dve", ufull[:, c, :], ht[:, c, :], cx.gmix[:, l * NCH + c:l * NCH + c + 1], rs[:, 0:T], ALU.mult, ALU.mult)
        if bi + 1 < nb:
            P.dma("sp", ht, hin_v[:, :, t0 + T:t0 + 2 * T], "ht")
        for i_, nm in enumerate(["cosq", "sinq", "cosk", "sink"]):
            P.dma("sp", ropeb[i_], cx.d_rope[nm][:, t0:t0 + T], "rope%d" % i_)
        def stream_main():
            vp = proj_tm(C_VC, 256)
            P.copy("act", vcb, vp)
            yield
            P.dma("sp", cx.d_vc[t0:t0 + T, :], vcb, "vc")
            yield
            if prec:
                P.copy("act", v32, vp)
                yield
                P.dma("sp", cx.d_v0, v32, "m0")
                yield
            gp = proj_tm(C_GC, 256)
            P.act(gcb, gp, AF.Sigmoid)
            yield
            P.dma("sp", cx.d_gc[t0:t0 + T, :], gcb, "gc")
            yield
            if prec:
                P.act(g32, gp, AF.Sigmoid)
                yield
                P.dma("sp", cx.d_g0, g32, "m0")
                yield
            vp = proj_tm(C_VB, 256)
            P.copy("act", vtokB, vp)
            yield
            vp = proj_tm(C_ID, 256)
            P.copy("act", vtokD, vp)
            yield
            bp = proj_tm(C_BA, 4)
            P.act(blk["betat"][:, 0:4], bp[:, 0:4], AF.Sigmoid)
            yield
            fp = proj_fm(main, C_FC, 4)
            P.act(c4["sp"], fp, AF.Exp, bias=nfb[:, 0:1], scale=-1.0)
            yield
            P.act(c4["sp"], c4["sp"], AF.Ln, bias=1.0, scale=1.0)
            yield
            P.op("dve", lambda e: e.tensor_tensor_scan(c4["r1"].ap, cx.onesf.ap[0:4, :], c4["sp"].ap, 0.0, ALU.mult, ALU.subtract),
                 reads=(cx.onesf, c4["sp"]), writes=(c4["r1"],))
            P.ts("dve", c4["c"], c4["r1"], ccar[:, 0:1], ALU.add)
            yield
            P.copy("dve", ccar, c4["c"][:, T - 1:T])
            if prec:
                P.dma("sp", cx.d_c0[0], c4["c"], "m0")
                yield
                P.ts("dve", c4["sp"], c4["c"], -1.0, ALU.mult)
                yield
                P.dma("sp", cx.d_c0[1], c4["sp"], "m0")
                yield
            P.copy("dve", c3b[:, 0, :], c4["c"])
            P.tt("dve", c4["r1"], c4["c"], c3b[:, 0, :], ALU.subtract)
            yield
            P.copy("dve", c3b[:, 1, :], c4["r1"])
            P.tt("dve", c4["r2"], c4["r1"], c3b[:, 1, :], ALU.subtract)
            yield
            P.copy("dve", c3b[:, 2, :], c4["r2"])
            P.ts("dve", c3n, c3b, -1.0, ALU.mult)
            yield
            P.dma("sp", cx.d_qaug[:, 64:67, t0:t0 + T], c3b, "c4q")
            yield
            P.dma("sp", cx.d_kaug[:, 67:70, t0:t0 + T], c3n, "c4k")
            yield
            ap_ = proj_fm(main, C_AA, 4)
            P.act(c4["r1"], ap_, AF.Exp, bias=p4[:, 1:2], scale=1.0)
            yield
            P.act(c4["r1"], c4["r1"], AF.Ln, bias=1.0, scale=1.0)
            yield
            P.ts("dve", c4["r2"], c4["r1"], nega[:, 0:1], ALU.mult)
            yield
            P.op("dve", lambda e: e.tensor_tensor_scan(g8.ap[0:4, :], cx.onesf.ap[0:4, :], c4["r2"].ap, 0.0, ALU.mult, ALU.add),
                 reads=(cx.onesf, c4["r2"]), writes=(g8,))
            P.ts("dve", gneg[0:4, :], g8[0:4, :], -1.0, ALU.mult)
            yield
            tp = main.ps()
            P.tr(tp[:, 0:4], g8[0:4, :], cx.ident[0:4, 0:4])
            P.copy("act", blk["gt8"][:, 0:4], tp[:, 0:4])
            yield
            P.ts("dve", blk["gt8"][:, 4:8], blk["gt8"][:, 0:4], -1.0, ALU.mult)
            yield
            P.act(blk["bexp"][:, 0:4], blk["gt8"][:, 0:4], AF.Exp)
            yield
            P.tt("dve", blk["bexp"][:, 4:8], blk["bexp"][:, 0:4], blk["betat"][:, 0:4], ALU.mult)
            yield
            tp2 = main.ps()
            P.tr(tp2[:, 0:4], c4["r2"], cx.ident[0:4, 0:4])
            latok = main.tmp()
            P.copy("act", latok[:, 0:4], tp2[:, 0:4])
            yield
            gep = main.ps()
            P.mm(gep[:, 0:4], cx.onesf, latok[:, 0:4], True, True, sig=True)
            P.copy("act", blk["egend"][:, 0:4], gep[:, 0:4])
            yield
            P.act(blk["egend"][:, 4:8], gep[:, 0:4], AF.Exp)
            yield
        s4 = streams(4)
        if 'C' in cx.en:
          run_streams([stream_Cqk(s4[0], 0, "q", C_QC, pq2s[:, 0:1], cx.d_qaug, t0), stream_Cqk(s4[1], 0, "k", C_KC, pq[:, 3:4], cx.d_kaug, t0),
                     stream_Cqk(s4[2], 1, "q", C_QC, pq2s[:, 0:1], cx.d_qaug, t0), stream_Cqk(s4[3], 1, "k", C_KC, pq[:, 3:4], cx.d_kaug, t0), stream_main()])
        s4 = streams(4)
        if 'B' in cx.en:
          run_streams([stream_B(s4[0], 0, t0), stream_B(s4[1], 1, t0), stream_D(s4[2], 0, t0), stream_D(s4[3], 1, t0)])
        s6 = streams(6)
        if 'A' in cx.en or 'X' in cx.en:
          run_streams([stream_Aconv(s6[gi * 2 + fc], gi, nm, col, fc) for gi, (nm, col) in enumerate((("qa", C_QA), ("ka", C_KA), ("va", C_VA))) for fc in range(2)])
        s4 = streams(4)
        if 'A' in cx.en or 'Y' in cx.en:
          run_streams([stream_Ahead(s4[h], h) for h in range(4)])
        s4 = streams(4)
        if 'A' in cx.en or 'Z' in cx.en:
          run_streams([stream_Afin(s4[fc], fc, t0) for fc in range(2)])
    P.barrier()
    A.reset(m)


def pass2(P, A, cx, l, L, S, h_in, h_out):
    m = A.mark()
    nb = S // T
    Wout = A.sb([128, NCH, D], BF16)
    stage = [A.sb([128, 1024], F32) for _ in range(2)]
    Kc = A.sb([70, 4, S], BF16)
    Vc = A.sb([128, nb, 4, 66], BF16)
    ps = cx.ps
    for c in range(NCH):
        st = stage[c % 2]
        P.dma("sp", st, V(cx.d_wout[l, c * 128:(c + 1) * 128, :], [cx.r_w]), "stg%d" % (c % 2))
        if c % 2 == 0:
            P.copy("act", Wout[:, c, :], st)
        else:
            P.copy("dve", Wout[:, c, :], st)
    P.memset("dve", Vc, 1.0)
    P.memset("dve", Kc[64:70, :, :], 1.0)
    for h in range(4):
        P.dma("sp", Kc[0:64, h, :], cx.d_kaug[h, 0:64, :], "kc")
        P.dma("sp", Kc[67:70, h, :], cx.d_kaug[h, 67:70, :], "kc")
    vsrc = cx.d_vc.re("(n p) (h d) -> p n h d", p=128, h=4)
    for h in range(4):
        P.dma("sp", Vc[:, :, h, 0:64], vsrc[:, :, h, :], "vcl")
    qa = [A.sb([70, 4, T], BF16) for _ in range(2)]
    for q_ in qa:
        P.memset("dve", q_[64:70, :, :], 1.0)
    gq = [A.sb([128, 256], BF16) for _ in range(2)]
    mx = [A.sb([128, NCH, T], BF16) for _ in range(2)]
    ht = [A.sb([128, NCH, T], F32) for _ in range(2)]
    pT = [A.sb([128, T], BF16) for _ in range(4)]
    oc = A.sb([128, 256], F32)
    rl = A.sb([128, 4], F32)
    hin_v = h_in.re("(c p) s -> p c s", p=128)
    hout_v = h_out.re("(c p) s -> p c s", p=128)
    mixv = cx.d_mixT.re("(c p) s -> p c s", p=128)
    qsrc = cx.d_qaug.re("h r s -> r h s")
    Qa0 = A.sb([66, 4, T], F32)
    Ka0 = A.sb([66, 4, T], F32)
    V0 = A.sb([128, 4, 66], F32)
    gq0 = A.sb([128, 256], F32)
    mx0 = A.sb([128, NCH, T], F32)
    pT0 = [A.sb([128, T], F32) for _ in range(2)]
    wt0 = [A.sb([128, NCH, 128], F32) for _ in range(2)]
    qa5 = [A.sb([70, 4, 4 * T], BF16) for _ in range(2)]
    for q_ in qa5:
        P.memset("dve", q_[64:70, :, :], 1.0)
    pT5 = [A.sb([128, 4 * T], BF16) for _ in range(3)]
    cx.kk = 0

    def load_tile_inputs(sl, t0):
        P.dma("sp", gq[sl], cx.d_gc[t0:t0 + T, :], "gq%d" % sl)
        P.dma("sp", mx[sl][:, 0:4, :], mixv[:, 0:4, t0:t0 + T], "mx%d" % sl)
        P.dma("sp", mx[sl][:, 6:8, :], mixv[:, 6:8, t0:t0 + T], "mx%d" % sl)
        P.dma("sp", ht[sl], hin_v[:, :, t0:t0 + T], "ht%d" % sl)

    def epilogue(o_ps, sl, t0, tpb, pbs):
        P.op("dve", lambda e: e.reciprocal(rl.ap, o_ps.ap[:, 0:260].rearrange("p (h d) -> p h d", h=4)[:, :, 64]), reads=(o_ps,), writes=(rl,))
        for h in range(4):
            P.ts("dve", oc[:, h * 64:(h + 1) * 64], o_ps[:, h * 65:h * 65 + 64], rl[:, h:h + 1], ALU.mult)
        P.tt("dve", oc, oc, gq[sl], ALU.mult)
        for fc in range(2):
            tp = tpb[fc][:, 0:T]
            P.tr(tp, oc[:, fc * 128:(fc + 1) * 128], cx.ident)
            P.copy("act", mx[sl][:, 4 + fc, :], tp)
        for dc in range(NCH):
            pb = pbs[cx.kk % len(pbs)][:, 0:T]
            cx.kk += 1
            for kc in range(NCH):
                P.mm(pb, Wout[:, kc, dc * 128:(dc + 1) * 128], mx[sl][:, kc, :], kc == 0, kc == NCH - 1, sig=(kc == NCH - 1))
            P.tt("dve", ht[sl][:, dc, :], ht[sl][:, dc, :], pb, ALU.add)
        P.dma("sp", hout_v[:, :, t0:t0 + T], ht[sl], "hs%d" % sl)

    k = 0
    for qt in range(nb):
        t0 = qt * T
        sl = qt % 2
        if qt == 0:
            P.memset("dve", Qa0[64:66, :, :], 1.0)
            P.memset("dve", Ka0[64:66, :, :], 1.0)
            P.memset("dve", V0, 1.0)
            P.dma("sp", Qa0[0:64, :, :], cx.d_q0.re("h d t -> d h t"), "p0a")
            P.dma("sp", Ka0[0:64, :, :], cx.d_k0.re("h d t -> d h t"), "p0b")
            P.dma("sp", Qa0[64:65, :, :], cx.d_c0[0:1, :, :], "p0c")
            P.dma("sp", Ka0[65:66, :, :], cx.d_c0[1:2, :, :], "p0d")
            P.dma("sp", V0[:, :, 0:64], cx.d_v0.re("t (h d) -> t h d", h=4), "p0e")
            P.dma("sp", gq0, cx.d_g0, "p0f")
            m0v = cx.d_mix0.re("(c p) t -> p c t", p=128)
            P.dma("sp", mx0[:, 0:4, :], m0v[:, 0:4, :], "p0g")
            P.dma("sp", mx0[:, 6:8, :], m0v[:, 6:8, :], "p0h")
            P.dma("sp", ht[sl], hin_v[:, :, t0:t0 + T], "ht%d" % sl)
            o_ps = ps[6]
            for h in range(4):
                sp_ = ps[k % 4][:, 0:T]
                P.mm(sp_, Ka0[:, h, :], Qa0[:, h, :], True, False)
                P.mm(sp_, cx.ident, cx.mneg_tril, False, True, sig=True)
                pt = pT0[h % 2]
                P.act(pt, sp_, AF.Exp)
                P.mm(o_ps[:, h * 65:(h + 1) * 65], pt, V0[:, h, 0:65], True, True, sig=True)
                k += 1
            P.op("dve", lambda e: e.reciprocal(rl.ap, o_ps.ap[:, 0:260].rearrange("p (h d) -> p h d", h=4)[:, :, 64]), reads=(o_ps,), writes=(rl,))
            for h in range(4):
                P.ts("dve", oc[:, h * 64:(h + 1) * 64], o_ps[:, h * 65:h * 65 + 64], rl[:, h:h + 1], ALU.mult)
            P.tt("dve", oc, oc, gq0, ALU.mult)
            for fc in range(2):
                tp = ps[4 + fc][:, 0:T]
                P.tr(tp, oc[:, fc * 128:(fc + 1) * 128], cx.ident)
                P.copy("act", mx0[:, 4 + fc, :], tp)
            for dc in range(NCH):
                wt = wt0[dc % 2]
                P.dma("sp", wt, V(cx.d_wout[l, :, dc * 128:(dc + 1) * 128].rearrange("(c p) m -> p c m", p=128), [cx.r_w]), "w0%d" % (dc % 2))
                pb = ps[k % 4][:, 0:T]
                k += 1
                for kc in range(NCH):
                    P.mm(pb, wt[:, kc, :], mx0[:, kc, :], kc == 0, kc == NCH - 1, sig=(kc == NCH - 1))
                P.tt("dve", ht[sl][:, dc, :], ht[sl][:, dc, :], pb, ALU.add)
            P.dma("sp", hout_v[:, :, t0:t0 + T], ht[sl], "hs%d" % sl)
            continue
        if qt >= 4 and nb % 4 == 0:
            if qt % 4 != 0:
                continue
            q0 = qt
            s5 = (qt // 4) % 2
            P.dma("sp", qa5[s5][0:67, :, :], qsrc[0:67, :, t0:t0 + 4 * T], "qa5%d" % s5)
            for i in range(4):
                sli = (q0 + i) % 2
                ti = t0 + i * T
                if i < 2:
                    load_tile_inputs(sli, ti)
            obank = [ps[4 + i] for i in range(4)]
            for h in range(4):
                for kb in range(q0 + 4):
                    j = kb - q0
                    c0 = 128 * j if j >= 0 else 0
                    sp_ = ps[k % 3]
                    P.mm(sp_[:, c0:512], Kc[:, h, kb * T:(kb + 1) * T], qa5[s5][:, h, c0:512], True, j < 0, sig=(j < 0))
                    if j >= 0:
                        P.mm(sp_[:, c0:c0 + T], cx.ident_bf, cx.mneg_tril_bf, False, True, sig=True)
                    pt = pT5[k % 3]
                    P.act(pt[:, c0:512], sp_[:, c0:512], AF.Exp)
                    for i in range(max(j, 0), 4):
                        last = kb == q0 + i
                        P.mm(obank[i][:, h * 65:(h + 1) * 65], pt[:, i * T:(i + 1) * T], Vc[:, kb, h, 0:65], kb == 0, last, sig=last)
                    k += 1
            for i in range(4):
                sli = (q0 + i) % 2
                ti = t0 + i * T
                if i >= 2:
                    load_tile_inputs(sli, ti)
                epilogue(obank[i], sli, ti, [ps[3], ps[3]], ps[0:3])
            continue
        P.dma("sp", qa[sl][0:67, :, :], qsrc[0:67, :, t0:t0 + T], "qa%d" % sl)
        load_tile_inputs(sl, t0)
        o_ps = ps[6]
        for h in range(4):
            for kb in range(qt + 1):
                sp_ = ps[k % 4][:, 0:T]
                diag = kb == qt
                P.mm(sp_, Kc[:, h, kb * T:(kb + 1) * T], qa[sl][:, h, :], True, not diag, sig=not diag)
                if diag:
                    P.mm(sp_, cx.ident_bf, cx.mneg_tril_bf, False, True, sig=True)
                pt = pT[k % 4]
                P.act(pt, sp_, AF.Exp)
                P.mm(o_ps[:, h * 65:(h + 1) * 65], pt, Vc[:, kb, h, 0:65], kb == 0, kb == qt, sig=(kb == qt))
                k += 1
        epilogue(o_ps, sl, t0, [ps[4], ps[5]], ps[0:4])
    P.barrier()
    A.reset(m)


def setup_consts(P, A, cx, L):
    def cst(name, shape, dt=F32):
        t = A.sb(shape, F32)
        P.dma("sp", t, cx.dc[name], "const")
        return t
    cx.ones_bf = A.sb([128, 128], BF16)
    P.memset("dve", cx.ones_bf, 1.0)
    cx.onesf = A.sb([128, 128], F32)
    P.memset("dve", cx.onesf, 1.0)
    cx.ones4b = A.sb([4, T], BF16)
    P.memset("dve", cx.ones4b, 1.0)
    cx.epsc = A.sb([128, 1], F32)
    P.memset("dve", cx.epsc, EPS)
    cx.onec = A.sb([128, 1], F32)
    P.memset("dve", cx.onec, 1.0)
    cx.c0125 = A.sb([128, 1], F32)
    P.memset("dve", cx.c0125, 0.125)
    cx.gffn = cst("gffn", [128, L * NCH])
    cx.gmix = cst("gmix", [128, L * NCH])
    cx.cw = cst("cw", [128, L * NF * 3])
    cx.ident = cst("ident", [128, 128])
    cx.mavg = cst("mavg", [128, 128])
    cx.mask_bd = cst("mask_bd", [128, 128])
    cx.mneg_strict = cst("mneg_strict", [128, 128])
    cx.mneg_tril = cst("mneg_tril", [128, 128])
    cx.sel4 = cst("sel4", [4, 512])
    cx.lgB = [cst("lgB%d" % i, [128, T]) for i in range(2)]
    lbd = cst("lbd", [128, 2 * L])
    P.barrier()
    cx.ident_bf = A.sb([128, 128], BF16)
    P.copy("dve", cx.ident_bf, cx.ident)
    cx.mneg_tril_bf = A.sb([128, 128], BF16)
    P.copy("dve", cx.mneg_tril_bf, cx.mneg_tril)
    e = A.sb([128, 2 * L], F32)
    P.act(e, lbd, AF.Exp)
    cx.lbs = A.sb([128, 2 * L], F32)
    ssum = A.sb([128, 2], F32)
    for fc in range(2):
        P.copy("dve", ssum[:, fc:fc + 1], e[:, fc * L:fc * L + 1])
        for l in range(1, L):
            P.tt("dve", ssum[:, fc:fc + 1], ssum[:, fc:fc + 1], e[:, fc * L + l:fc * L + l + 1], ALU.add)
    P.op("dve", lambda en: en.reciprocal(ssum.ap, ssum.ap), reads=(ssum,), writes=(ssum,))
    for fc in range(2):
        P.memset("dve", cx.lbs[:, fc * L:fc * L + 1], 0.0)
        for l in range(1, L):
            P.stt("dve", cx.lbs[:, fc * L + l:fc * L + l + 1], e[:, fc * L + l:fc * L + l + 1], ssum[:, fc:fc + 1],
                  cx.lbs[:, fc * L + l - 1:fc * L + l], ALU.mult, ALU.add)


CONST_SHAPES = None


def build(S, L, run_layers=None, dbg=None, en="ABCD"):
    nc = bass.Bass("TRN2", target_bir_lowering=False)
    cx = Ctx()
    cx.en = en
    P = Prog(nc)
    st = contextlib.ExitStack()
    with st:
        def din(name, shape):
            return nc.dram_tensor(name, list(shape), F32, kind="ExternalInput").ap()
        cx.d_xT = V(din("xT", [D, S]), [Res()])
        cx.dc = {}
        for name, shape in (("gffn", [128, L * NCH]), ("gmix", [128, L * NCH]), ("cw", [128, L * NF * 3]), ("ident", [128, 128]),
                            ("mavg", [128, 128]), ("mask_bd", [128, 128]), ("mneg_strict", [128, 128]), ("mneg_tril", [128, 128]),
                            ("sel4", [4, 512]), ("lgB0", [128, T]), ("lgB1", [128, T]), ("lbd", [128, 2 * L])):
            cx.dc[name] = V(din(name, shape), [Res()])
        cx.d_pp = V(din("pp", [128, L * 24]), [Res()])
        cx.d_pq = V(din("pq", [128, L * 16]), [Res()])
        cx.d_p4 = V(din("p4", [4, L * 8]), [Res()])
        cx.d_rope = {nm: V(din(nm, [128, S]), [Res()]) for nm in ("cosq", "sinq", "cosk", "sink")}
        cx.d_win = din("w_in", [L, D, WC])
        cx.d_wout = din("w_out", [L, D, D])
        cx.d_wup = din("w_up", [L, D, 2 * DFF])
        cx.d_wdn = din("w_down", [L, DFF, D])
        cx.r_w = Res()
        yT = V(nc.dram_tensor("yT", [D, S], F32, kind="ExternalOutput").ap(), [Res()])
        hA = V(nc.dram_tensor("hA", [D, S], F32, kind="Internal").ap(), [Res()])
        hB = V(nc.dram_tensor("hB", [D, S], F32, kind="Internal").ap(), [Res()])
        cx.d_mixT = V(nc.dram_tensor("mixT", [D, S], BF16, kind="Internal").ap(), [Res()])
        cx.d_qaug = V(nc.dram_tensor("qaug", [4, 70, S], BF16, kind="Internal").ap(), [Res()])
        cx.d_kaug = V(nc.dram_tensor("kaug", [4, 70, S], BF16, kind="Internal").ap(), [Res()])
        cx.d_vc = V(nc.dram_tensor("vcs", [S, 256], BF16, kind="Internal").ap(), [Res()])
        cx.d_gc = V(nc.dram_tensor("gcs", [S, 256], BF16, kind="Internal").ap(), [Res()])
        cx.d_mix0 = V(nc.dram_tensor("mix0", [D, T], F32, kind="Internal").ap(), [Res()])
        cx.d_q0 = V(nc.dram_tensor("q0s", [4, 64, T], F32, kind="Internal").ap(), [Res()])
        cx.d_k0 = V(nc.dram_tensor("k0s", [4, 64, T], F32, kind="Internal").ap(), [Res()])
        cx.d_c0 = V(nc.dram_tensor("c0s", [2, 4, T], F32, kind="Internal").ap(), [Res()])
        cx.d_v0 = V(nc.dram_tensor("v0s", [T, 256], F32, kind="Internal").ap(), [Res()])
        cx.d_g0 = V(nc.dram_tensor("g0s", [T, 256], F32, kind="Internal").ap(), [Res()])
        A = Alloc(nc)
        cx.ps = [V(st.enter_context(nc.psum_tensor("ps%d" % i, [128, 512], F32))[:, :], [Res("ps%d" % i, excl=True)]) for i in range(8)]
        setup_consts(P, A, cx, L)
        P.barrier()
        layers = list(range(L)) if run_layers is None else run_layers
        h_cur = cx.d_xT
        for i, l in enumerate(layers):
            last = i == len(layers) - 1
            if dbg != "nop1":
                pass1(P, A, cx, l, L, S, h_cur)
            if dbg == "p1":
                break
            if dbg == "p2" or dbg == "nop1":
                pass2(P, A, cx, l, L, S, h_cur, yT)
                break
            pass2(P, A, cx, l, L, S, h_cur, hB)
            pass3(P, A, cx, l, L, S, hB, yT if last else hA)
            h_cur = hA
        P.barrier()
        P.emit(st)
    return nc


def host_inputs(inp, S, L):
    f32 = np.float32
    GW = 256
    offs = np.cumsum([0, GW, GW, GW, GW, 4, 4, GW, GW, GW, GW, GW, GW, GW, GW, 4, GW, GW, GW, GW])
    (o_qa, o_ka, o_va, o_ga, o_ba, o_aa, o_qb, o_kb, o_vb, o_gb, o_qc, o_kc, o_vc, o_gc, o_fc, o_qd, o_fd, o_id, o_gd) = offs[:-1]
    sw = np.arange(256).reshape(4, 64)
    sw = np.concatenate([sw[:, 32:], sw[:, :32]], axis=1).reshape(-1)
    cols = np.concatenate([
        o_qa + np.arange(256), o_ka + np.arange(256), o_va + np.arange(256), o_ga + np.arange(256),
        o_qb + np.arange(256), o_qb + sw, o_kb + np.arange(256), o_kb + sw, o_gb + np.arange(256),
        o_qc + np.arange(256), o_kc + np.arange(256),
        o_qd + np.arange(256), o_fd + np.arange(256), o_gd + np.arange(256),
        o_vb + np.arange(256), o_vc + np.arange(256), o_id + np.arange(256), o_gc + np.arange(256),
        o_ba + np.arange(4), o_aa + np.arange(4), o_fc + np.arange(4), np.zeros(4, np.int64)])
    assert cols.shape[0] == WC
    w_in = np.ascontiguousarray(np.asarray(inp["w_in"], f32)[:, :, cols])

    def lay_g(g):
        return np.ascontiguousarray(np.asarray(g, f32).reshape(L, NCH, 128).transpose(2, 0, 1).reshape(128, L * NCH))
    cw = np.ascontiguousarray(np.asarray(inp["conv_ffn"], f32).reshape(L, 3, NF, 128).transpose(3, 0, 2, 1).reshape(128, L * NF * 3))
    pp = np.ascontiguousarray(np.asarray(inp["conv_qkv_a"], f32).reshape(L, 4, 3, 2, 128).transpose(4, 0, 2, 3, 1).reshape(128, L * 24))
    pq = np.zeros((128, L, 16), f32)
    for j, nm in ((0, "onorm_a"), (1, "onorm_b"), (2, "qnorm_c"), (3, "knorm_c"), (4, "onorm_d")):
        v = np.asarray(inp[nm], f32)
        pq[:, :, j] = np.concatenate([v, v], axis=1).T
    p4 = np.zeros((4, L, 8), f32)
    p4[:, :, 1] = np.asarray(inp["dt_bias_a"], f32).T
    p4[:, :, 2] = np.asarray(inp["fbias_c"], f32).T
    p4[:, :, 3] = np.asarray(inp["a_log_a"], f32).T
    lbd = np.ascontiguousarray(np.asarray(inp["lower_bound_d"], f32).reshape(L, 2, 128).transpose(2, 1, 0).reshape(128, 2 * L))
    ar = np.arange(128)
    ident = np.eye(128, dtype=f32)
    mavg = (ar[:, None] // 64 == ar[None, :] // 64).astype(f32) / 64.0
    mask_bd = ((ar[:, None] // 64 == ar[None, :] // 64) & (ar[:, None] <= ar[None, :])).astype(f32)
    mneg_strict = np.where(ar[None, :] < ar[:, None], 0.0, NEG).astype(f32)
    mneg_tril = np.where(ar[:, None] <= ar[None, :], 0.0, NEG).astype(f32)
    sel4 = np.zeros((4, 4, 128), f32)
    for h in range(4):
        sel4[h, h, :] = 1.0
    sel4 = np.ascontiguousarray(sel4.transpose(1, 0, 2).reshape(4, 512))
    lgh = np.log1p(-np.exp2(-5.0 - np.arange(4, dtype=f32))).astype(f32)
    lgB = [np.ascontiguousarray(np.repeat(lgh[2 * fc + ar // 64][:, None], T, axis=1).astype(f32)) for fc in range(2)]
    inv_freq = (10000.0 ** (-np.arange(0, 64, 2, dtype=f32) / 64)).astype(f32)
    ang = (np.arange(S, dtype=f32)[:, None] * inv_freq[None, :]).astype(f32)
    cos, sin = np.cos(ang).astype(f32), np.sin(ang).astype(f32)
    d = ar % 64
    cosq = np.ascontiguousarray(cos[:, d % 32].T)
    sinq = np.ascontiguousarray((sin[:, d % 32] * np.where(d < 32, -1.0, 1.0)[None, :]).T.astype(f32))
    out = {
        "xT": None, "gffn": lay_g(inp["norm_ffn"]), "gmix": lay_g(inp["norm_mix"]), "cw": cw, "ident": ident, "mavg": mavg,
        "mask_bd": mask_bd, "mneg_strict": mneg_strict, "mneg_tril": mneg_tril, "sel4": sel4, "lgB0": lgB[0], "lgB1": lgB[1], "lbd": lbd,
        "pp": pp, "pq": np.ascontiguousarray(pq.reshape(128, L * 16)), "p4": np.ascontiguousarray(p4.reshape(4, L * 8)),
        "cosq": cosq, "sinq": sinq, "cosk": (cosq * f32(0.125)).astype(f32), "sink": (sinq * f32(0.125)).astype(f32),
        "w_in": w_in, "w_out": np.asarray(inp["w_out"], f32), "w_up": np.asarray(inp["w_up"], f32), "w_down": np.asarray(inp["w_down"], f32),
    }
    return out


_NC_CACHE = {}


def kernel(**inputs):
    x = np.asarray(inputs["x"], np.float32)
    B, S, _ = x.shape
    L = np.asarray(inputs["w_in"]).shape[0]
    key = (S, L)
    if key not in _NC_CACHE:
        _NC_CACHE[key] = build(S, L)
    nc = _NC_CACHE[key]
    base = host_inputs(inputs, S, L)
    ncores = B
    in_maps = []
    for c in range(ncores):
        d = dict(base)
        d["xT"] = np.ascontiguousarray(x[c % B].T)
        in_maps.append(d)
    res = run_bass_kernel_spmd(nc, in_maps, core_ids=list(range(ncores)))
    out = np.stack([np.ascontiguousarray(res.results[b]["yT"].T) for b in range(B)], axis=0)
    return out.astype(np.float32)
```

```python
import contextlib
import numpy as np
import concourse.bass as bass
import concourse.mybir as mybir
from concourse.bass_utils import run_bass_kernel_spmd

F32 = mybir.dt.float32
BF16 = mybir.dt.bfloat16
AF = mybir.ActivationFunctionType
ALU = mybir.AluOpType
AX = mybir.AxisListType

EPOCH = 20000
ENGS = ("pe", "act", "dve", "pool", "sp")


class Res:
    __slots__ = ("name", "w", "r", "excl")

    def __init__(self, name="", excl=False):
        self.name = name
        self.w = None
        self.r = []
        self.excl = excl


class V:
    __slots__ = ("ap", "res")

    def __init__(self, ap, res):
        self.ap = ap
        self.res = tuple(res)

    def __getitem__(self, idx):
        return V(self.ap[idx], self.res)

    def re(self, pat, **kw):
        return V(self.ap.rearrange(pat, **kw), self.res)

    def with_res(self, *res):
        return V(self.ap, res)


class Tok:
    __slots__ = ("eng", "idx", "ordn", "dma", "sig")

    def __init__(self, eng, idx, ordn, dma, sig=True):
        self.eng, self.idx, self.ordn, self.dma, self.sig = eng, idx, ordn, dma, sig


class Prog:
    def __init__(self, nc):
        self.nc = nc
        self.ops = {e: [] for e in ENGS}
        self.sigc = {e: 0 for e in ENGS}
        self.seen = {e: {} for e in ENGS}
        self.dma_cnt = {}
        self.n = 0

    def op(self, eng, fn, reads=(), writes=(), sig=True, dma=None):
        lst = self.ops[eng]
        idx = len(lst)
        waits = []
        seen = self.seen[eng]

        def need(t, kind):
            if t.dma is not None:
                key, val = t.dma
                val = self.dma_cnt[key]
                if seen.get(key, 0) < val:
                    seen[key] = val
                    waits.append((key, val))
                return
            if t.eng == eng:
                if not t.sig:
                    return
                o = t.ordn
            else:
                o = t.ordn
                if o is None:
                    raise RuntimeError("dep on non-signalling op")
                if o > self.sigc[t.eng]:
                    raise RuntimeError("dep on a signal not yet emitted (%s->%s)" % (t.eng, eng))
            if seen.get(t.eng, 0) < o:
                seen[t.eng] = o
                waits.append((t.eng, o))

        for r in reads:
            for rr in r.res:
                if rr.w is not None:
                    need(rr.w, "raw")
                if rr.excl:
                    for t in rr.r:
                        if t.eng != eng:
                            need(t, "rar")
        for r in writes:
            for rr in r.res:
                if rr.w is not None:
                    need(rr.w, "waw")
                for t in rr.r:
                    need(t, "war")
        if dma is not None:
            self.dma_cnt[dma] = self.dma_cnt.get(dma, 0) + 16
            tok = Tok(eng, idx, None, (dma, self.dma_cnt[dma]))
            sig = False
        else:
            if sig:
                self.sigc[eng] += 1
                tok = Tok(eng, idx, self.sigc[eng], None)
            else:
                tok = Tok(eng, idx, self.sigc[eng] + 1, None, False)
        for r in reads:
            for rr in r.res:
                rr.r.append(tok)
        for r in writes:
            for rr in r.res:
                rr.w = tok
                rr.r = []
        lst.append((fn, waits, sig, dma))
        self.n += 1
        return tok

    def barrier(self):
        for e in ENGS:
            waits = []
            seen = self.seen[e]
            for e2 in ENGS:
                if e2 != e and self.sigc[e2] > seen.get(e2, 0):
                    seen[e2] = self.sigc[e2]
                    waits.append((e2, self.sigc[e2]))
            for k, v in self.dma_cnt.items():
                if seen.get(k, 0) < v:
                    seen[k] = v
                    waits.append((k, v))
            if waits:
                self.ops[e].append((None, waits, False, None))

    def emit(self, stack):
        nc = self.nc
        sems = {}

        def getsem(key, ep=0):
            k = (key, ep)
            if k not in sems:
                sems[k] = stack.enter_context(nc.semaphore("s_%s_%d" % (str(key), ep)))
            return sems[k]

        for e in ENGS:
            for ep in range((self.sigc[e] + EPOCH - 1) // EPOCH + 1):
                getsem(e, ep)
        for k in self.dma_cnt:
            getsem(k, 0)

        def run(engname, eng):
            cnt = 0
            for fn, waits, sig, dma in self.ops[engname]:
                for key, val in waits:
                    if key in ENGS:
                        ep, v = (val - 1) // EPOCH, (val - 1) % EPOCH + 1
                        eng.wait_ge(getsem(key, ep), v)
                    else:
                        eng.wait_ge(getsem(key, 0), val)
                if fn is None:
                    continue
                ins = fn(eng)
                if dma is not None:
                    ins.then_inc(getsem(dma, 0), 16)
                elif sig:
                    cnt += 1
                    ins.then_inc(getsem(engname, (cnt - 1) // EPOCH), 1)

        with nc.Block() as block:
            @block.tensor
            def _(eng):
                run("pe", eng)

            @block.scalar
            def _(eng):
                run("act", eng)

            @block.vector
            def _(eng):
                run("dve", eng)

            @block.gpsimd
            def _(eng):
                run("pool", eng)

            @block.sync
            def _(eng):
                run("sp", eng)

    def mm(self, out, lhsT, rhs, start=True, stop=True, sig=False):
        return self.op("pe", lambda e: e.matmul(out.ap, lhsT.ap, rhs.ap, start=start, stop=stop),
                       reads=(lhsT, rhs), writes=(out,), sig=sig)

    def tr(self, out, in_, ident, sig=True):
        return self.op("pe", lambda e: e.transpose(out.ap, in_.ap, ident.ap),
                       reads=(in_, ident), writes=(out,), sig=sig)

    def act(self, out, in_, func, bias=None, scale=1.0, accum=None, eng="act"):
        reads = [in_]
        kw = {}
        if bias is not None:
            if isinstance(bias, V):
                reads.append(bias)
                kw["bias"] = bias.ap
            else:
                kw["bias"] = bias
        if isinstance(scale, V):
            reads.append(scale)
            kw["scale"] = scale.ap
        else:
            kw["scale"] = scale
        writes = [out]
        if accum is not None:
            writes.append(accum)
            kw["accum_out"] = accum.ap
        return self.op("act", lambda e: e.activation(out.ap, in_.ap, func, **kw), reads=reads, writes=writes)

    def tt(self, eng, out, in0, in1, op):
        return self.op(eng, lambda e: e.tensor_tensor(out.ap, in0.ap, in1.ap, op), reads=(in0, in1), writes=(out,))

    def ts(self, eng, out, in0, s1, op0, s2=None, op1=None):
        reads = [in0]
        a1 = s1
        if isinstance(s1, V):
            reads.append(s1)
            a1 = s1.ap
        a2 = s2
        if isinstance(s2, V):
            reads.append(s2)
            a2 = s2.ap
        if op1 is None:
            return self.op(eng, lambda e: e.tensor_scalar(out.ap, in0.ap, a1, None, op0), reads=reads, writes=(out,))
        return self.op(eng, lambda e: e.tensor_scalar(out.ap, in0.ap, a1, a2, op0, op1), reads=reads, writes=(out,))

    def stt(self, eng, out, in0, s, in1, op0, op1):
        reads = [in0, in1]
        a = s
        if isinstance(s, V):
            reads.append(s)
            a = s.ap
        return self.op(eng, lambda e: e.scalar_tensor_tensor(out.ap, in0.ap, a, in1.ap, op0, op1), reads=reads, writes=(out,))

    def copy(self, eng, out, in_):
        if eng == "act":
            return self.op(eng, lambda e: e.copy(out.ap, in_.ap), reads=(in_,), writes=(out,))
        return self.op(eng, lambda e: e.tensor_copy(out.ap, in_.ap), reads=(in_,), writes=(out,))

    def memset(self, eng, out, val):
        return self.op(eng, lambda e: e.memset(out.ap, val), writes=(out,))

    def dma(self, eng, out, in_, key):
        return self.op(eng, lambda e: e.dma_start(out=out.ap, in_=in_.ap), reads=(in_,), writes=(out,), dma=key)


class Alloc:
    def __init__(self, nc, base=16640, limit=229000):
        self.nc, self.base, self.off, self.limit = nc, base, base, limit
        self.cnt = 0

    def mark(self):
        return self.off

    def reset(self, m):
        self.off = m

    def sb(self, shape, dtype, nres=1):
        nbytes = int(np.prod(shape[1:])) * (4 if dtype == F32 else 2)
        nbytes = (nbytes + 63) // 64 * 64
        if self.off + nbytes > self.limit:
            raise RuntimeError("SBUF overflow: %d + %d" % (self.off, nbytes))
        self.cnt += 1
        t = self.nc.alloc_sbuf_tensor_at("t%d" % self.cnt, list(shape), dtype, offset=self.off)
        self.off += nbytes
        return V(t[tuple(slice(None) for _ in shape)], [Res("t%d" % self.cnt) for _ in range(nres)])


D = 1024
NCH = 8
DFF = 2816
NF = 44
NJ = 22
EPS = 1e-6
POOLENG = "dve"
TT3 = 256


class Ctx:
    pass


def rmsnorm_tile(P, cx, ht, sq, uT, ps, rs, TT, ei, ulo=None, uf=None):
    P.act(sq, ht, AF.Square)
    for c in range(NCH):
        P.mm(ps[:, 0:TT], cx.ones_bf, sq[:, c, :], start=(c == 0), stop=(c == NCH - 1), sig=(c == NCH - 1))
    P.act(rs[:, 0:TT], ps[:, 0:TT], AF.Sqrt, bias=cx.epsc, scale=1.0 / D)
    P.op("dve", lambda e: e.reciprocal(rs.ap[:, 0:TT], rs.ap[:, 0:TT]), reads=(rs,), writes=(rs,))
    for c in range(NCH):
        if ulo is None:
            P.tt("dve" if c % 2 == 0 else POOLENG, uT[:, c, :], ht[:, c, :], rs[:, 0:TT], ALU.mult)
        else:
            f = uf[c % 2]
            P.tt("dve", f, ht[:, c, :], rs[:, 0:TT], ALU.mult)
            P.copy("act", uT[:, c, :], f)
            P.tt("dve", ulo[:, c, :], f, uT[:, c, :], ALU.subtract)


def load_weights_cast(P, cx, dst, src_ap_fn, nparts, stage, scale_fn, k0=0):
    pass


def pass3(P, A, cx, l, L, S, h_in, h_out):
    TT = TT3
    m = A.mark()
    ps = cx.ps
    ht = A.sb([128, NCH, TT], F32)
    X = [A.sb([128, TT + 2], F32) for _ in range(4)]
    Y = [A.sb([128, TT], F32) for _ in range(4)]
    SA = [A.sb([128, TT], F32) for _ in range(2)]
    rs = A.sb([128, TT], F32)
    hres = [A.sb([128, TT], F32) for _ in range(3)]
    carry = [A.sb([128, 2], F32) for _ in range(NF)]
    sqb = A.sb([128, NCH, TT], BF16)
    uT = A.sb([128, NCH, TT], BF16)
    for fc in range(NF):
        P.memset("dve", carry[fc], 0.0)
    nt = S // TT
    hin_v = h_in.re("(c p) s -> p c s", p=128)
    hout_v = h_out.re("(c p) s -> p c s", p=128)
    P.dma("sp", ht, hin_v[:, :, 0:TT], "ht")

    def ffn_tile(t, up_group, down_group, Gbuf, prep):
        t0 = t * TT
        rmsnorm_tile(P, cx, ht, sqb, uT, ps[7], rs, TT, 0)
        prep()
        k = 0
        for j in range(NJ):
            ys = []
            for half in range(2):
                fc = j + half * NJ
                pb = ps[k % 4]
                up_group(pb, fc)
                x = X[k % 4]
                y = Y[k % 4]
                P.copy("dve", x[:, 0:2], carry[fc])
                P.act(x[:, 2:TT + 2], pb[:, 0:TT], AF.Copy)
                cwb = (l * NF + fc) * 3
                P.ts("dve", y, x[:, 0:TT], cx.cw[:, cwb:cwb + 1], ALU.mult)
                P.stt("dve", y, x[:, 1:TT + 1], cx.cw[:, cwb + 1:cwb + 2], y, ALU.mult, ALU.add)
                P.stt("dve", y, x[:, 2:TT + 2], cx.cw[:, cwb + 2:cwb + 3], y, ALU.mult, ALU.add)
                P.copy("dve", carry[fc], x[:, TT:TT + 2])
                ys.append(y)
                k += 1
            sa = SA[j % 2]
            P.act(sa, ys[0], AF.Silu)
            P.tt("dve", Gbuf[:, j, :], sa, ys[1], ALU.mult)
        if t + 1 < nt:
            P.dma("sp", ht, hin_v[:, :, t0 + TT:t0 + 2 * TT], "ht")
        for dc in range(NCH):
            hr = hres[dc % 3]
            P.dma("sp", hr, hin_v[:, dc, t0:t0 + TT], "hres%d" % (dc % 3))
            pb = ps[4 + dc % 2]
            down_group(pb, dc)
            P.tt("dve", hr, hr, pb[:, 0:TT], ALU.add)
            P.dma("pool", hout_v[:, dc, t0:t0 + TT], hr, "hst%d" % (dc % 3))

    m2 = A.mark()
    ufull = A.sb([128, NCH, TT], F32)
    G32 = A.sb([128, NJ, TT], F32)
    wtu = [A.sb([128, NCH, 128], F32) for _ in range(2)]
    wtd = [A.sb([128, NJ, 128], F32) for _ in range(2)]
    cnt = [0, 0]

    def prep0():
        for c in range(NCH):
            P.stt("dve", ufull[:, c, :], ht[:, c, :], cx.gffn[:, l * NCH + c:l * NCH + c + 1], rs[:, 0:TT], ALU.mult, ALU.mult)

    def up0(pb, fc):
        wt = wtu[cnt[0] % 2]
        P.dma("sp", wt, V(cx.d_wup[l, :, fc * 128:(fc + 1) * 128].rearrange("(c p) m -> p c m", p=128), [cx.r_w]), "wu%d" % (cnt[0] % 2))
        cnt[0] += 1
        for c in range(NCH):
            P.mm(pb[:, 0:TT], wt[:, c, :], ufull[:, c, :], start=(c == 0), stop=(c == NCH - 1), sig=(c == NCH - 1))

    def down0(pb, dc):
        wt = wtd[cnt[1] % 2]
        P.dma("sp", wt, V(cx.d_wdn[l, :, dc * 128:(dc + 1) * 128].rearrange("(j p) m -> p j m", p=128), [cx.r_w]), "wd%d" % (cnt[1] % 2))
        cnt[1] += 1
        for j in range(NJ):
            P.mm(pb[:, 0:TT], wt[:, j, :], G32[:, j, :], start=(j == 0), stop=(j == NJ - 1), sig=(j == NJ - 1))

    ffn_tile(0, up0, down0, G32, prep0)
    A.reset(m2)
    if nt > 1:
        Wup = A.sb([128, NCH, 2 * DFF], BF16)
        Wdn = A.sb([128, NJ, D], BF16)
        stage = [A.sb([128, 1408], F32) for _ in range(2)]
        G = A.sb([128, NJ, TT], BF16)
        wi = 0
        for c in range(NCH):
            for hf in range(4):
                st = stage[wi % 2]
                P.dma("sp", st, V(cx.d_wup[l, c * 128:(c + 1) * 128, hf * 1408:(hf + 1) * 1408], [cx.r_w]), "stg%d" % (wi % 2))
                dst = Wup[:, c, hf * 1408:(hf + 1) * 1408]
                g = cx.gffn[:, l * NCH + c:l * NCH + c + 1]
                if wi % 2 == 0:
                    P.act(dst, st, AF.Copy, scale=g)
                else:
                    P.ts("dve", dst, st, g, ALU.mult)
                wi += 1
        for j in range(NJ):
            st = stage[wi % 2]
            P.dma("sp", st[:, 0:1024], V(cx.d_wdn[l, j * 128:(j + 1) * 128, :], [cx.r_w]), "stg%d" % (wi % 2))
            if wi % 2 == 0:
                P.copy("act", Wdn[:, j, :], st[:, 0:1024])
            else:
                P.copy("dve", Wdn[:, j, :], st[:, 0:1024])
            wi += 1

        def up1(pb, fc):
            for c in range(NCH):
                P.mm(pb[:, 0:TT], Wup[:, c, fc * 128:(fc + 1) * 128], uT[:, c, :], start=(c == 0), stop=(c == NCH - 1), sig=(c == NCH - 1))

        def down1(pb, dc):
            for j in range(NJ):
                P.mm(pb[:, 0:TT], Wdn[:, j, dc * 128:(dc + 1) * 128], G[:, j, :], start=(j == 0), stop=(j == NJ - 1), sig=(j == NJ - 1))

        for t in range(1, nt):
            ffn_tile(t, up1, down1, G, lambda: None)
    P.barrier()
    A.reset(m)


T = 128
NEG = -30000.0
WC = 4624
C_QA, C_KA, C_VA, C_GA = 0, 256, 512, 768
C_QB, C_QBS, C_KB, C_KBS, C_GB = 1024, 1280, 1536, 1792, 2048
C_QC, C_KC = 2304, 2560
C_QD, C_FD, C_GD = 2816, 3072, 3328
C_VB, C_VC, C_ID, C_GC = 3584, 3840, 4096, 4352
C_BA, C_AA, C_FC = 4608, 4612, 4616
LO_RANGES = ()


def lo_col(col):
    for a0, a1, l0 in LO_RANGES:
        if a0 <= col < a1:
            return l0 + col - a0
    return None


class Stream:
    def __init__(self, tmps, pss):
        self.tmps, self.pss, self.ti, self.pi = tmps, pss, 0, 0

    def tmp(self):
        self.ti += 1
        return self.tmps[self.ti % len(self.tmps)]

    def ps(self):
        self.pi += 1
        return self.pss[self.pi % len(self.pss)]


def run_streams(gens):
    gens = list(gens)
    while gens:
        nxt = []
        for g in gens:
            try:
                next(g)
                nxt.append(g)
            except StopIteration:
                pass
        gens = nxt


def pass1(P, A, cx, l, L, S, h_in):
    m = A.mark()
    W = A.sb([128, NCH, WC], BF16)
    ufull = A.sb([128, NCH, T], F32)
    wring = [A.sb([128, NCH, 256], F32) for _ in range(2)]
    cx.wri = 0
    NRING = 104
    ringmem = A.sb([128, NRING * T], F32, nres=NRING)
    ring = [V(ringmem.ap[:, i * T:(i + 1) * T], [ringmem.res[i]]) for i in range(NRING)]
    stage = [V(ringmem.ap[:, k * 1280:k * 1280 + 1156], ringmem.res[10 * k:10 * k + 10]) for k in range(2)]
    ps = cx.ps
    pslot = [V(ps[i // 4].ap[:, (i % 4) * T:(i % 4 + 1) * T], ps[i // 4].res) for i in range(32)]
    main = Stream(ring[80:104], [pslot[29], pslot[30], pslot[31]])
    ps256 = [V(ps[6].ap[:, 0:256], ps[6].res), V(ps[6].ap[:, 256:512], ps[6].res)]
    cx.p256 = 0

    def streams(n):
        nt_, np_ = 80 // n, 24 // n
        return [Stream(ring[i * nt_:(i + 1) * nt_], pslot[i * np_:(i + 1) * np_]) for i in range(n)]
    wi = 0
    PW = 1156
    for c in range(NCH):
        for hf in range(4):
            st = stage[wi % 2]
            c0 = hf * PW
            P.dma("sp", st, V(cx.d_win[l, c * 128:(c + 1) * 128, c0:c0 + PW], [cx.r_w]), "stg%d" % (wi % 2))
            dst = W[:, c, c0:c0 + PW]
            g = cx.gmix[:, l * NCH + c:l * NCH + c + 1]
            if wi % 2 == 0:
                P.act(dst, st, AF.Copy, scale=g)
            else:
                P.ts("dve", dst, st, g, ALU.mult)
            wi += 1
    pp = A.sb([128, 24], F32)
    P.dma("sp", pp, cx.d_pp[:, l * 24:(l + 1) * 24], "pp")
    p4 = A.sb([4, 8], F32)
    P.dma("sp", p4, cx.d_p4[:, l * 8:(l + 1) * 8], "pp")
    pq = A.sb([128, 16], F32)
    P.dma("sp", pq, cx.d_pq[:, l * 16:(l + 1) * 16], "pp")
    nega = A.sb([4, 1], F32)
    P.act(nega, p4[:, 3:4], AF.Exp)
    P.ts("dve", nega, nega, -1.0, ALU.mult)
    oml = A.sb([128, 4], F32)
    lbv = A.sb([128, 2], F32)
    for fc_ in range(2):
        P.copy("dve", lbv[:, fc_:fc_ + 1], cx.lbs[:, fc_ * L + l:fc_ * L + l + 1])
    P.ts("dve", oml[:, 0:2], lbv, -1.0, ALU.mult, 1.0, ALU.add)
    P.ts("dve", oml[:, 2:4], oml[:, 0:2], -1.0, ALU.mult)
    SA = [A.sb([128, 64], F32) for _ in range(2)]
    SB = [A.sb([128, 64], F32) for _ in range(2)]
    SD = [A.sb([128, 64], F32) for _ in range(2)]
    for s_ in SA + SB + SD:
        P.memset("dve", s_, 0.0)
    ccar = A.sb([4, 1], F32)
    P.memset("dve", ccar, 0.0)
    cvx = [A.sb([128, T + 3], F32) for _ in range(6)]
    for b_ in cvx:
        P.memset("dve", b_, 0.0)
    ht = A.sb([128, NCH, T], F32)
    sqb = A.sb([128, NCH, T], BF16)
    uT = A.sb([128, NCH, T], BF16)
    rs = A.sb([128, T], F32)
    blk = {k: A.sb([128, T], F32) for k in ["qa0", "qa1", "ka0", "ka1", "va0", "va1", "vtokA0", "vtokA1", "ktokA0", "ktokA1",
                                             "gt8", "egend", "betat", "bexp", "oA0", "oA1"]}
    vtokB = A.sb([128, 256], F32)
    vtokD = A.sb([128, 256], F32)
    g8 = A.sb([8, T], F32)
    gneg = A.sb([8, T], F32)
    ropeb = [A.sb([128, T], F32) for _ in range(4)]
    pq2s = A.sb([128, 1], F32)
    P.ts("dve", pq2s, pq[:, 2:3], 0.125, ALU.mult)
    nfb = A.sb([4, 1], F32)
    P.ts("dve", nfb, p4[:, 2:3], -1.0, ALU.mult)
    c4 = {k: A.sb([4, T], F32) for k in ["c", "r1", "r2", "sp"]}
    c3b = A.sb([4, 3, T], BF16)
    c3n = A.sb([4, 3, T], BF16)
    obf = [A.sb([128, T], BF16) for _ in range(8)]
    vcb = A.sb([128, 256], BF16)
    v32 = A.sb([128, 256], F32)
    g32 = A.sb([128, 256], F32)
    gcb = A.sb([128, 256], BF16)
    cx.oi = 0
    cx.prec = False
    hin_v = h_in.re("(c p) s -> p c s", p=128)
    nb = S // T

    def wload(col, M):
        cx.wri += 1
        wt = wring[cx.wri % 2]
        src = cx.d_win[l, :, col:col + M].rearrange("(c p) m -> p c m", p=128)
        P.dma("sp", wt[:, :, 0:M], V(src, [cx.r_w]), "wr%d" % (cx.wri % 2))
        return wt

    def proj_fm(st, col, M=128):
        pb = st.ps()
        if cx.prec:
            wt = wload(col, M)
            for c in range(NCH):
                P.mm(pb[0:M, :], wt[:, c, 0:M], ufull[:, c, :], start=(c == 0), stop=(c == NCH - 1), sig=(c == NCH - 1))
        else:
            for c in range(NCH):
                P.mm(pb[0:M, :], W[:, c, col:col + M], uT[:, c, :], start=(c == 0), stop=(c == NCH - 1), sig=(c == NCH - 1))
        return pb[0:M, :]

    def proj_tm(col, N):
        if N > 128:
            cx.p256 += 1
            pb = ps256[cx.p256 % 2]
        else:
            pb = main.ps()
        if cx.prec:
            wt = wload(col, N)
            for c in range(NCH):
                P.mm(pb[:, 0:N], ufull[:, c, :], wt[:, c, 0:N], start=(c == 0), stop=(c == NCH - 1), sig=(c == NCH - 1))
        else:
            for c in range(NCH):
                P.mm(pb[:, 0:N], uT[:, c, :], W[:, c, col:col + N], start=(c == 0), stop=(c == NCH - 1), sig=(c == NCH - 1))
        return pb[:, 0:N]

    def headnorm(st, src, kind, gain, sum_scale=1.0):
        if kind == "ln":
            mp = st.ps()
            P.mm(mp, cx.mavg, src, True, True, sig=True)
            yield
            mean = st.tmp()
            P.copy("act", mean, mp)
            yield
            cen = st.tmp()
            P.tt("dve", cen, src, mean, ALU.subtract)
            src = cen
            yield
        sq = st.tmp()
        P.tt("dve", sq, src, src, ALU.mult)
        yield
        e2 = st.ps()
        P.mm(e2, cx.mavg, sq, True, True, sig=True)
        yield
        r = st.tmp()
        P.act(r, e2, AF.Sqrt, bias=cx.epsc, scale=sum_scale)
        yield
        P.op("dve", lambda e: e.reciprocal(r.ap, r.ap), reads=(r,), writes=(r,))
        yield
        y = st.tmp()
        P.stt("dve", y, src, gain, r, ALU.mult, ALU.mult)
        yield
        return y

    def emit_mix(st, y, gate_col, fc, chunk, t0, func=AF.Silu):
        gate_ps = proj_fm(st, gate_col + fc * 128)
        yield
        sg = st.tmp()
        P.act(sg, gate_ps, func)
        yield
        ob = obf[cx.oi % 8]
        cx.oi += 1
        P.tt("dve", ob, y, sg, ALU.mult)
        P.dma("pool", cx.d_mixT[chunk * 128:(chunk + 1) * 128, t0:t0 + T], ob, "pmx%d" % (cx.oi % 8))
        if cx.prec:
            o32 = st.tmp()
            P.tt("dve", o32, y, sg, ALU.mult)
            P.dma("sp", cx.d_mix0[chunk * 128:(chunk + 1) * 128, :], o32, "m0")
        yield

    def gla(st, qT, kT, lfT, vtok, St, fc, kind, gain, gate_col, chunk, t0):
        bc = st.tmp()
        for ci in range(2):
            P.op("dve", lambda e, ci=ci: e.tensor_tensor_scan(bc.ap[:, ci * 64:(ci + 1) * 64], cx.onesf.ap[:, 0:64], lfT.ap[:, ci * 64:(ci + 1) * 64], 0.0, ALU.mult, ALU.add),
                 reads=(cx.onesf, lfT), writes=(bc,))
        yield
        eb = st.tmp()
        P.act(eb, bc, AF.Exp)
        enb = st.tmp()
        P.act(enb, bc, AF.Exp, scale=-1.0)
        ke = st.tmp()
        for ci in range(2):
            P.act(ke[:, ci * 64:(ci + 1) * 64], bc[:, ci * 64:(ci + 1) * 64], AF.Exp, bias=bc[:, ci * 64 + 63:ci * 64 + 64], scale=-1.0)
        yield
        qt = st.tmp()
        P.tt("dve", qt, qT, eb, ALU.mult)
        kt = st.tmp()
        P.tt("dve", kt, kT, enb, ALU.mult)
        kend = st.tmp()
        P.tt("dve", kend, kT, ke, ALU.mult)
        yield
        kp = st.ps()
        P.tr(kp, kend, cx.ident)
        yield
        kendTok = st.tmp()
        P.copy("act", kendTok, kp)
        yield
        o = st.tmp()
        for hh in range(2):
            b = 64 * hh
            h = 2 * fc + hh
            sc = st.ps()
            P.mm(sc, kt[b:b + 64, :], qt[b:b + 64, :], True, True, sig=True)
            u0 = st.ps()
            P.mm(u0[b:b + 64, 0:64], kendTok[0:64, b:b + 64], vtok[0:64, h * 64:(h + 1) * 64], True, True, sig=True)
            yield
            PT = st.tmp()
            P.tt("dve", PT, sc, cx.mask_bd, ALU.mult)
            yield
            oa = st.ps()
            P.mm(oa[b:b + 64, :], vtok[:, h * 64:(h + 1) * 64], PT, True, False)
            P.mm(oa[b:b + 64, 0:64], St[b:b + 64, :], qt[b:b + 64, 0:64], False, True, sig=True)
            P.stt("dve", St[b:b + 64, :], St[b:b + 64, :], eb[b:b + 64, 63:64], u0[b:b + 64, 0:64], ALU.mult, ALU.add)
            yield
            ob_ = st.ps()
            P.mm(ob_[b:b + 64, 0:64], St[b:b + 64, :], qt[b:b + 64, 64:128], True, True, sig=True)
            u1 = st.ps()
            P.mm(u1[b:b + 64, 0:64], kendTok[64:128, b:b + 64], vtok[64:128, h * 64:(h + 1) * 64], True, True, sig=True)
            P.copy("act", o[b:b + 64, :], oa[b:b + 64, :])
            yield
            P.tt("dve", o[b:b + 64, 64:128], o[b:b + 64, 64:128], ob_[b:b + 64, 0:64], ALU.add)
            P.stt("dve", St[b:b + 64, :], St[b:b + 64, :], eb[b:b + 64, 127:128], u1[b:b + 64, 0:64], ALU.mult, ALU.add)
            yield
        y = yield from headnorm(st, o, kind, gain)
        yield from emit_mix(st, y, gate_col, fc, chunk, t0)

    def stream_B(st, fc, t0):
        out = {}
        for nm, c0, c1, ct in (("q", C_QB, C_QBS, 0), ("k", C_KB, C_KBS, 2)):
            a_ = proj_fm(st, c0 + fc * 128)
            b_ = proj_fm(st, c1 + fc * 128)
            yield
            t1 = st.tmp()
            P.tt("dve", t1, a_, ropeb[ct], ALU.mult)
            t2 = st.tmp()
            P.tt("dve", t2, b_, ropeb[ct + 1], ALU.mult)
            yield
            r_ = st.tmp()
            P.tt("dve", r_, t1, t2, ALU.add)
            out[nm] = r_
            yield
        yield from gla(st, out["q"], out["k"], cx.lgB[fc], vtokB, SB[fc], fc, "ln", pq[:, 1:2], C_GB, 2 + fc, t0)

    def stream_D(st, fc, t0):
        qp = proj_fm(st, C_QD + fc * 128)
        fp_ = proj_fm(st, C_FD + fc * 128)
        yield
        qd = st.tmp()
        P.act(qd, qp, AF.Silu)
        sg = st.tmp()
        P.act(sg, fp_, AF.Sigmoid)
        yield
        fg = st.tmp()
        P.ts("dve", fg, sg, oml[:, fc:fc + 1], ALU.mult, lbv[:, fc:fc + 1], ALU.add)
        kd = st.tmp()
        P.ts("dve", kd, sg, oml[:, 2 + fc:3 + fc], ALU.mult, oml[:, fc:fc + 1], ALU.add)
        yield
        lf = st.tmp()
        P.act(lf, fg, AF.Ln)
        yield
        yield from gla(st, qd, kd, lf, vtokD, SD[fc], fc, "rms", pq[:, 4:5], C_GD, 6 + fc, t0)

    def stream_Cqk(st, fc, which, col, gain_, dst, t0):
        pp_ = proj_fm(st, col + fc * 128)
        yield
        o = st.tmp()
        P.copy("act", o, pp_)
        yield
        y = yield from headnorm(st, o, "rms", gain_)
        ob = obf[cx.oi % 8]
        cx.oi += 1
        P.copy("dve", ob, y)
        for hh in range(2):
            P.dma("sp", dst[2 * fc + hh, 0:64, t0:t0 + T], ob[64 * hh:64 * hh + 64, :], "cq%d" % (cx.oi % 8))
            if cx.prec:
                d0 = cx.d_q0 if which == "q" else cx.d_k0
                P.dma("sp", d0[2 * fc + hh, :, :], y[64 * hh:64 * hh + 64, :], "m0")
        yield

    def stream_Aconv(st, gi, nm, col, fc):
        xb = cvx[gi * 2 + fc]
        pp_ = proj_fm(st, col + fc * 128)
        P.copy("dve", xb[:, 0:3], xb[:, T:T + 3])
        yield
        P.copy("act", xb[:, 3:T + 3], pp_)
        yield
        y = st.tmp()
        cb = gi * 8 + fc * 4
        P.ts("dve", y, xb[:, 0:T], pp[:, cb:cb + 1], ALU.mult)
        yield
        for k_ in range(1, 4):
            P.stt("dve", y, xb[:, k_:k_ + T], pp[:, cb + k_:cb + k_ + 1], y, ALU.mult, ALU.add)
            yield
        if nm == "va":
            P.act(blk["va%d" % fc], y, AF.Silu)
            yield
            src = blk["va%d" % fc]
        else:
            z = st.tmp()
            P.act(z, y, AF.Silu)
            yield
            yn = yield from headnorm(st, z, "rms", cx.c0125 if nm == "qa" else cx.onec, sum_scale=64.0)
            P.copy("act", blk[nm + str(fc)], yn)
            yield
            src = blk[nm + str(fc)]
        if nm != "qa":
            tp = st.ps()
            P.tr(tp, src, cx.ident)
            yield
            P.copy("act", blk[("ktokA" if nm == "ka" else "vtokA") + str(fc)], tp)
            yield

    def stream_Ahead(st, h):
        fc, hh = h // 2, h % 2
        b = 64 * hh
        kT = blk["ka%d" % fc]
        qT = blk["qa%d" % fc]
        ktok = blk["ktokA%d" % fc][:, b:b + 64]
        vtok = blk["vtokA%d" % fc][:, b:b + 64]
        St = SA[fc]
        kk = st.ps()
        P.mm(kk, kT[b:b + 64, :], kT[b:b + 64, :], True, True, sig=True)
        gb = st.ps()
        P.mm(gb, cx.sel4[:, h * 128:(h + 1) * 128], gneg[0:4, :], True, False)
        P.mm(gb, cx.ident, cx.mneg_strict, False, True, sig=True)
        x = st.tmp()
        P.ts("dve", x[:, 0:64], vtok, blk["betat"][:, h:h + 1], ALU.mult)
        P.ts("dve", x[:, 64:128], ktok, blk["bexp"][:, 4 + h:5 + h], ALU.mult)
        yield
        gam = st.tmp()
        P.act(gam, gb, AF.Exp, bias=blk["gt8"][:, h:h + 1], scale=1.0)
        yield
        Am = st.tmp()
        P.stt("dve", Am, kk, blk["betat"][:, h:h + 1], gam, ALU.mult, ALU.mult)
        yield
        atp = st.ps()
        P.tr(atp, Am, cx.ident)
        yield
        AT = st.tmp()
        P.copy("act", AT, atp)
        yield
        xp = st.ps()
        P.mm(xp, AT, x, True, True, sig=True)
        p2 = st.ps()
        P.mm(p2, Am, AT, True, True, sig=True)
        p1 = st.ps()
        P.mm(p1, AT, Am, True, True, sig=True)
        yield
        Pm, PTm = Am, AT
        for it in range(6):
            x2 = st.tmp()
            P.tt("dve", x2, x, xp, ALU.subtract if it == 0 else ALU.add)
            x = x2
            nPT = st.tmp()
            P.copy("act", nPT, p2)
            if it < 5:
                nP = st.tmp()
                P.copy("dve", nP, p1)
            else:
                nP = None
            yield
            xp = st.ps()
            P.mm(xp, nPT, x, True, True, sig=True)
            if it < 5:
                p2 = st.ps()
                P.mm(p2, nP, nPT, True, True, sig=True)
                if it < 4:
                    p1 = st.ps()
                    P.mm(p1, nPT, nP, True, True, sig=True)
            yield
            Pm, PTm = nP, nPT
        x2 = st.tmp()
        P.tt("dve", x2, x, xp, ALU.add)
        x = x2
        yield
        wtp = st.ps()
        if hh == 0:
            P.tr(wtp[0:64, :], x[:, 64:128], cx.ident)
        else:
            P.tr(wtp, x, cx.ident)
        kq = st.ps()
        P.mm(kq, kT[b:b + 64, :], qT[b:b + 64, :], True, True, sig=True)
        gb2a = st.ps()
        P.mm(gb2a, cx.sel4[:, h * 128:(h + 1) * 128], g8[0:4, :], True, True, sig=True)
        gb2 = st.ps()
        P.mm(gb2, cx.sel4[:, h * 128:(h + 1) * 128], g8[0:4, :], True, False)
        P.mm(gb2, cx.ident, cx.mneg_tril, False, True, sig=True)
        yield
        wT = st.tmp()
        P.copy("act", wT[b:b + 64, :], wtp[b:b + 64, :])
        egbc = st.tmp()
        P.act(egbc, gb2a, AF.Exp)
        gam2 = st.tmp()
        P.act(gam2, gb2, AF.Exp, bias=blk["gt8"][:, 4 + h:5 + h], scale=1.0)
        ed = st.tmp()
        P.act(ed[:, 0:1], blk["gt8"][:, h:h + 1], AF.Exp, bias=blk["egend"][:, h:h + 1], scale=-1.0)
        yield
        ws = st.ps()
        P.mm(ws[:, 0:64], wT[b:b + 64, :], St[b:b + 64, :], True, True, sig=True)
        PT = st.tmp()
        P.tt("dve", PT, kq, gam2, ALU.mult)
        qg = st.tmp()
        P.tt("dve", qg[b:b + 64, :], qT[b:b + 64, :], egbc[b:b + 64, :], ALU.mult)
        kd = st.tmp()
        P.ts("dve", kd[:, 0:64], ktok, ed[:, 0:1], ALU.mult)
        yield
        vnew = st.tmp()
        P.tt("dve", vnew[:, 0:64], x[:, 0:64], ws[:, 0:64], ALU.subtract)
        yield
        op_ = st.ps()
        P.mm(op_[b:b + 64, :], St[b:b + 64, :], qg[b:b + 64, :], True, False)
        P.mm(op_[b:b + 64, :], vnew[:, 0:64], PT, False, True, sig=True)
        up = st.ps()
        P.mm(up[b:b + 64, 0:64], kd[:, 0:64], vnew[:, 0:64], True, True, sig=True)
        yield
        P.copy("act", blk["oA%d" % fc][b:b + 64, :], op_[b:b + 64, :])
        P.stt("dve", St[b:b + 64, :], St[b:b + 64, :], blk["egend"][b:b + 64, 4 + h:5 + h], up[b:b + 64, 0:64], ALU.mult, ALU.add)
        yield

    def stream_Afin(st, fc, t0):
        y = yield from headnorm(st, blk["oA%d" % fc], "rms", pq[:, 0:1])
        yield from emit_mix(st, y, C_GA, fc, fc, t0)

    P.dma("sp", ht, hin_v[:, :, 0:T], "ht")
    for bi in range(nb):
        t0 = bi * T
        prec = bi == 0
        cx.prec = prec
        rmsnorm_tile(P, cx, ht, sqb, uT, ps[7], rs, T, 0)
        if prec:
            for c in range(NCH):
                P.stt("dve", ufull[:, c, :], ht[:, c, :], cx.gmix[:, l * NCH + c:l * NCH + c + 1], rs[:, 0:T], ALU.mult, ALU.mult)
        if bi + 1 < nb:
            P.dma("sp", ht, hin_v[:, :, t0 + T:t0 + 2 * T], "ht")
        for i_, nm in enumerate(["cosq", "sinq", "cosk", "sink"]):
            P.dma("sp", ropeb[i_], cx.d_rope[nm][:, t0:t0 + T], "rope%d" % i_)
        vp = proj_tm(C_VC, 256)
        P.copy("act", vcb, vp)
        P.dma("sp", cx.d_vc[t0:t0 + T, :], vcb, "vc")
        if prec:
            P.copy("act", v32, vp)
            P.dma("sp", cx.d_v0, v32, "m0")
        gp = proj_tm(C_GC, 256)
        P.act(gcb, gp, AF.Sigmoid)
        P.dma("sp", cx.d_gc[t0:t0 + T, :], gcb, "gc")
        if prec:
            P.act(g32, gp, AF.Sigmoid)
            P.dma("sp", cx.d_g0, g32, "m0")
        vp = proj_tm(C_VB, 256)
        P.copy("act", vtokB, vp)
        vp = proj_tm(C_ID, 256)
        P.copy("act", vtokD, vp)
        bp = proj_tm(C_BA, 4)
        P.act(blk["betat"][:, 0:4], bp[:, 0:4], AF.Sigmoid)
        fp = proj_fm(main, C_FC, 4)
        P.act(c4["sp"], fp, AF.Exp, bias=nfb[:, 0:1], scale=-1.0)
        P.act(c4["sp"], c4["sp"], AF.Ln, bias=1.0, scale=1.0)
        P.op("dve", lambda e: e.tensor_tensor_scan(c4["r1"].ap, cx.onesf.ap[0:4, :], c4["sp"].ap, 0.0, ALU.mult, ALU.subtract),
             reads=(cx.onesf, c4["sp"]), writes=(c4["r1"],))
        P.ts("dve", c4["c"], c4["r1"], ccar[:, 0:1], ALU.add)
        P.copy("dve", ccar, c4["c"][:, T - 1:T])
        if prec:
            P.dma("sp", cx.d_c0[0], c4["c"], "m0")
            P.ts("dve", c4["sp"], c4["c"], -1.0, ALU.mult)
            P.dma("sp", cx.d_c0[1], c4["sp"], "m0")
        P.copy("dve", c3b[:, 0, :], c4["c"])
        P.tt("dve", c4["r1"], c4["c"], c3b[:, 0, :], ALU.subtract)
        P.copy("dve", c3b[:, 1, :], c4["r1"])
        P.tt("dve", c4["r2"], c4["r1"], c3b[:, 1, :], ALU.subtract)
        P.copy("dve", c3b[:, 2, :], c4["r2"])
        P.ts("dve", c3n, c3b, -1.0, ALU.mult)
        P.dma("sp", cx.d_qaug[:, 64:67, t0:t0 + T], c3b, "c4q")
        P.dma("sp", cx.d_kaug[:, 67:70, t0:t0 + T], c3n, "c4k")
        ap_ = proj_fm(main, C_AA, 4)
        P.act(c4["r1"], ap_, AF.Exp, bias=p4[:, 1:2], scale=1.0)
        P.act(c4["r1"], c4["r1"], AF.Ln, bias=1.0, scale=1.0)
        P.ts("dve", c4["r2"], c4["r1"], nega[:, 0:1], ALU.mult)
        P.op("dve", lambda e: e.tensor_tensor_scan(g8.ap[0:4, :], cx.onesf.ap[0:4, :], c4["r2"].ap, 0.0, ALU.mult, ALU.add),
             reads=(cx.onesf, c4["r2"]), writes=(g8,))
        P.ts("dve", gneg[0:4, :], g8[0:4, :], -1.0, ALU.mult)
        tp = main.ps()
        P.tr(tp[:, 0:4], g8[0:4, :], cx.ident[0:4, 0:4])
        P.copy("act", blk["gt8"][:, 0:4], tp[:, 0:4])
        P.ts("dve", blk["gt8"][:, 4:8], blk["gt8"][:, 0:4], -1.0, ALU.mult)
        P.act(blk["bexp"][:, 0:4], blk["gt8"][:, 0:4], AF.Exp)
        P.tt("dve", blk["bexp"][:, 4:8], blk["bexp"][:, 0:4], blk["betat"][:, 0:4], ALU.mult)
        tp2 = main.ps()
        P.tr(tp2[:, 0:4], c4["r2"], cx.ident[0:4, 0:4])
        latok = main.tmp()
        P.copy("act", latok[:, 0:4], tp2[:, 0:4])
        gep = main.ps()
        P.mm(gep[:, 0:4], cx.onesf, latok[:, 0:4], True, True, sig=True)
        P.copy("act", blk["egend"][:, 0:4], gep[:, 0:4])
        P.act(blk["egend"][:, 4:8], gep[:, 0:4], AF.Exp)
        s4 = streams(4)
        if 'C' in cx.en:
          run_streams([stream_Cqk(s4[0], 0, "q", C_QC, pq2s[:, 0:1], cx.d_qaug, t0), stream_Cqk(s4[1], 0, "k", C_KC, pq[:, 3:4], cx.d_kaug, t0),
                     stream_Cqk(s4[2], 1, "q", C_QC, pq2s[:, 0:1], cx.d_qaug, t0), stream_Cqk(s4[3], 1, "k", C_KC, pq[:, 3:4], cx.d_kaug, t0)])
        s4 = streams(4)
        if 'B' in cx.en:
          run_streams([stream_B(s4[0], 0, t0), stream_B(s4[1], 1, t0), stream_D(s4[2], 0, t0), stream_D(s4[3], 1, t0)])
        s6 = streams(6)
        if 'A' in cx.en or 'X' in cx.en:
          run_streams([stream_Aconv(s6[gi * 2 + fc], gi, nm, col, fc) for gi, (nm, col) in enumerate((("qa", C_QA), ("ka", C_KA), ("va", C_VA))) for fc in range(2)])
        s4 = streams(4)
        if 'A' in cx.en or 'Y' in cx.en:
          run_streams([stream_Ahead(s4[h], h) for h in range(4)])
        s4 = streams(4)
        if 'A' in cx.en or 'Z' in cx.en:
          run_streams([stream_Afin(s4[fc], fc, t0) for fc in range(2)])
    P.barrier()
    A.reset(m)


def pass2(P, A, cx, l, L, S, h_in, h_out):
    m = A.mark()
    nb = S // T
    Wout = A.sb([128, NCH, D], BF16)
    stage = [A.sb([128, 1024], F32) for _ in range(2)]
    Kc = A.sb([70, 4, S], BF16)
    Vc = A.sb([128, nb, 4, 66], BF16)
    ps = cx.ps
    for c in range(NCH):
        st = stage[c % 2]
        P.dma("sp", st, V(cx.d_wout[l, c * 128:(c + 1) * 128, :], [cx.r_w]), "stg%d" % (c % 2))
        if c % 2 == 0:
            P.copy("act", Wout[:, c, :], st)
        else:
            P.copy("dve", Wout[:, c, :], st)
    P.memset("dve", Vc, 1.0)
    P.memset("dve", Kc[64:70, :, :], 1.0)
    for h in range(4):
        P.dma("sp", Kc[0:64, h, :], cx.d_kaug[h, 0:64, :], "kc")
        P.dma("sp", Kc[67:70, h, :], cx.d_kaug[h, 67:70, :], "kc")
    vsrc = cx.d_vc.re("(n p) (h d) -> p n h d", p=128, h=4)
    for h in range(4):
        P.dma("sp", Vc[:, :, h, 0:64], vsrc[:, :, h, :], "vcl")
    qa = [A.sb([70, 4, T], BF16) for _ in range(2)]
    for q_ in qa:
        P.memset("dve", q_[64:70, :, :], 1.0)
    gq = [A.sb([128, 256], BF16) for _ in range(2)]
    mx = [A.sb([128, NCH, T], BF16) for _ in range(2)]
    ht = [A.sb([128, NCH, T], F32) for _ in range(2)]
    pT = [A.sb([128, T], BF16) for _ in range(4)]
    oc = A.sb([128, 256], F32)
    rl = A.sb([128, 4], F32)
    hin_v = h_in.re("(c p) s -> p c s", p=128)
    hout_v = h_out.re("(c p) s -> p c s", p=128)
    mixv = cx.d_mixT.re("(c p) s -> p c s", p=128)
    qsrc = cx.d_qaug.re("h r s -> r h s")
    Qa0 = A.sb([66, 4, T], F32)
    Ka0 = A.sb([66, 4, T], F32)
    V0 = A.sb([128, 4, 66], F32)
    gq0 = A.sb([128, 256], F32)
    mx0 = A.sb([128, NCH, T], F32)
    pT0 = [A.sb([128, T], F32) for _ in range(2)]
    wt0 = [A.sb([128, NCH, 128], F32) for _ in range(2)]
    qa5 = [A.sb([70, 4, 4 * T], BF16) for _ in range(2)]
    for q_ in qa5:
        P.memset("dve", q_[64:70, :, :], 1.0)
    pT5 = [A.sb([128, 4 * T], BF16) for _ in range(3)]
    cx.kk = 0

    def load_tile_inputs(sl, t0):
        P.dma("sp", gq[sl], cx.d_gc[t0:t0 + T, :], "gq%d" % sl)
        P.dma("sp", mx[sl][:, 0:4, :], mixv[:, 0:4, t0:t0 + T], "mx%d" % sl)
        P.dma("sp", mx[sl][:, 6:8, :], mixv[:, 6:8, t0:t0 + T], "mx%d" % sl)
        P.dma("sp", ht[sl], hin_v[:, :, t0:t0 + T], "ht%d" % sl)

    def epilogue(o_ps, sl, t0, tpb, pbs):
        P.op("dve", lambda e: e.reciprocal(rl.ap, o_ps.ap[:, 0:260].rearrange("p (h d) -> p h d", h=4)[:, :, 64]), reads=(o_ps,), writes=(rl,))
        for h in range(4):
            P.ts("dve", oc[:, h * 64:(h + 1) * 64], o_ps[:, h * 65:h * 65 + 64], rl[:, h:h + 1], ALU.mult)
        P.tt("dve", oc, oc, gq[sl], ALU.mult)
        for fc in range(2):
            tp = tpb[fc][:, 0:T]
            P.tr(tp, oc[:, fc * 128:(fc + 1) * 128], cx.ident)
            P.copy("act", mx[sl][:, 4 + fc, :], tp)
        for dc in range(NCH):
            pb = pbs[cx.kk % len(pbs)][:, 0:T]
            cx.kk += 1
            for kc in range(NCH):
                P.mm(pb, Wout[:, kc, dc * 128:(dc + 1) * 128], mx[sl][:, kc, :], kc == 0, kc == NCH - 1, sig=(kc == NCH - 1))
            P.tt("dve", ht[sl][:, dc, :], ht[sl][:, dc, :], pb, ALU.add)
        P.dma("pool", hout_v[:, :, t0:t0 + T], ht[sl], "hs%d" % sl)

    k = 0
    for qt in range(nb):
        t0 = qt * T
        sl = qt % 2
        if qt == 0:
            P.memset("dve", Qa0[64:66, :, :], 1.0)
            P.memset("dve", Ka0[64:66, :, :], 1.0)
            P.memset("dve", V0, 1.0)
            P.dma("sp", Qa0[0:64, :, :], cx.d_q0.re("h d t -> d h t"), "p0a")
            P.dma("sp", Ka0[0:64, :, :], cx.d_k0.re("h d t -> d h t"), "p0b")
            P.dma("sp", Qa0[64:65, :, :], cx.d_c0[0:1, :, :], "p0c")
            P.dma("sp", Ka0[65:66, :, :], cx.d_c0[1:2, :, :], "p0d")
            P.dma("sp", V0[:, :, 0:64], cx.d_v0.re("t (h d) -> t h d", h=4), "p0e")
            P.dma("sp", gq0, cx.d_g0, "p0f")
            m0v = cx.d_mix0.re("(c p) t -> p c t", p=128)
            P.dma("sp", mx0[:, 0:4, :], m0v[:, 0:4, :], "p0g")
            P.dma("sp", mx0[:, 6:8, :], m0v[:, 6:8, :], "p0h")
            P.dma("sp", ht[sl], hin_v[:, :, t0:t0 + T], "ht%d" % sl)
            o_ps = ps[6]
            for h in range(4):
                sp_ = ps[k % 4][:, 0:T]
                P.mm(sp_, Ka0[:, h, :], Qa0[:, h, :], True, False)
                P.mm(sp_, cx.ident, cx.mneg_tril, False, True, sig=True)
                pt = pT0[h % 2]
                P.act(pt, sp_, AF.Exp)
                P.mm(o_ps[:, h * 65:(h + 1) * 65], pt, V0[:, h, 0:65], True, True, sig=True)
                k += 1
            P.op("dve", lambda e: e.reciprocal(rl.ap, o_ps.ap[:, 0:260].rearrange("p (h d) -> p h d", h=4)[:, :, 64]), reads=(o_ps,), writes=(rl,))
            for h in range(4):
                P.ts("dve", oc[:, h * 64:(h + 1) * 64], o_ps[:, h * 65:h * 65 + 64], rl[:, h:h + 1], ALU.mult)
            P.tt("dve", oc, oc, gq0, ALU.mult)
            for fc in range(2):
                tp = ps[4 + fc][:, 0:T]
                P.tr(tp, oc[:, fc * 128:(fc + 1) * 128], cx.ident)
                P.copy("act", mx0[:, 4 + fc, :], tp)
            for dc in range(NCH):
                wt = wt0[dc % 2]
                P.dma("sp", wt, V(cx.d_wout[l, :, dc * 128:(dc + 1) * 128].rearrange("(c p) m -> p c m", p=128), [cx.r_w]), "w0%d" % (dc % 2))
                pb = ps[k % 4][:, 0:T]
                k += 1
                for kc in range(NCH):
                    P.mm(pb, wt[:, kc, :], mx0[:, kc, :], kc == 0, kc == NCH - 1, sig=(kc == NCH - 1))
                P.tt("dve", ht[sl][:, dc, :], ht[sl][:, dc, :], pb, ALU.add)
            P.dma("pool", hout_v[:, :, t0:t0 + T], ht[sl], "hs%d" % sl)
            continue
        if qt >= 4 and nb % 4 == 0:
            if qt % 4 != 0:
                continue
            q0 = qt
            s5 = (qt // 4) % 2
            P.dma("sp", qa5[s5][0:67, :, :], qsrc[0:67, :, t0:t0 + 4 * T], "qa5%d" % s5)
            for i in range(4):
                sli = (q0 + i) % 2
                ti = t0 + i * T
                if i < 2:
                    load_tile_inputs(sli, ti)
            obank = [ps[4 + i] for i in range(4)]
            for h in range(4):
                for kb in range(q0 + 4):
                    j = kb - q0
                    c0 = 128 * j if j >= 0 else 0
                    sp_ = ps[k % 3]
                    P.mm(sp_[:, c0:512], Kc[:, h, kb * T:(kb + 1) * T], qa5[s5][:, h, c0:512], True, j < 0, sig=(j < 0))
                    if j >= 0:
                        P.mm(sp_[:, c0:c0 + T], cx.ident_bf, cx.mneg_tril_bf, False, True, sig=True)
                    pt = pT5[k % 3]
                    P.act(pt[:, c0:512], sp_[:, c0:512], AF.Exp)
                    for i in range(max(j, 0), 4):
                        last = kb == q0 + i
                        P.mm(obank[i][:, h * 65:(h + 1) * 65], pt[:, i * T:(i + 1) * T], Vc[:, kb, h, 0:65], kb == 0, last, sig=last)
                    k += 1
            for i in range(4):
                sli = (q0 + i) % 2
                ti = t0 + i * T
                if i >= 2:
                    load_tile_inputs(sli, ti)
                epilogue(obank[i], sli, ti, [ps[3], ps[3]], ps[0:3])
            continue
        P.dma("sp", qa[sl][0:67, :, :], qsrc[0:67, :, t0:t0 + T], "qa%d" % sl)
        load_tile_inputs(sl, t0)
        o_ps = ps[6]
        for h in range(4):
            for kb in range(qt + 1):
                sp_ = ps[k % 4][:, 0:T]
                diag = kb == qt
                P.mm(sp_, Kc[:, h, kb * T:(kb + 1) * T], qa[sl][:, h, :], True, not diag, sig=not diag)
                if diag:
                    P.mm(sp_, cx.ident_bf, cx.mneg_tril_bf, False, True, sig=True)
                pt = pT[k % 4]
                P.act(pt, sp_, AF.Exp)
                P.mm(o_ps[:, h * 65:(h + 1) * 65], pt, Vc[:, kb, h, 0:65], kb == 0, kb == qt, sig=(kb == qt))
                k += 1
        epilogue(o_ps, sl, t0, [ps[4], ps[5]], ps[0:4])
    P.barrier()
    A.reset(m)


def setup_consts(P, A, cx, L):
    def cst(name, shape, dt=F32):
        t = A.sb(shape, F32)
        P.dma("sp", t, cx.dc[name], "const")
        return t
    cx.ones_bf = A.sb([128, 128], BF16)
    P.memset("dve", cx.ones_bf, 1.0)
    cx.onesf = A.sb([128, 128], F32)
    P.memset("dve", cx.onesf, 1.0)
    cx.ones4b = A.sb([4, T], BF16)
    P.memset("dve", cx.ones4b, 1.0)
    cx.epsc = A.sb([128, 1], F32)
    P.memset("dve", cx.epsc, EPS)
    cx.onec = A.sb([128, 1], F32)
    P.memset("dve", cx.onec, 1.0)
    cx.c0125 = A.sb([128, 1], F32)
    P.memset("dve", cx.c0125, 0.125)
    cx.gffn = cst("gffn", [128, L * NCH])
    cx.gmix = cst("gmix", [128, L * NCH])
    cx.cw = cst("cw", [128, L * NF * 3])
    cx.ident = cst("ident", [128, 128])
    cx.mavg = cst("mavg", [128, 128])
    cx.mask_bd = cst("mask_bd", [128, 128])
    cx.mneg_strict = cst("mneg_strict", [128, 128])
    cx.mneg_tril = cst("mneg_tril", [128, 128])
    cx.sel4 = cst("sel4", [4, 512])
    cx.lgB = [cst("lgB%d" % i, [128, T]) for i in range(2)]
    lbd = cst("lbd", [128, 2 * L])
    P.barrier()
    cx.ident_bf = A.sb([128, 128], BF16)
    P.copy("dve", cx.ident_bf, cx.ident)
    cx.mneg_tril_bf = A.sb([128, 128], BF16)
    P.copy("dve", cx.mneg_tril_bf, cx.mneg_tril)
    e = A.sb([128, 2 * L], F32)
    P.act(e, lbd, AF.Exp)
    cx.lbs = A.sb([128, 2 * L], F32)
    ssum = A.sb([128, 2], F32)
    for fc in range(2):
        P.copy("dve", ssum[:, fc:fc + 1], e[:, fc * L:fc * L + 1])
        for l in range(1, L):
            P.tt("dve", ssum[:, fc:fc + 1], ssum[:, fc:fc + 1], e[:, fc * L + l:fc * L + l + 1], ALU.add)
    P.op("dve", lambda en: en.reciprocal(ssum.ap, ssum.ap), reads=(ssum,), writes=(ssum,))
    for fc in range(2):
        P.memset("dve", cx.lbs[:, fc * L:fc * L + 1], 0.0)
        for l in range(1, L):
            P.stt("dve", cx.lbs[:, fc * L + l:fc * L + l + 1], e[:, fc * L + l:fc * L + l + 1], ssum[:, fc:fc + 1],
                  cx.lbs[:, fc * L + l - 1:fc * L + l], ALU.mult, ALU.add)


CONST_SHAPES = None


def build(S, L, run_layers=None, dbg=None, en="ABCD"):
    nc = bass.Bass("TRN2", target_bir_lowering=False)
    cx = Ctx()
    cx.en = en
    P = Prog(nc)
    st = contextlib.ExitStack()
    with st:
        def din(name, shape):
            return nc.dram_tensor(name, list(shape), F32, kind="ExternalInput").ap()
        cx.d_xT = V(din("xT", [D, S]), [Res()])
        cx.dc = {}
        for name, shape in (("gffn", [128, L * NCH]), ("gmix", [128, L * NCH]), ("cw", [128, L * NF * 3]), ("ident", [128, 128]),
                            ("mavg", [128, 128]), ("mask_bd", [128, 128]), ("mneg_strict", [128, 128]), ("mneg_tril", [128, 128]),
                            ("sel4", [4, 512]), ("lgB0", [128, T]), ("lgB1", [128, T]), ("lbd", [128, 2 * L])):
            cx.dc[name] = V(din(name, shape), [Res()])
        cx.d_pp = V(din("pp", [128, L * 24]), [Res()])
        cx.d_pq = V(din("pq", [128, L * 16]), [Res()])
        cx.d_p4 = V(din("p4", [4, L * 8]), [Res()])
        cx.d_rope = {nm: V(din(nm, [128, S]), [Res()]) for nm in ("cosq", "sinq", "cosk", "sink")}
        cx.d_win = din("w_in", [L, D, WC])
        cx.d_wout = din("w_out", [L, D, D])
        cx.d_wup = din("w_up", [L, D, 2 * DFF])
        cx.d_wdn = din("w_down", [L, DFF, D])
        cx.r_w = Res()
        yT = V(nc.dram_tensor("yT", [D, S], F32, kind="ExternalOutput").ap(), [Res()])
        hA = V(nc.dram_tensor("hA", [D, S], F32, kind="Internal").ap(), [Res()])
        hB = V(nc.dram_tensor("hB", [D, S], F32, kind="Internal").ap(), [Res()])
        cx.d_mixT = V(nc.dram_tensor("mixT", [D, S], BF16, kind="Internal").ap(), [Res()])
        cx.d_qaug = V(nc.dram_tensor("qaug", [4, 70, S], BF16, kind="Internal").ap(), [Res()])
        cx.d_kaug = V(nc.dram_tensor("kaug", [4, 70, S], BF16, kind="Internal").ap(), [Res()])
        cx.d_vc = V(nc.dram_tensor("vcs", [S, 256], BF16, kind="Internal").ap(), [Res()])
        cx.d_gc = V(nc.dram_tensor("gcs", [S, 256], BF16, kind="Internal").ap(), [Res()])
        cx.d_mix0 = V(nc.dram_tensor("mix0", [D, T], F32, kind="Internal").ap(), [Res()])
        cx.d_q0 = V(nc.dram_tensor("q0s", [4, 64, T], F32, kind="Internal").ap(), [Res()])
        cx.d_k0 = V(nc.dram_tensor("k0s", [4, 64, T], F32, kind="Internal").ap(), [Res()])
        cx.d_c0 = V(nc.dram_tensor("c0s", [2, 4, T], F32, kind="Internal").ap(), [Res()])
        cx.d_v0 = V(nc.dram_tensor("v0s", [T, 256], F32, kind="Internal").ap(), [Res()])
        cx.d_g0 = V(nc.dram_tensor("g0s", [T, 256], F32, kind="Internal").ap(), [Res()])
        A = Alloc(nc)
        cx.ps = [V(st.enter_context(nc.psum_tensor("ps%d" % i, [128, 512], F32))[:, :], [Res("ps%d" % i, excl=True)]) for i in range(8)]
        setup_consts(P, A, cx, L)
        P.barrier()
        layers = list(range(L)) if run_layers is None else run_layers
        h_cur = cx.d_xT
        for i, l in enumerate(layers):
            last = i == len(layers) - 1
            if dbg != "nop1":
                pass1(P, A, cx, l, L, S, h_cur)
            if dbg == "p1":
                break
            if dbg == "p2" or dbg == "nop1":
                pass2(P, A, cx, l, L, S, h_cur, yT)
                break
            pass2(P, A, cx, l, L, S, h_cur, hB)
            pass3(P, A, cx, l, L, S, hB, yT if last else hA)
            h_cur = hA
        P.barrier()
        P.emit(st)
    return nc


def host_inputs(inp, S, L):
    f32 = np.float32
    GW = 256
    offs = np.cumsum([0, GW, GW, GW, GW, 4, 4, GW, GW, GW, GW, GW, GW, GW, GW, 4, GW, GW, GW, GW])
    (o_qa, o_ka, o_va, o_ga, o_ba, o_aa, o_qb, o_kb, o_vb, o_gb, o_qc, o_kc, o_vc, o_gc, o_fc, o_qd, o_fd, o_id, o_gd) = offs[:-1]
    sw = np.arange(256).reshape(4, 64)
    sw = np.concatenate([sw[:, 32:], sw[:, :32]], axis=1).reshape(-1)
    cols = np.concatenate([
        o_qa + np.arange(256), o_ka + np.arange(256), o_va + np.arange(256), o_ga + np.arange(256),
        o_qb + np.arange(256), o_qb + sw, o_kb + np.arange(256), o_kb + sw, o_gb + np.arange(256),
        o_qc + np.arange(256), o_kc + np.arange(256),
        o_qd + np.arange(256), o_fd + np.arange(256), o_gd + np.arange(256),
        o_vb + np.arange(256), o_vc + np.arange(256), o_id + np.arange(256), o_gc + np.arange(256),
        o_ba + np.arange(4), o_aa + np.arange(4), o_fc + np.arange(4), np.zeros(4, np.int64)])
    assert cols.shape[0] == WC
    w_in = np.ascontiguousarray(np.asarray(inp["w_in"], f32)[:, :, cols])

    def lay_g(g):
        return np.ascontiguousarray(np.asarray(g, f32).reshape(L, NCH, 128).transpose(2, 0, 1).reshape(128, L * NCH))
    cw = np.ascontiguousarray(np.asarray(inp["conv_ffn"], f32).reshape(L, 3, NF, 128).transpose(3, 0, 2, 1).reshape(128, L * NF * 3))
    pp = np.ascontiguousarray(np.asarray(inp["conv_qkv_a"], f32).reshape(L, 4, 3, 2, 128).transpose(4, 0, 2, 3, 1).reshape(128, L * 24))
    pq = np.zeros((128, L, 16), f32)
    for j, nm in ((0, "onorm_a"), (1, "onorm_b"), (2, "qnorm_c"), (3, "knorm_c"), (4, "onorm_d")):
        v = np.asarray(inp[nm], f32)
        pq[:, :, j] = np.concatenate([v, v], axis=1).T
    p4 = np.zeros((4, L, 8), f32)
    p4[:, :, 1] = np.asarray(inp["dt_bias_a"], f32).T
    p4[:, :, 2] = np.asarray(inp["fbias_c"], f32).T
    p4[:, :, 3] = np.asarray(inp["a_log_a"], f32).T
    lbd = np.ascontiguousarray(np.asarray(inp["lower_bound_d"], f32).reshape(L, 2, 128).transpose(2, 1, 0).reshape(128, 2 * L))
    ar = np.arange(128)
    ident = np.eye(128, dtype=f32)
    mavg = (ar[:, None] // 64 == ar[None, :] // 64).astype(f32) / 64.0
    mask_bd = ((ar[:, None] // 64 == ar[None, :] // 64) & (ar[:, None] <= ar[None, :])).astype(f32)
    mneg_strict = np.where(ar[None, :] < ar[:, None], 0.0, NEG).astype(f32)
    mneg_tril = np.where(ar[:, None] <= ar[None, :], 0.0, NEG).astype(f32)
    sel4 = np.zeros((4, 4, 128), f32)
    for h in range(4):
        sel4[h, h, :] = 1.0
    sel4 = np.ascontiguousarray(sel4.transpose(1, 0, 2).reshape(4, 512))
    lgh = np.log1p(-np.exp2(-5.0 - np.arange(4, dtype=f32))).astype(f32)
    lgB = [np.ascontiguousarray(np.repeat(lgh[2 * fc + ar // 64][:, None], T, axis=1).astype(f32)) for fc in range(2)]
    inv_freq = (10000.0 ** (-np.arange(0, 64, 2, dtype=f32) / 64)).astype(f32)
    ang = (np.arange(S, dtype=f32)[:, None] * inv_freq[None, :]).astype(f32)
    cos, sin = np.cos(ang).astype(f32), np.sin(ang).astype(f32)
    d = ar % 64
    cosq = np.ascontiguousarray(cos[:, d % 32].T)
    sinq = np.ascontiguousarray((sin[:, d % 32] * np.where(d < 32, -1.0, 1.0)[None, :]).T.astype(f32))
    out = {
        "xT": None, "gffn": lay_g(inp["norm_ffn"]), "gmix": lay_g(inp["norm_mix"]), "cw": cw, "ident": ident, "mavg": mavg,
        "mask_bd": mask_bd, "mneg_strict": mneg_strict, "mneg_tril": mneg_tril, "sel4": sel4, "lgB0": lgB[0], "lgB1": lgB[1], "lbd": lbd,
        "pp": pp, "pq": np.ascontiguousarray(pq.reshape(128, L * 16)), "p4": np.ascontiguousarray(p4.reshape(4, L * 8)),
        "cosq": cosq, "sinq": sinq, "cosk": (cosq * f32(0.125)).astype(f32), "sink": (sinq * f32(0.125)).astype(f32),
        "w_in": w_in, "w_out": np.asarray(inp["w_out"], f32), "w_up": np.asarray(inp["w_up"], f32), "w_down": np.asarray(inp["w_down"], f32),
    }
    return out


_NC_CACHE = {}


def kernel(**inputs):
    x = np.asarray(inputs["x"], np.float32)
    B, S, _ = x.shape
    L = np.asarray(inputs["w_in"]).shape[0]
    key = (S, L)
    if key not in _NC_CACHE:
        _NC_CACHE[key] = build(S, L)
    nc = _NC_CACHE[key]
    base = host_inputs(inputs, S, L)
    ncores = B
    in_maps = []
    for c in range(ncores):
        d = dict(base)
        d["xT"] = np.ascontiguousarray(x[c % B].T)
        in_maps.append(d)
    res = run_bass_kernel_spmd(nc, in_maps, core_ids=list(range(ncores)))
    out = np.stack([np.ascontiguousarray(res.results[b]["yT"].T) for b in range(B)], axis=0)
    return out.astype(np.float32)
```
